# Optimizing a Trainium2 kernel written in Bass

```python
import math
import jax, jax.numpy as jnp
from jax import lax
import numpy as np

D_MODEL = 4096
BATCH = 4
SEQ = 2048
DEPTH = 1
DEC_BATCH = 128
DEC_SEQ = 1
PAST_LEN = 16384
PAGE_SIZE = 128

CONV_WIDTH = D_MODEL // 2
CONV_K = 31
SSM_WIDTH = D_MODEL // 2
SSM_GROUP = 16
SSM_GROUPS = SSM_WIDTH // SSM_GROUP
SSM_STATE = 64
D_FF = 256 * ((8 * D_MODEL // 3 + 255) // 256)
FFN_K = 3
IN_WIDTH = 2 * CONV_WIDTH + SSM_WIDTH + 2 * D_MODEL
EPS = 1e-6

kernel_name = "gated_conformer_conv_s5_convffn_step"


def rmsnorm(x, g):
    xf = x.astype(jnp.float32)
    r = xf * lax.rsqrt(jnp.mean(xf * xf, axis=-1, keepdims=True) + EPS)
    return (r * g.astype(jnp.float32)).astype(x.dtype)


def layernorm(x, g, b):
    xf = x.astype(jnp.float32)
    mu = jnp.mean(xf, axis=-1, keepdims=True)
    var = jnp.mean(jnp.square(xf - mu), axis=-1, keepdims=True)
    r = (xf - mu) * lax.rsqrt(var + EPS)
    return (r * g.astype(jnp.float32) + b.astype(jnp.float32)).astype(x.dtype)


def causal_dwconv(x, buf, w, b):
    k = w.shape[0]
    xp = jnp.concatenate([buf.astype(x.dtype), x], axis=1)
    out = lax.conv_general_dilated(
        xp, w[:, None, :].astype(x.dtype), window_strides=(1,), padding='VALID',
        dimension_numbers=('NWC', 'WIO', 'NWC'), feature_group_count=x.shape[-1])
    return out + b.astype(x.dtype), xp[:, xp.shape[1] - (k - 1):]


def _complex_combine(e1, e2):
    a1r, a1i, b1r, b1i = e1
    a2r, a2i, b2r, b2i = e2
    return (a2r * a1r - a2i * a1i,
            a2r * a1i + a2i * a1r,
            a2r * b1r - a2i * b1i + b2r,
            a2r * b1i + a2i * b1r + b2i)


def s5_layer(u, h0_re, h0_im, lam_re, lam_im, log_dt, b_re, b_im, c_re, c_im, d_skip):
    n, s, _ = u.shape
    f32 = jnp.float32
    uf = u.astype(f32)
    ug = uf.reshape(n, s, SSM_GROUPS, SSM_GROUP)
    lr, li = lam_re.astype(f32), lam_im.astype(f32)
    dt = jnp.exp(log_dt.astype(f32))[:, None]
    mag = jnp.exp(lr * dt)
    ab_re, ab_im = mag * jnp.cos(li * dt), mag * jnp.sin(li * dt)
    den = lr * lr + li * li
    nr, ni = ab_re - 1.0, ab_im
    f_re = (nr * lr + ni * li) / den
    f_im = (ni * lr - nr * li) / den
    br, bi = b_re.astype(f32), b_im.astype(f32)
    bb_re = f_re[..., None] * br - f_im[..., None] * bi
    bb_im = f_re[..., None] * bi + f_im[..., None] * br
    bu_re = jnp.einsum('nsgh,gph->nsgp', ug, bb_re)
    bu_im = jnp.einsum('nsgh,gph->nsgp', ug, bb_im)
    a_re = jnp.broadcast_to(ab_re, bu_re.shape)
    a_im = jnp.broadcast_to(ab_im, bu_im.shape)
    acc_r, acc_i, hr, hi = lax.associative_scan(
        _complex_combine, (a_re, a_im, bu_re, bu_im), axis=1)
    h0r = h0_re.astype(f32)[:, None]
    h0i = h0_im.astype(f32)[:, None]
    hr = hr + acc_r * h0r - acc_i * h0i
    hi = hi + acc_r * h0i + acc_i * h0r
    y = (jnp.einsum('nsgp,ghp->nsgh', hr, c_re.astype(f32))
         - jnp.einsum('nsgp,ghp->nsgh', hi, c_im.astype(f32)))
    y = y.reshape(n, s, SSM_WIDTH) + d_skip.astype(f32) * uf
    return y.astype(u.dtype), hr[:, -1], hi[:, -1]


def mixer_block(xn, conv_buf, h_re, h_im, p):
    proj = xn @ p['w_in']
    a_in, a_gate, s_in, g_a, g_b = jnp.split(
        proj, [CONV_WIDTH, 2 * CONV_WIDTH, 2 * CONV_WIDTH + SSM_WIDTH,
               2 * CONV_WIDTH + SSM_WIDTH + D_MODEL], axis=-1)
    glu = a_in * jax.nn.sigmoid(a_gate)
    c, new_conv = causal_dwconv(glu, conv_buf, p['conv_w'], p['conv_b'])
    c = jax.nn.silu(layernorm(c, p['ln_g'], p['ln_b']))
    y_a = c @ p['w_conv_out']
    s, nh_re, nh_im = s5_layer(s_in, h_re, h_im, p['lam_re'], p['lam_im'], p['log_dt'],
                               p['b_re'], p['b_im'], p['c_re'], p['c_im'], p['d_skip'])
    sg = jax.nn.gelu(s)
    y_b = (sg * jax.nn.sigmoid(sg @ p['w_glu'])) @ p['w_ssm_out']
    merged = jax.nn.sigmoid(g_a) * y_a + jax.nn.sigmoid(g_b) * y_b
    return merged @ p['w_o'], new_conv, nh_re, nh_im


def ffn_block(xn, buf, p):
    gate, val = jnp.split(xn @ p['w_up'], [D_FF], axis=-1)
    gc, new_buf = causal_dwconv(gate, buf, p['ffn_conv_w'], p['ffn_conv_b'])
    return (jax.nn.silu(gc) * val) @ p['w_down'], new_buf


def layer(x, conv_buf, h_re, h_im, ffn_buf, p):
    m, new_conv, nh_re, nh_im = mixer_block(rmsnorm(x, p['norm_mix_g']), conv_buf, h_re, h_im, p)
    x = x + m
    f, new_ffn = ffn_block(rmsnorm(x, p['norm_ffn_g']), ffn_buf, p)
    return x + f, new_conv, nh_re, nh_im, new_ffn


def setup_inputs(seed: int = 0) -> dict:
    key = jax.random.key(seed)
    ks = jax.random.split(key, 40)
    f32 = jnp.float32

    def nrm(k, shape, scale):
        return jax.random.normal(k, shape, f32) * scale

    L = DEPTH
    lam_im = (math.pi * jnp.arange(SSM_STATE, dtype=f32))[None, None, :] + nrm(ks[10], (L, SSM_GROUPS, SSM_STATE), 0.01)
    return {
        'x_prompt': nrm(ks[0], (BATCH, SEQ, D_MODEL), 1.0),
        'x_sample': nrm(ks[1], (DEC_BATCH, DEC_SEQ, D_MODEL), 1.0),
        'state_conv': nrm(ks[2], (L, DEC_BATCH, CONV_K - 1, CONV_WIDTH), 0.5),
        'state_ssm_re': nrm(ks[3], (L, DEC_BATCH, SSM_GROUPS, SSM_STATE), 0.3),
        'state_ssm_im': nrm(ks[4], (L, DEC_BATCH, SSM_GROUPS, SSM_STATE), 0.3),
        'state_ffn_conv': nrm(ks[5], (L, DEC_BATCH, FFN_K - 1, D_FF), 1.0),
        'norm_mix_g': 1.0 + nrm(ks[6], (L, D_MODEL), 0.02),
        'w_in': nrm(ks[7], (L, D_MODEL, IN_WIDTH), D_MODEL ** -0.5),
        'conv_w': nrm(ks[8], (L, CONV_K, CONV_WIDTH), CONV_K ** -0.5),
        'conv_b': nrm(ks[9], (L, CONV_WIDTH), 0.02),
        'ln_g': 1.0 + nrm(ks[11], (L, CONV_WIDTH), 0.02),
        'ln_b': nrm(ks[12], (L, CONV_WIDTH), 0.02),
        'w_conv_out': nrm(ks[13], (L, CONV_WIDTH, D_MODEL), CONV_WIDTH ** -0.5),
        'lam_re': -0.5 + nrm(ks[14], (L, SSM_GROUPS, SSM_STATE), 0.01),
        'lam_im': lam_im,
        'log_dt': jax.random.uniform(ks[15], (L, SSM_GROUPS), f32, math.log(1e-3), math.log(1e-1)),
        'b_re': nrm(ks[16], (L, SSM_GROUPS, SSM_STATE, SSM_GROUP), (2.0 * SSM_GROUP) ** -0.5),
        'b_im': nrm(ks[17], (L, SSM_GROUPS, SSM_STATE, SSM_GROUP), (2.0 * SSM_GROUP) ** -0.5),
        'c_re': nrm(ks[18], (L, SSM_GROUPS, SSM_GROUP, SSM_STATE), (2.0 * SSM_STATE) ** -0.5),
        'c_im': nrm(ks[19], (L, SSM_GROUPS, SSM_GROUP, SSM_STATE), (2.0 * SSM_STATE) ** -0.5),
        'd_skip': nrm(ks[20], (L, SSM_WIDTH), 1.0),
        'w_glu': nrm(ks[21], (L, SSM_WIDTH, SSM_WIDTH), SSM_WIDTH ** -0.5),
        'w_ssm_out': nrm(ks[22], (L, SSM_WIDTH, D_MODEL), SSM_WIDTH ** -0.5),
        'w_o': nrm(ks[23], (L, D_MODEL, D_MODEL), D_MODEL ** -0.5),
        'norm_ffn_g': 1.0 + nrm(ks[24], (L, D_MODEL), 0.02),
        'w_up': nrm(ks[25], (L, D_MODEL, 2 * D_FF), D_MODEL ** -0.5),
        'ffn_conv_w': nrm(ks[26], (L, FFN_K, D_FF), FFN_K ** -0.5),
        'ffn_conv_b': nrm(ks[27], (L, D_FF), 0.02),
        'w_down': nrm(ks[28], (L, D_FF, D_MODEL), D_FF ** -0.5),
        'final_norm_g': 1.0 + nrm(ks[29], (D_MODEL,), 0.02),
    }


def reference(x_prompt, x_sample, state_conv, state_ssm_re, state_ssm_im, state_ffn_conv,
              norm_mix_g, w_in, conv_w, conv_b, ln_g, ln_b, w_conv_out,
              lam_re, lam_im, log_dt, b_re, b_im, c_re, c_im, d_skip, w_glu, w_ssm_out, w_o,
              norm_ffn_g, w_up, ffn_conv_w, ffn_conv_b, w_down, final_norm_g):
    xp, xs = x_prompt, x_sample
    sdt = jnp.float32
    conv_p, conv_s, ssr_p, ssi_p, ssr_s, ssi_s, ffn_p, ffn_s = [], [], [], [], [], [], [], []
    for l in range(DEPTH):
        p = dict(norm_mix_g=norm_mix_g[l], w_in=w_in[l], conv_w=conv_w[l], conv_b=conv_b[l],
                 ln_g=ln_g[l], ln_b=ln_b[l], w_conv_out=w_conv_out[l], lam_re=lam_re[l],
                 lam_im=lam_im[l], log_dt=log_dt[l], b_re=b_re[l], b_im=b_im[l], c_re=c_re[l],
                 c_im=c_im[l], d_skip=d_skip[l], w_glu=w_glu[l], w_ssm_out=w_ssm_out[l], w_o=w_o[l],
                 norm_ffn_g=norm_ffn_g[l], w_up=w_up[l], ffn_conv_w=ffn_conv_w[l],
                 ffn_conv_b=ffn_conv_b[l], w_down=w_down[l])
        nb = xp.shape[0]
        xp, c1, r1, i1, f1 = layer(
            xp,
            jnp.zeros((nb, CONV_K - 1, CONV_WIDTH), xp.dtype),
            jnp.zeros((nb, SSM_GROUPS, SSM_STATE), sdt),
            jnp.zeros((nb, SSM_GROUPS, SSM_STATE), sdt),
            jnp.zeros((nb, FFN_K - 1, D_FF), xp.dtype), p)
        xs, c2, r2, i2, f2 = layer(xs, state_conv[l], state_ssm_re[l], state_ssm_im[l],
                                   state_ffn_conv[l], p)
        conv_p.append(c1); ssr_p.append(r1); ssi_p.append(i1); ffn_p.append(f1)
        conv_s.append(c2); ssr_s.append(r2); ssi_s.append(i2); ffn_s.append(f2)
    y_prompt = rmsnorm(xp, final_norm_g)
    y_sample = rmsnorm(xs, final_norm_g)
    return (y_prompt, y_sample,
            jnp.stack(conv_p), jnp.stack(conv_s),
            jnp.stack(ssr_p), jnp.stack(ssi_p), jnp.stack(ssr_s), jnp.stack(ssi_s),
            jnp.stack(ffn_p), jnp.stack(ffn_s))
```

```python
import math
import numpy as np
from contextlib import ExitStack
import concourse.bass as bass
import concourse.mybir as mybir
from concourse.bass_utils import run_bass_kernel_spmd

F32 = mybir.dt.float32
BF16 = mybir.dt.bfloat16
AF = mybir.ActivationFunctionType
ALU = mybir.AluOpType
AX = mybir.AxisListType

PE, ACT, DVE, POOL, SP = "pe", "act", "dve", "pool", "sp"

D = 4096
CWD = 2048
SWD = 2048
DFF = 11008
INW = 14336
NP = 512
NS = 8
N = NP + NS
HN = N // 2
PADH = 30
PAD = 256
NSLOT = 6
EPS = 1e-6
NCH = 43


class _Rec:
    __slots__ = ("eng", "fn", "waits", "needs_inc", "dma_sem", "dma_val", "cnt", "seq")

    def __init__(self, eng, fn):
        self.eng = eng
        self.fn = fn
        self.waits = []
        self.needs_inc = False
        self.dma_sem = None
        self.dma_val = 0
        self.cnt = 0
        self.seq = 0


class Prog:
    def __init__(self, nc, dry=False):
        self.nc = nc
        self.dry = dry
        self.ops = {e: [] for e in (PE, ACT, DVE, POOL, SP)}
        self.res = {}
        self.dma_cnt = {}
        self.last_dma = {}
        self.pending = {}

    def _deps(self, rec, reads, writes, acc):
        if self.dry:
            return
        deps = []
        pend = self.pending.get(rec.eng)
        if pend:
            deps.extend(pend)
            self.pending[rec.eng] = None
        for r in reads:
            st = self.res.get(r)
            if st:
                deps.extend(st[0])
        for w in writes:
            st = self.res.get(w)
            if st:
                deps.extend(st[0])
                deps.extend(st[1])
        for w in acc:
            st = self.res.get(w)
            if st:
                deps.extend(st[0])
                deps.extend(st[1])
        best = {}
        for d in deps:
            if d is rec:
                continue
            if d.dma_sem is None and rec.dma_sem is None and d.eng == rec.eng and rec.eng == PE:
                continue
            key = ("d", d.dma_sem) if d.dma_sem is not None else ("e", d.eng)
            val = d.dma_val if d.dma_sem is not None else d.seq
            cur = best.get(key)
            if cur is None or val > cur[0]:
                best[key] = (val, d)
        for _, d in best.values():
            rec.waits.append(d)
            if d.dma_sem is None:
                d.needs_inc = True
        for r in reads:
            self.res.setdefault(r, [[], []])[1].append(rec)
        for w in writes:
            self.res[w] = [[rec], []]
        for w in acc:
            st = self.res.setdefault(w, [[], []])
            st[0].append(rec)
            st[1] = []

    def op(self, eng, fn, reads=(), writes=(), acc=()):
        rec = _Rec(eng, fn)
        rec.seq = len(self.ops[eng]) + 1
        self._deps(rec, reads, writes, acc)
        if not self.dry:
            self.ops[eng].append(rec)
        return rec

    def dma(self, eng, fn, sem, reads=(), writes=(), acc=()):
        rec = _Rec(eng, fn)
        rec.dma_sem = sem
        self.dma_cnt[sem] = self.dma_cnt.get(sem, 0) + 16
        rec.dma_val = self.dma_cnt[sem]
        self._deps(rec, reads, writes, acc)
        if not self.dry:
            self.ops[eng].append(rec)
            self.last_dma[sem] = rec
        return rec

    def barrier(self):
        if self.dry:
            return
        snap = []
        for e, lst in self.ops.items():
            for rec in reversed(lst):
                if rec.dma_sem is None:
                    snap.append(rec)
                    break
        for k, rec in self.last_dma.items():
            if not (isinstance(k, tuple) and k[0] == "w"):
                snap.append(rec)
        for e in (PE, ACT, DVE, SP):
            self.pending[e] = list(snap)

    def emit(self, stack):
        nc = self.nc
        esem = {e: stack.enter_context(nc.semaphore("es_" + e)) for e in (PE, ACT, DVE, POOL)}
        dsem = {}
        for k in self.dma_cnt:
            dsem[k] = stack.enter_context(nc.semaphore("ds_%d" % len(dsem)))
        for e, lst in self.ops.items():
            c = 0
            for rec in lst:
                if rec.dma_sem is None and rec.needs_inc:
                    c += 1
                rec.cnt = c
            assert c < 65000, (e, c)
        for k, v in self.dma_cnt.items():
            assert v < 65000, (k, v)
        block = stack.enter_context(nc.Block())

        def run(e, eng, final=False):
            waited = {}
            for rec in self.ops[e]:
                for d in rec.waits:
                    if d.dma_sem is not None:
                        key, val, sem = ("d", d.dma_sem), d.dma_val, dsem[d.dma_sem]
                    else:
                        key, val, sem = ("e", d.eng), d.cnt, esem[d.eng]
                    if waited.get(key, 0) >= val:
                        continue
                    waited[key] = val
                    eng.wait_ge(sem, val)
                ins = rec.fn(eng)
                if rec.dma_sem is not None:
                    ins.then_inc(dsem[rec.dma_sem], 16)
                elif rec.needs_inc:
                    ins.then_inc(esem[e], 1)
            if final:
                for k, v in self.dma_cnt.items():
                    if waited.get(("d", k), 0) < v:
                        eng.wait_ge(dsem[k], v)

        block.tensor(lambda eng: run(PE, eng))
        block.scalar(lambda eng: run(ACT, eng))
        block.vector(lambda eng: run(DVE, eng))
        block.gpsimd(lambda eng: run(POOL, eng))
        block.sync(lambda eng: run(SP, eng, final=True))


def build_nc(NT):
    nc = bass.Bass("TRN2", target_bir_lowering=False)
    NSA = NT * NS

    def din(name, shape):
        return nc.dram_tensor(name, list(shape), F32, kind="ExternalInput").ap()

    def dout(name, shape):
        return nc.dram_tensor(name, list(shape), F32, kind="ExternalOutput").ap()

    xp = din("xp", [NT * NP, D]); xs = din("xs", [NSA, D])
    sc = din("sc", [NSA, 30, CWD]); sr = din("sr", [NSA, 8192]); si = din("si", [NSA, 8192])
    sf = din("sf", [NSA, 2, DFF])
    norm_mix_g = din("norm_mix_g", [D]); w_in = din("w_in", [D, INW])
    conv_w = din("conv_w", [31, CWD]); conv_b = din("conv_b", [CWD])
    ln_g = din("ln_g", [CWD]); ln_b = din("ln_b", [CWD]); w_conv_out = din("w_conv_out", [CWD, D])
    lam_re = din("lam_re", [128, 64]); lam_im = din("lam_im", [128, 64]); log_dt = din("log_dt", [128])
    b_re = din("b_re", [128, 64, 16]); b_im = din("b_im", [128, 64, 16])
    c_re = din("c_re", [128, 16, 64]); c_im = din("c_im", [128, 16, 64])
    d_skip = din("d_skip", [SWD]); w_glu = din("w_glu", [SWD, SWD]); w_ssm_out = din("w_ssm_out", [SWD, D])
    w_o = din("w_o", [D, D]); norm_ffn_g = din("norm_ffn_g", [D]); w_up = din("w_up", [D, 2 * DFF])
    ffn_conv_w = din("ffn_conv_w", [3, DFF]); ffn_conv_b = din("ffn_conv_b", [DFF])
    w_down = din("w_down", [DFF, D]); final_norm_g = din("final_norm_g", [D])
    ident = din("ident", [128, 128])

    y_p = dout("y_p", [NT * NP, D]); y_s = dout("y_s", [NSA, D])
    conv_p = dout("conv_p", [30, CWD]); conv_s = dout("conv_s", [NSA, 30, CWD])
    ssr_p = dout("ssr_p", [8192]); ssi_p = dout("ssi_p", [8192])
    ssr_s = dout("ssr_s", [NSA, 8192]); ssi_s = dout("ssi_s", [NSA, 8192])
    ffn_p = dout("ffn_p", [2, DFF]); ffn_s = dout("ffn_s", [NSA, 2, DFF])

    XMID = nc.dram_tensor("xmid", [NT, N, D], F32).ap()
    LMD = nc.dram_tensor("lmd", [64, 128, 4, 128], BF16).ap()

    st = ExitStack()
    with st:
        def sb(name, shape, dt=F32):
            return st.enter_context(nc.sbuf_tensor(name, list(shape), dt))

        R1 = sb("R1", [128, 32 * N], BF16)
        RA = sb("RA", [128, 34240], BF16)
        R5 = sb("R5", [128, 16 * N], BF16)
        TA = sb("TA", [128, 7680], BF16)
        WR = sb("WR", [128, NSLOT, 16, 256], BF16)
        IDENT = sb("IDENT", [128, 128]); ONES = sb("ONES", [128, 128])
        G1 = sb("G1", [128, 32, 1]); G2 = sb("G2", [128, 32, 1])
        CWT = sb("CWT", [128, 16, 31]); CBS = sb("CBS", [128, 16, 1])
        LNG = sb("LNG", [128, 16, 1]); LNB = sb("LNB", [128, 16, 1])
        FCW = sb("FCW", [128, 86, 3]); FCB = sb("FCB", [128, 86, 1]); DSK = sb("DSK", [128, 16, 1])
        PWR = sb("PWR", [128, 64, 9]); PWI = sb("PWI", [128, 64, 9]); PWN = sb("PWN", [128, 64, 9])
        HALO = sb("HALO", [128, 16, PADH]); FH = sb("FH", [128, 86, 2]); CARRY = sb("CARRY", [128, 64, 2])
        SS = sb("SS", [128, 2]); RS = sb("RS", [128, 2])
        PS = st.enter_context(nc.psum_tensor("PS", [128, 8, 512], F32))

        def f32v(t, b0, nbytes):
            return t[:, b0 // 2:(b0 + nbytes) // 2].bitcast(F32)

        def b16v(t, b0, nbytes):
            return t[:, b0 // 2:(b0 + nbytes) // 2]

        XN = R1[:, :].rearrange("p (c n) -> p c n", c=32)
        GF = f32v(R1, 0, 4 * D)
        R2o, R3o = 0, 35200
        GLU = f32v(RA, R2o, 16 * (PADH + N) * 4).rearrange("p (c n) -> p c n", c=16)
        U16 = b16v(RA, R2o, 16 * N * 2).rearrange("p (c n) -> p c n", c=16)
        SG16 = b16v(RA, R2o + 16 * N * 2, 16 * N * 2).rearrange("p (c n) -> p c n", c=16)
        XT = f32v(RA, R2o, 2 * D * 4).rearrange("p (s n) -> p s n", s=2)
        MG16 = b16v(RA, R2o, 32 * N * 2).rearrange("p (c n) -> p c n", c=32)
        CV = f32v(RA, R3o, 16 * N * 4).rearrange("p (c n) -> p c n", c=16)
        SGG16 = b16v(RA, R3o, 16 * N * 2).rearrange("p (c n) -> p c n", c=16)
        AB = f32v(RA, R3o + 16640, 4 * (PAD + N) * 4).rearrange("p (c n) -> p c n", c=4)
        H16S = b16v(RA, R3o + 16640 + 12416, 4 * N * 2).rearrange("p (s r n) -> p s r n", s=2, r=2)
        SQJ = b16v(RA, R3o, D * 2)
        H16F = b16v(RA, 0, NCH * N * 2).rearrange("p (c n) -> p c n", c=NCH)
        FSo = NCH * N * 2
        FS = f32v(RA, FSo, 86 * 16 * 4).rearrange("p (c r) -> p c r", c=86)
        GS = f32v(RA, FSo + 86 * 16 * 4, 86 * 8 * 4).rearrange("p (c r) -> p c r", c=86)
        CB16 = R5[:, :].rearrange("p (c n) -> p c n", c=16)
        SIGT = f32v(TA, 0, 2080)
        SQ = f32v(TA, 2080, 2080); MU = f32v(TA, 4160, 2080); RSTD = f32v(TA, 6240, 2080)
        CSTG = f32v(TA, 8320, 2048).rearrange("p (s b n) -> p s b n", s=2, b=2)
        TMPS = f32v(TA, 10368, 960).rearrange("p (n k) -> p n k", n=8)
        CSS = f32v(TA, 11328, 32)
        YT = f32v(TA, 2080, 2080)
        LMS = b16v(TA, 4160, 2048).rearrange("p (s m n) -> p s m n", s=2, m=4)
        H0 = f32v(TA, 6208, 4096).rearrange("p (s r n) -> p s r n", s=64, r=2)
        HS = f32v(TA, 10304, 4096).rearrange("p (s r n) -> p s r n", s=64, r=2)
        GT1 = f32v(TA, 14400, 480)
        MA = f32v(TA, 2080, 4160).rearrange("p (m n) -> p m n", m=2)
        T2 = f32v(TA, 6240, 2080)
        XSL = f32v(TA, 8320, 3072).rearrange("p (s n) -> p s n", s=3)
        XMS = f32v(TA, 11392, 3072).rearrange("p (s n) -> p s n", s=3)
        G32 = f32v(TA, 0, 2088)
        GCT = f32v(TA, 2088, 2080)
        SILU = f32v(TA, 4168, 4160).rearrange("p (m n) -> p m n", m=2)
        XSL2 = f32v(TA, 8328, 3072).rearrange("p (s n) -> p s n", s=3)
        XMS2 = f32v(TA, 11400, 3072).rearrange("p (s n) -> p s n", s=3)

        CSTG = f32v(TA, 8320, 2048).rearrange("p (s b n) -> p s b n", s=2, b=2)

        def emit_all(P, blocks):
            state = {"wi": 0, "acc": 0, "tm": 0, "issued": 0}

            def wissue(i):
                if i >= len(blocks):
                    return
                W, row0, nk, col0, ncols = blocks[i]
                slot = i % NSLOT
                src = W[row0:row0 + nk * 128, col0:col0 + ncols].rearrange("(k p) n -> p k n", p=128)
                P.dma(POOL, lambda e, slot=slot, src=src, nk=nk, ncols=ncols: e.dma_start(out=WR[:, slot, 0:nk, 0:ncols], in_=src),
                      ("w", slot), writes=[("w", slot)])

            def wnext(W, row0, nk, col0, ncols):
                i = state["wi"]
                state["wi"] += 1
                if P.dry:
                    blocks.append((W, row0, nk, col0, ncols))
                else:
                    assert blocks[i][1:] == (row0, nk, col0, ncols)
                return i % NSLOT

            def wrelease(slot_unused=None):
                if P.dry:
                    return
                wissue(state["issued"])
                state["issued"] += 1

            if not P.dry:
                for i in range(min(NSLOT, len(blocks))):
                    wissue(i)
                state["issued"] = NSLOT

            def next_acc():
                b = state["acc"] * 2
                state["acc"] = (state["acc"] + 1) % 4
                return b

            def psv(b):
                return PS[:, b:b + 2, 0:HN]

            def half(ap2d):
                return ap2d.rearrange("p (h n) -> p h n", h=2)

            def mm_group(slots, K, m, rhs_fn, b):
                for k in range(K):
                    slot = slots[k // 16]
                    rap, rkey = rhs_fn(k)
                    for h in range(2):
                        P.op(PE, lambda e, slot=slot, k=k, m=m, h=h, rap=rap, b=b, K=K: e.matmul(
                            PS[:, b + h, 0:HN], lhsT=WR[:, slot, k % 16, m * 128:(m + 1) * 128],
                            rhs=rap[:, h * HN:(h + 1) * HN], start=(k == 0), stop=(k == K - 1)),
                            reads=[("w", slot), rkey], acc=[("ps", b + h)])

            def linear_fm(W, col0, ncols_total, K, rhs_fn, epilogue, row0=0):
                nblk = (K + 15) // 16
                ci = 0
                for s0 in range(col0, col0 + ncols_total, 256):
                    ncols = min(256, col0 + ncols_total - s0)
                    slots = [wnext(W, row0 + kb * 2048, min(16, K - kb * 16), s0, ncols) for kb in range(nblk)]
                    for m in range(ncols // 128):
                        b = next_acc()
                        mm_group(slots, K, m, rhs_fn, b)
                        epilogue(ci, b)
                        ci += 1
                    for _ in slots:
                        wrelease()

            def tb_info(tb):
                return (128, tb * 128) if tb < 4 else (NS, NP)

            def linear_tm(W, row0, K, lhs_fn, epilogue):
                nblk = (K + 15) // 16
                for q in range(16):
                    banks = []
                    for tb in range(5):
                        banks.append(state["tm"] % 8)
                        state["tm"] += 1
                    for kb in range(nblk):
                        nk = min(16, K - kb * 16)
                        slot = wnext(W, row0 + kb * 2048, nk, q * 256, 256)
                        for tb in range(5):
                            rows, c0 = tb_info(tb)
                            for k in range(nk):
                                kk = kb * 16 + k
                                lap, lkey = lhs_fn(kk)
                                P.op(PE, lambda e, slot=slot, k=k, kk=kk, lap=lap, rows=rows, c0=c0, bk=banks[tb], K=K: e.matmul(
                                    PS[:rows, bk, 0:256], lhsT=lap[:, c0:c0 + rows], rhs=WR[:, slot, k, 0:256],
                                    start=(kk == 0), stop=(kk == K - 1)),
                                    reads=[("w", slot), lkey], acc=[("ps", banks[tb])])
                        wrelease()
                    for tb in range(5):
                        epilogue(q, tb, banks[tb])

            def elem_dma(out_ap, in_ap, sem, reads=(), writes=(), acc=()):
                P.dma(SP, lambda e, o=out_ap, i=in_ap: e.dma_start(out=o, in_=i, allow_slow_non_contiguous=True), sem, reads=reads, writes=writes, acc=acc)

            P.dma(SP, lambda e: e.dma_start(out=IDENT[:], in_=ident), "c0", writes=["IDENT"])
            P.op(DVE, lambda e: e.memset(ONES[:], 1.0), writes=["ONES"])
            P.op(DVE, lambda e: e.memset(HALO[:], 0.0), writes=["HALO"])
            P.op(DVE, lambda e: e.memset(FH[:], 0.0), writes=["FH"])
            P.op(DVE, lambda e: e.memset(CARRY[:], 0.0), writes=["CARRY"])
            elem_dma(G1[:, :, 0], norm_mix_g.rearrange("(c p) -> p c", p=128), "c1", writes=["G1"])
            elem_dma(G2[:, :, 0], norm_ffn_g.rearrange("(c p) -> p c", p=128), "c2", writes=["G2"])
            for k in range(31):
                elem_dma(CWT[:, :, k], conv_w[k].rearrange("(c p) -> p c", p=128), "c3", writes=[], reads=[])
            P.res["CWT"] = [[P.last_dma.get("c3")] if not P.dry else [], []]
            elem_dma(CBS[:, :, 0], conv_b.rearrange("(c p) -> p c", p=128), "c4", writes=["CBS"])
            elem_dma(LNG[:, :, 0], ln_g.rearrange("(c p) -> p c", p=128), "c5", writes=["LNG"])
            elem_dma(LNB[:, :, 0], ln_b.rearrange("(c p) -> p c", p=128), "c6", writes=["LNB"])
            for k in range(3):
                elem_dma(FCW[:, :, k], ffn_conv_w[k].rearrange("(c p) -> p c", p=128), "c7")
            P.res["FCW"] = [[P.last_dma.get("c7")] if not P.dry else [], []]
            elem_dma(FCB[:, :, 0], ffn_conv_b.rearrange("(c p) -> p c", p=128), "c8", writes=["FCB"])
            elem_dma(DSK[:, :, 0], d_skip.rearrange("(c p) -> p c", p=128), "c9", writes=["DSK"])

            def ra32(b0, shape):
                n = int(np.prod(shape))
                v = f32v(RA, b0, n * 4)
                if len(shape) == 2:
                    return v.rearrange("p (a b) -> p a b", a=shape[0])
                if len(shape) == 3:
                    return v.rearrange("p (a b c) -> p a b c", a=shape[0], b=shape[1])
                return v

            o = 32768
            names = ["LR", "LI", "LDT", "DT", "LRD", "LID", "MAG", "YA", "COSV", "SINV", "ABR", "ABI", "DEN", "NR", "T1", "T2s", "FRE", "FIM"]
            V = {}
            for nm in names:
                V[nm] = ra32(o, [64, 1]); o += 256
            BRE = ra32(o, [64, 16]); o += 4096
            BIM = ra32(o, [64, 16]); o += 4096
            BBR = ra32(o, [64, 16]); o += 4096
            BBI = ra32(o, [64, 16]); o += 4096
            TB1 = ra32(o, [64, 16]); o += 4096
            STG = b16v(RA, o, 1024).rearrange("p (s n) -> p s n", s=4); o += 1024
            STG2 = b16v(RA, o, 1024).rearrange("p (s n) -> p s n", s=4); o += 1024
            ZB = ra32(0, [64, 128])
            ZB4 = f32v(RA, 0, 32768).rearrange("p (c j n) -> p c j n", c=16, j=4)

            lam2 = lambda a: a.rearrange("(s two) p -> two p s", two=2)
            for gl in range(2):
                elem_dma(V["LR"][gl * 64:(gl + 1) * 64, :, 0], lam2(lam_re)[gl], "s0")
                elem_dma(V["LI"][gl * 64:(gl + 1) * 64, :, 0], lam2(lam_im)[gl], "s0")
                P.dma(SP, lambda e, gl=gl: e.dma_start(out=V["LDT"][gl * 64:(gl + 1) * 64, :, 0],
                                                       in_=log_dt.rearrange("(s two) -> two s", two=2)[gl].partition_broadcast(64), allow_slow_non_contiguous=True), "s0")
                P.dma(SP, lambda e, gl=gl: e.dma_start(out=BRE[gl * 64:(gl + 1) * 64], in_=b_re.rearrange("(s two) p h -> two p s h", two=2)[gl]), "s0")
                P.dma(SP, lambda e, gl=gl: e.dma_start(out=BIM[gl * 64:(gl + 1) * 64], in_=b_im.rearrange("(s two) p h -> two p s h", two=2)[gl]), "s0")
            if not P.dry:
                P.res["SPRM"] = [[P.last_dma["s0"]], []]

            def dve(fn, r=(), w=()):
                P.op(DVE, fn, reads=r, writes=w)

            def actop(fn, r=(), w=()):
                P.op(ACT, fn, reads=r, writes=w)

            S_ = "SPRM"
            actop(lambda e: e.activation(out=V["DT"][:], in_=V["LDT"][:], func=AF.Exp), [S_], [S_])
            dve(lambda e: e.tensor_tensor(out=V["LRD"][:], in0=V["LR"][:], in1=V["DT"][:], op=ALU.mult), [S_], [S_])
            dve(lambda e: e.tensor_tensor(out=V["LID"][:], in0=V["LI"][:], in1=V["DT"][:], op=ALU.mult), [S_], [S_])
            actop(lambda e: e.activation(out=V["MAG"][:], in_=V["LRD"][:], func=AF.Exp, scale=1.0 / 32), [S_], [S_])
            actop(lambda e: e.activation(out=V["SINV"][:], in_=V["LID"][:], func=AF.Sin, scale=1.0 / 32), [S_], [S_])
            actop(lambda e: e.activation(out=V["COSV"][:], in_=V["LID"][:], func=AF.Sin, scale=1.0 / 32, bias=0.5 * math.pi), [S_], [S_])
            dve(lambda e: e.tensor_tensor(out=V["ABR"][:], in0=V["MAG"][:], in1=V["COSV"][:], op=ALU.mult), [S_], [S_])
            dve(lambda e: e.tensor_tensor(out=V["ABI"][:], in0=V["MAG"][:], in1=V["SINV"][:], op=ALU.mult), [S_], [S_])
            for _sq in range(5):
                dve(lambda e: e.tensor_tensor(out=V["T1"][:], in0=V["ABR"][:], in1=V["ABR"][:], op=ALU.mult), [S_], [S_])
                dve(lambda e: e.tensor_tensor(out=V["T2s"][:], in0=V["ABI"][:], in1=V["ABI"][:], op=ALU.mult), [S_], [S_])
                dve(lambda e: e.tensor_tensor(out=V["YA"][:], in0=V["ABR"][:], in1=V["ABI"][:], op=ALU.mult), [S_], [S_])
                dve(lambda e: e.tensor_tensor(out=V["ABR"][:], in0=V["T1"][:], in1=V["T2s"][:], op=ALU.subtract), [S_], [S_])
                dve(lambda e: e.tensor_scalar(out=V["ABI"][:], in0=V["YA"][:], scalar1=2.0, scalar2=None, op0=ALU.mult), [S_], [S_])
            dve(lambda e: e.tensor_tensor(out=V["DEN"][:], in0=V["LR"][:], in1=V["LR"][:], op=ALU.mult), [S_], [S_])
            dve(lambda e: e.tensor_tensor(out=V["T1"][:], in0=V["LI"][:], in1=V["LI"][:], op=ALU.mult), [S_], [S_])
            dve(lambda e: e.tensor_tensor(out=V["DEN"][:], in0=V["DEN"][:], in1=V["T1"][:], op=ALU.add), [S_], [S_])
            dve(lambda e: e.reciprocal(out=V["DEN"][:], in_=V["DEN"][:]), [S_], [S_])
            dve(lambda e: e.tensor_scalar(out=V["NR"][:], in0=V["ABR"][:], scalar1=-1.0, scalar2=None, op0=ALU.add), [S_], [S_])
            dve(lambda e: e.tensor_tensor(out=V["T1"][:], in0=V["NR"][:], in1=V["LR"][:], op=ALU.mult), [S_], [S_])
            dve(lambda e: e.tensor_tensor(out=V["T2s"][:], in0=V["ABI"][:], in1=V["LI"][:], op=ALU.mult), [S_], [S_])
            dve(lambda e: e.tensor_tensor(out=V["T1"][:], in0=V["T1"][:], in1=V["T2s"][:], op=ALU.add), [S_], [S_])
            dve(lambda e: e.tensor_tensor(out=V["FRE"][:], in0=V["T1"][:], in1=V["DEN"][:], op=ALU.mult), [S_], [S_])
            dve(lambda e: e.tensor_tensor(out=V["T1"][:], in0=V["ABI"][:], in1=V["LR"][:], op=ALU.mult), [S_], [S_])
            dve(lambda e: e.tensor_tensor(out=V["T2s"][:], in0=V["NR"][:], in1=V["LI"][:], op=ALU.mult), [S_], [S_])
            dve(lambda e: e.tensor_tensor(out=V["T1"][:], in0=V["T1"][:], in1=V["T2s"][:], op=ALU.subtract), [S_], [S_])
            dve(lambda e: e.tensor_tensor(out=V["FIM"][:], in0=V["T1"][:], in1=V["DEN"][:], op=ALU.mult), [S_], [S_])
            dve(lambda e: e.tensor_copy(out=PWR[:, :, 0:1], in_=V["ABR"][:]), [S_], ["PW"])
            dve(lambda e: e.tensor_copy(out=PWI[:, :, 0:1], in_=V["ABI"][:]), [S_], ["PW"])
            for j in range(8):
                dve(lambda e, j=j: e.tensor_tensor(out=V["T1"][:], in0=PWR[:, :, j:j + 1], in1=PWR[:, :, j:j + 1], op=ALU.mult), ["PW", S_], [S_])
                dve(lambda e, j=j: e.tensor_tensor(out=V["T2s"][:], in0=PWI[:, :, j:j + 1], in1=PWI[:, :, j:j + 1], op=ALU.mult), ["PW", S_], [S_])
                dve(lambda e, j=j: e.tensor_tensor(out=PWR[:, :, j + 1:j + 2], in0=V["T1"][:], in1=V["T2s"][:], op=ALU.subtract), [S_], ["PW"])
                dve(lambda e, j=j: e.tensor_tensor(out=V["T1"][:], in0=PWR[:, :, j:j + 1], in1=PWI[:, :, j:j + 1], op=ALU.mult), ["PW", S_], [S_])
                dve(lambda e, j=j: e.tensor_scalar(out=PWI[:, :, j + 1:j + 2], in0=V["T1"][:], scalar1=2.0, scalar2=None, op0=ALU.mult), [S_], ["PW"])
            dve(lambda e: e.tensor_scalar(out=PWN[:], in0=PWI[:], scalar1=-1.0, scalar2=None, op0=ALU.mult), ["PW"], ["PW"])
            bc = lambda a: a.to_broadcast([128, 64, 16])
            dve(lambda e: e.tensor_tensor(out=BBR[:], in0=BRE[:], in1=bc(V["FRE"][:]), op=ALU.mult), [S_], [S_])
            dve(lambda e: e.tensor_tensor(out=TB1[:], in0=BIM[:], in1=bc(V["FIM"][:]), op=ALU.mult), [S_], [S_])
            dve(lambda e: e.tensor_tensor(out=BBR[:], in0=BBR[:], in1=TB1[:], op=ALU.subtract), [S_], [S_])
            dve(lambda e: e.tensor_tensor(out=BBI[:], in0=BIM[:], in1=bc(V["FRE"][:]), op=ALU.mult), [S_], [S_])
            dve(lambda e: e.tensor_tensor(out=TB1[:], in0=BRE[:], in1=bc(V["FIM"][:]), op=ALU.mult), [S_], [S_])
            dve(lambda e: e.tensor_tensor(out=BBI[:], in0=BBI[:], in1=TB1[:], op=ALU.add), [S_], [S_])

            def transpose_store(mi, neg):
                for s4 in range(16):
                    bk = s4 % 2
                    for j in range(4):
                        s = s4 * 4 + j
                        P.op(PE, lambda e, s=s, j=j, bk=bk: e.transpose(out=PS[:, bk, j * 128:(j + 1) * 128], in_=ZB[:, s, :], identity=IDENT[:]),
                             reads=["ZB", "IDENT"], acc=[("ps", bk)])
                    stg = STG if s4 % 2 == 0 else STG2
                    skey = "STG%d" % (s4 % 2)
                    P.op(ACT, lambda e, bk=bk, stg=stg, neg=neg: e.activation(out=stg[:], in_=PS[:, bk, :].rearrange("p (j n) -> p j n", j=4),
                                                                             func=AF.Copy, scale=(-1.0 if neg else 1.0)),
                         writes=[("ps", bk), skey])
                    P.dma(SP, lambda e, s4=s4, stg=stg, mi=mi: e.dma_start(out=LMD[s4 * 4:(s4 + 1) * 4, :, mi, :].rearrange("s p n -> p s n"), in_=stg[:]),
                          "lm%d" % (s4 % 2), reads=[skey], writes=[("LMD", mi, s4)])

            dve(lambda e: e.memset(ZB[:], 0.0), [], ["ZB"])
            for mi, BB in ((0, BBR), (1, BBI)):
                BB4 = BB[:].rearrange("p (c j) h -> p c j h", j=4)
                for gl in range(2):
                    for j in range(4):
                        c0 = 32 * j + 16 * gl
                        dve(lambda e, gl=gl, j=j, c0=c0, BB4=BB4: e.tensor_copy(out=ZB4[gl * 64:(gl + 1) * 64, :, j, c0:c0 + 16],
                                                                               in_=BB4[gl * 64:(gl + 1) * 64, :, j, :]), [S_], ["ZB"])
                transpose_store(mi, False)
            dve(lambda e: e.memset(ZB[:], 0.0), [], ["ZB"])
            for mi, CC in ((2, c_re), (3, c_im)):
                CCv = CC.rearrange("(c r) h p -> r h c p", r=8)
                first = True
                for j in range(4):
                    for gl in range(2):
                        p0 = 32 * j + 16 * gl
                        P.dma(SP, lambda e, j=j, gl=gl, p0=p0, CCv=CCv: e.dma_start(out=ZB4[p0:p0 + 16, :, j, 64 * gl:64 * gl + 64], in_=CCv[2 * j + gl]),
                              "zc", reads=[], acc=["ZB"])
                transpose_store(mi, mi == 3)
            P.barrier()

            def xrows(t, tb):
                rows, c0 = tb_info(tb)
                if tb < 4:
                    return xp[t * NP + tb * 128: t * NP + (tb + 1) * 128, :], rows, c0
                return xs[t * NS:(t + 1) * NS, :], rows, c0

            def yrows(t, tb):
                if tb < 4:
                    return y_p[t * NP + tb * 128: t * NP + (tb + 1) * 128, :]
                return y_s[t * NS:(t + 1) * NS, :]

            def norm_stats(slot, rows):
                P.op(ACT, lambda e: e.activation(out=SQJ[:rows, :], in_=XT[:rows, slot, :], func=AF.Square, accum_out=SS[:rows, slot:slot + 1]),
                     reads=[("XT", slot)], writes=["SQJ", ("SS", slot)])
                P.op(ACT, lambda e: e.activation(out=RS[:rows, slot:slot + 1], in_=SS[:rows, slot:slot + 1], func=AF.Sqrt, scale=1.0 / D, bias=EPS),
                     reads=[("SS", slot)], writes=[("RS", slot)])
                P.op(DVE, lambda e: e.reciprocal(out=RS[:rows, slot:slot + 1], in_=RS[:rows, slot:slot + 1]), writes=[("RS", slot)])

            def norm_T(t, src_fn, G, gkey):
                for tb in range(5):
                    src, rows, c0 = src_fn(tb)
                    slot = tb % 2
                    P.dma(SP, lambda e, src=src, rows=rows, slot=slot: e.dma_start(out=XT[:rows, slot, :], in_=src), ("xt", slot),
                          writes=[("XT", slot)])
                    norm_stats(slot, rows)
                    P.op(ACT, lambda e, rows=rows, slot=slot: e.activation(out=XT[:rows, slot, :], in_=XT[:rows, slot, :], func=AF.Identity,
                                                                            scale=RS[:rows, slot:slot + 1]),
                         reads=[("RS", slot)], writes=[("XT", slot)])
                    for c4 in range(8):
                        bk = c4 % 2
                        for j in range(4):
                            c = c4 * 4 + j
                            P.op(PE, lambda e, rows=rows, slot=slot, c=c, j=j, bk=bk: e.transpose(
                                out=PS[:, bk, j * 128:j * 128 + rows], in_=XT[:rows, slot, c * 128:(c + 1) * 128], identity=IDENT[:rows, :rows]),
                                reads=[("XT", slot), "IDENT"], acc=[("ps", bk)])
                        P.op(DVE, lambda e, rows=rows, c4=c4, bk=bk, c0=c0, G=G: e.tensor_tensor(
                            out=XN[:, c4 * 4:(c4 + 1) * 4, c0:c0 + rows],
                            in0=PS[:, bk, :].rearrange("p (j n) -> p j n", j=4)[:, :, 0:rows],
                            in1=G[:, c4 * 4:(c4 + 1) * 4, :].to_broadcast([128, 4, rows]), op=ALU.mult),
                            reads=[gkey], writes=[("ps", bk)], acc=[("XN", c4 * 4 + j) for j in range(4)])

            def xn_rhs(k):
                return XN[:, k, :], ("XN", k)

            for t in range(NT):
                P.barrier()
                norm_T(t, lambda tb, t=t: xrows(t, tb), G1, "G1")
                P.barrier()
                def ep_ain(ci, b):
                    P.op(ACT, lambda e, ci=ci, b=b: e.activation(out=half(GLU[:, ci, PADH:PADH + N]), in_=psv(b), func=AF.Copy),
                         writes=[("ps", b), ("ps", b + 1), ("GLU", ci)])
                linear_fm(w_in, 0, CWD, 32, xn_rhs, ep_ain)

                def ep_agate(ci, b):
                    P.op(ACT, lambda e, b=b: e.activation(out=half(SIGT), in_=psv(b), func=AF.Sigmoid),
                         writes=[("ps", b), ("ps", b + 1), "SIGT"])
                    P.op(DVE, lambda e, ci=ci: e.tensor_tensor(out=GLU[:, ci, PADH:PADH + N], in0=GLU[:, ci, PADH:PADH + N], in1=SIGT, op=ALU.mult),
                         reads=["SIGT"], writes=[("GLU", ci)])
                linear_fm(w_in, CWD, CWD, 32, xn_rhs, ep_agate)
                P.op(DVE, lambda e: e.tensor_copy(out=GLU[:, :, 0:PADH], in_=HALO[:]), reads=["HALO"], acc=[("GLU", c) for c in range(16)])
                scv = sc[t * NS:(t + 1) * NS].rearrange("n k f -> (n k) f")
                for c in range(16):
                    cs = c % 2
                    P.dma(SP, lambda e, c=c, cs=cs, scv=scv: e.dma_start(out=CSTG[:120, cs, :, :],
                                                                  in_=scv[:, c * 128:(c + 1) * 128].rearrange("(b r) f -> r b f", b=2)),
                          ("cstg", cs), writes=[("CSTG", cs)])
                    P.op(ACT, lambda e, c=c: e.activation(out=CV[:, c, 0:NP], in_=GLU[:, c, PADH:PADH + NP], func=AF.Identity,
                                                           scale=CWT[:, c, 30:31], bias=CBS[:, c, :]),
                         reads=[("GLU", c), "CWT", "CBS"], writes=[("CV", c)])
                    for k in range(30):
                        P.op(DVE, lambda e, c=c, k=k: e.scalar_tensor_tensor(out=CV[:, c, 0:NP], in0=GLU[:, c, k:k + NP], scalar=CWT[:, c, k:k + 1],
                                                                              in1=CV[:, c, 0:NP], op0=ALU.mult, op1=ALU.add),
                             reads=[("GLU", c), "CWT"], writes=[("CV", c)])
                    bk = 4 + cs
                    for b2 in range(2):
                        P.op(PE, lambda e, cs=cs, b2=b2, bk=bk: e.transpose(out=PS[:, bk, b2 * 120:(b2 + 1) * 120], in_=CSTG[:120, cs, b2, :],
                                                                             identity=IDENT[:120, :120]),
                             reads=[("CSTG", cs), "IDENT"], acc=[("ps", bk)])
                    P.op(DVE, lambda e, c=c, bk=bk: e.tensor_tensor(out=TMPS[:], in0=PS[:, bk, 0:240].rearrange("p (n k) -> p n k", n=8),
                                                                    in1=CWT[:, c:c + 1, 0:30].to_broadcast([128, 8, 30]), op=ALU.mult),
                         reads=["CWT"], writes=[("ps", bk), "TMPS"])
                    P.op(DVE, lambda e: e.tensor_reduce(out=CSS, in_=TMPS[:], axis=AX.X, op=ALU.add), reads=["TMPS"], writes=["CSS"])
                    P.op(DVE, lambda e, c=c: e.scalar_tensor_tensor(out=CSS, in0=GLU[:, c, PADH + NP:PADH + N], scalar=CWT[:, c, 30:31], in1=CSS,
                                                                     op0=ALU.mult, op1=ALU.add), reads=[("GLU", c), "CWT"], writes=["CSS"])
                    P.op(DVE, lambda e, c=c: e.tensor_scalar(out=CV[:, c, NP:N], in0=CSS, scalar1=CBS[:, c, :], scalar2=None, op0=ALU.add),
                         reads=["CSS", "CBS"], writes=[("CV", c)])
                    P.op(ACT, lambda e, c=c: e.activation(out=SQ, in_=CV[:, c, :], func=AF.Square), reads=[("CV", c)], writes=["SQ"])
                    for h in range(2):
                        P.op(PE, lambda e, c=c, h=h: e.matmul(PS[:, 0 + h, 0:HN], lhsT=ONES[:], rhs=CV[:, c, h * HN:(h + 1) * HN], start=(c == 0), stop=(c == 15)),
                             reads=["ONES", ("CV", c)], acc=[("ps", 0 + h)])
                        P.op(PE, lambda e, c=c, h=h: e.matmul(PS[:, 2 + h, 0:HN], lhsT=ONES[:], rhs=SQ[:, h * HN:(h + 1) * HN], start=(c == 0), stop=(c == 15)),
                             reads=["ONES", "SQ"], acc=[("ps", 2 + h)])
                P.op(DVE, lambda e: e.tensor_copy(out=HALO[:], in_=GLU[:, :, NP:NP + PADH]), reads=[("GLU", c) for c in range(16)], writes=["HALO"])
                P.dma(SP, lambda e, t=t: e.dma_start(out=conv_s[t * NS:(t + 1) * NS, 0:29, :], in_=sc[t * NS:(t + 1) * NS, 1:30, :]), "cso")
                for n in range(NS):
                    elem_dma(conv_s[t * NS + n, 29, :].rearrange("(c p) -> p c", p=128), GLU[:, :, PADH + NP + n], "cso",
                             reads=[("GLU", c) for c in range(16)])
                if t == NT - 1:
                    for c in range(16):
                        elem_dma(conv_p[:, c * 128:(c + 1) * 128].rearrange("k p -> p k"), HALO[:, c, :], "cpo", reads=["HALO"])
                P.op(DVE, lambda e: e.tensor_scalar(out=half(MU), in0=psv(0), scalar1=1.0 / CWD, scalar2=None, op0=ALU.mult),
                     writes=[("ps", 0), ("ps", 1), "MU"])
                P.op(DVE, lambda e: e.tensor_tensor(out=SQ, in0=MU, in1=MU, op=ALU.mult), reads=["MU"], writes=["SQ"])
                P.op(DVE, lambda e: e.scalar_tensor_tensor(out=half(RSTD), in0=psv(2), scalar=1.0 / CWD, in1=half(SQ), op0=ALU.mult, op1=ALU.subtract),
                     reads=["SQ"], writes=[("ps", 2), ("ps", 3), "RSTD"])
                P.op(ACT, lambda e: e.activation(out=RSTD, in_=RSTD, func=AF.Sqrt, bias=EPS), writes=["RSTD"])
                P.op(DVE, lambda e: e.reciprocal(out=RSTD, in_=RSTD), writes=["RSTD"])
                for c in range(16):
                    P.op(DVE, lambda e, c=c: e.tensor_tensor(out=CV[:, c, :], in0=CV[:, c, :], in1=MU, op=ALU.subtract), reads=["MU"], writes=[("CV", c)])
                    P.op(DVE, lambda e, c=c: e.tensor_tensor(out=CV[:, c, :], in0=CV[:, c, :], in1=RSTD, op=ALU.mult), reads=["RSTD"], writes=[("CV", c)])
                    P.op(ACT, lambda e, c=c: e.activation(out=CB16[:, c, :], in_=CV[:, c, :], func=AF.Silu, scale=LNG[:, c, :], bias=LNB[:, c, :]),
                         reads=[("CV", c), "LNG", "LNB"], writes=[("CB", c)])
                P.barrier()
                def ep_u(ci, b):
                    P.op(ACT, lambda e, ci=ci, b=b: e.activation(out=half(U16[:, ci, :]), in_=psv(b), func=AF.Copy),
                         writes=[("ps", b), ("ps", b + 1), ("U", ci)])
                linear_fm(w_in, 2 * CWD, SWD, 32, xn_rhs, ep_u)
                P.op(DVE, lambda e: e.memset(AB[:, :, 0:PAD], 0.0), writes=["AB"])
                for n in range(NS):
                    row = t * NS + n
                    elem_dma(H0[:, :, 0, n], sr[row, :].rearrange("(s p) -> p s", p=128), "h0", acc=["H0"])
                    elem_dma(H0[:, :, 1, n], si[row, :].rearrange("(s p) -> p s", p=128), "h0", acc=["H0"])

                def bproj(s):
                    c = s // 4
                    sl = s % 2
                    P.dma(SP, lambda e, s=s, sl=sl: e.dma_start(out=LMS[:, sl], in_=LMD[s]), ("lms", sl),
                          reads=[("LMD", mi, s // 4) for mi in range(4)], writes=[("LMS", sl)])
                    for ri in range(2):
                        for h in range(2):
                            P.op(PE, lambda e, sl=sl, c=c, ri=ri, h=h: e.matmul(PS[:, 2 * ri + h, 0:HN], lhsT=LMS[:, sl, ri, :], rhs=U16[:, c, h * HN:(h + 1) * HN],
                                                                             start=True, stop=True),
                                 reads=[("LMS", sl), ("U", c)], acc=[("ps", 2 * ri + h)])

                def evac(s):
                    for ri in range(2):
                        P.op(ACT, lambda e, ri=ri: e.activation(out=half(AB[:, ri, PAD:PAD + N]), in_=psv(2 * ri), func=AF.Copy),
                             writes=[("ps", 2 * ri), ("ps", 2 * ri + 1), "AB"])

                def scan(s):
                    ar, ai, an = PWR[:, s, 0:1], PWI[:, s, 0:1], PWN[:, s, 0:1]
                    c0 = slice(PAD, PAD + 1)
                    sm = slice(PAD + NP, PAD + N)

                    def stt(out, in0, scalar, in1, r=()):
                        P.op(DVE, lambda e: e.scalar_tensor_tensor(out=out, in0=in0, scalar=scalar, in1=in1, op0=ALU.mult, op1=ALU.add),
                             reads=["PW"] + list(r), writes=["AB"])
                    stt(AB[:, 0, c0], CARRY[:, s, 0:1], ar, AB[:, 0, c0], ["CARRY"])
                    stt(AB[:, 0, c0], CARRY[:, s, 1:2], an, AB[:, 0, c0], ["CARRY"])
                    stt(AB[:, 1, c0], CARRY[:, s, 1:2], ar, AB[:, 1, c0], ["CARRY"])
                    stt(AB[:, 1, c0], CARRY[:, s, 0:1], ai, AB[:, 1, c0], ["CARRY"])
                    stt(AB[:, 2, sm], H0[:, s, 0, :], ar, AB[:, 0, sm], ["H0"])
                    stt(AB[:, 2, sm], H0[:, s, 1, :], an, AB[:, 2, sm], ["H0"])
                    stt(AB[:, 3, sm], H0[:, s, 1, :], ar, AB[:, 1, sm], ["H0"])
                    stt(AB[:, 3, sm], H0[:, s, 0, :], ai, AB[:, 3, sm], ["H0"])
                    src, dst = 0, 2
                    for j in range(9):
                        d = 1 << j
                        pr, pi_, pn = PWR[:, s, j:j + 1], PWI[:, s, j:j + 1], PWN[:, s, j:j + 1]
                        cur = slice(PAD, PAD + NP)
                        sh = slice(PAD - d, PAD + NP - d)
                        stt(AB[:, dst, cur], AB[:, src, sh], pr, AB[:, src, cur])
                        stt(AB[:, dst, cur], AB[:, src + 1, sh], pn, AB[:, dst, cur])
                        stt(AB[:, dst + 1, cur], AB[:, src + 1, sh], pr, AB[:, src + 1, cur])
                        stt(AB[:, dst + 1, cur], AB[:, src, sh], pi_, AB[:, dst + 1, cur])
                        src, dst = dst, src
                    return src

                def cast(s, fb):
                    hs = s % 2
                    for ri in range(2):
                        P.op(ACT, lambda e, hs=hs, ri=ri, fb=fb: e.activation(out=H16S[:, hs, ri, :], in_=AB[:, fb + ri, PAD:PAD + N], func=AF.Copy),
                             reads=["AB"], writes=[("H16S", hs)])
                        P.op(ACT, lambda e, s=s, ri=ri, fb=fb: e.activation(out=CARRY[:, s, ri:ri + 1], in_=AB[:, fb + ri, PAD + NP - 1:PAD + NP], func=AF.Copy),
                             reads=["AB"], writes=["CARRY"])
                        P.op(ACT, lambda e, s=s, ri=ri, fb=fb: e.activation(out=HS[:, s, ri, :], in_=AB[:, fb + ri, PAD + NP:PAD + N], func=AF.Copy),
                             reads=["AB"], writes=["HS"])

                def cproj(s):
                    c, j = s // 4, s % 4
                    sl, hs = s % 2, s % 2
                    by = 4 + 2 * (c % 2)
                    for ri in range(2):
                        for h in range(2):
                            P.op(PE, lambda e, sl=sl, hs=hs, ri=ri, h=h, by=by, j=j: e.matmul(
                                PS[:, by + h, 0:HN], lhsT=LMS[:, sl, 2 + ri, :], rhs=H16S[:, hs, ri, h * HN:(h + 1) * HN],
                                start=(j == 0 and ri == 0), stop=(j == 3 and ri == 1)),
                                reads=[("LMS", sl), ("H16S", hs)], acc=[("ps", by + h)])
                    if j == 3:
                        P.op(DVE, lambda e, c=c, by=by: e.scalar_tensor_tensor(out=half(YT), in0=half(U16[:, c, :]), scalar=DSK[:, c, :], in1=psv(by),
                                                                             op0=ALU.mult, op1=ALU.add),
                             reads=[("U", c), "DSK"], writes=[("ps", by), ("ps", by + 1), "YT"])
                        P.op(ACT, lambda e: e.activation(out=SIGT, in_=YT, func=AF.Square), reads=["YT"], writes=["SIGT"])
                        P.op(DVE, lambda e: e.tensor_scalar(out=SIGT, in0=SIGT, scalar1=0.044715, scalar2=1.0, op0=ALU.mult, op1=ALU.add), writes=["SIGT"])
                        P.op(DVE, lambda e: e.tensor_tensor(out=SIGT, in0=SIGT, in1=YT, op=ALU.mult), reads=["YT"], writes=["SIGT"])
                        P.op(ACT, lambda e: e.activation(out=SIGT, in_=SIGT, func=AF.Sigmoid, scale=2.0 * math.sqrt(2.0 / math.pi)), writes=["SIGT"])
                        P.op(DVE, lambda e, c=c: e.tensor_tensor(out=SG16[:, c, :], in0=YT, in1=SIGT, op=ALU.mult), reads=["YT", "SIGT"], writes=[("SG", c)])

                bproj(0)
                evac(0)
                for s in range(64):
                    if s + 1 < 64:
                        bproj(s + 1)
                    fb = scan(s)
                    cast(s, fb)
                    if s + 1 < 64:
                        evac(s + 1)
                    cproj(s)
                for n in range(NS):
                    row = t * NS + n
                    elem_dma(ssr_s[row, :].rearrange("(s p) -> p s", p=128), HS[:, :, 0, n], "hso", reads=["HS"])
                    elem_dma(ssi_s[row, :].rearrange("(s p) -> p s", p=128), HS[:, :, 1, n], "hso", reads=["HS"])
                if t == NT - 1:
                    elem_dma(ssr_p.rearrange("(s p) -> p s", p=128), CARRY[:, :, 0], "hpo", reads=["CARRY"])
                    elem_dma(ssi_p.rearrange("(s p) -> p s", p=128), CARRY[:, :, 1], "hpo", reads=["CARRY"])
                def ep_glu(ci, b):
                    P.op(ACT, lambda e, b=b: e.activation(out=half(SIGT), in_=psv(b), func=AF.Sigmoid), writes=[("ps", b), ("ps", b + 1), "SIGT"])
                    P.op(DVE, lambda e, ci=ci: e.tensor_tensor(out=SGG16[:, ci, :], in0=SG16[:, ci, :], in1=SIGT, op=ALU.mult),
                         reads=[("SG", ci), "SIGT"], writes=[("SGG", ci)])
                linear_fm(w_glu, 0, SWD, 16, lambda k: (SG16[:, k, :], ("SG", k)), ep_glu)
                P.barrier()
                for q in range(16):
                    co = wnext(w_conv_out, 0, 16, q * 256, 256)
                    ga0 = wnext(w_in, 0, 16, 3 * CWD + q * 256, 256)
                    ga1 = wnext(w_in, 2048, 16, 3 * CWD + q * 256, 256)
                    for m in range(2):
                        b1 = next_acc()
                        mm_group([co], 16, m, lambda k: (CB16[:, k, :], ("CB", k)), b1)
                        b2 = next_acc()
                        mm_group([ga0, ga1], 32, m, xn_rhs, b2)
                        P.op(ACT, lambda e, b2=b2: e.activation(out=half(SIGT), in_=psv(b2), func=AF.Sigmoid), writes=[("ps", b2), ("ps", b2 + 1), "SIGT"])
                        P.op(DVE, lambda e, m=m, b1=b1: e.tensor_tensor(out=half(MA[:, m, :]), in0=psv(b1), in1=half(SIGT), op=ALU.mult),
                             reads=["SIGT"], writes=[("ps", b1), ("ps", b1 + 1), ("MA", m)])
                    wrelease(); wrelease(); wrelease()
                    so = wnext(w_ssm_out, 0, 16, q * 256, 256)
                    gb0 = wnext(w_in, 0, 16, 3 * CWD + D + q * 256, 256)
                    gb1 = wnext(w_in, 2048, 16, 3 * CWD + D + q * 256, 256)
                    for m in range(2):
                        b1 = next_acc()
                        mm_group([so], 16, m, lambda k: (SGG16[:, k, :], ("SGG", k)), b1)
                        b2 = next_acc()
                        mm_group([gb0, gb1], 32, m, xn_rhs, b2)
                        P.op(ACT, lambda e, b2=b2: e.activation(out=half(SIGT), in_=psv(b2), func=AF.Sigmoid), writes=[("ps", b2), ("ps", b2 + 1), "SIGT"])
                        P.op(DVE, lambda e, b1=b1: e.tensor_tensor(out=half(T2), in0=psv(b1), in1=half(SIGT), op=ALU.mult),
                             reads=["SIGT"], writes=[("ps", b1), ("ps", b1 + 1), "T2"])
                        P.op(DVE, lambda e, m=m, q=q: e.tensor_tensor(out=MG16[:, 2 * q + m, :], in0=MA[:, m, :], in1=T2, op=ALU.add),
                             reads=[("MA", m), "T2"], writes=[("MG", 2 * q + m)])
                    wrelease(); wrelease(); wrelease()
                xi = {"i": 0}

                def ep_wo(q, tb, bk, t=t):
                    src, rows, c0 = xrows(t, tb)
                    sl = xi["i"] % 3
                    xi["i"] += 1
                    P.dma(SP, lambda e: e.dma_start(out=XSL[:rows, sl, :], in_=src[:, q * 256:(q + 1) * 256]), ("xsl", sl), writes=[("XSL", sl)])
                    P.op(DVE, lambda e: e.tensor_tensor(out=XMS[:rows, sl, :], in0=PS[:rows, bk, 0:256], in1=XSL[:rows, sl, :], op=ALU.add),
                         reads=[("XSL", sl)], writes=[("ps", bk), ("XMS", sl)])
                    P.dma(SP, lambda e: e.dma_start(out=XMID[t, c0:c0 + rows, q * 256:(q + 1) * 256], in_=XMS[:rows, sl, :]), ("xms", sl),
                          reads=[("XMS", sl)], writes=[("xmid", tb, q)])
                linear_tm(w_o, 0, 32, lambda k: (MG16[:, k, :], ("MG", k)), ep_wo)
                P.barrier()
                def xmid_rows(tb, t=t):
                    rows, c0 = tb_info(tb)
                    return XMID[t, c0:c0 + rows, :], rows, c0
                norm_T(t, xmid_rows, G2, "G2")
                P.barrier()
                for n in range(NS):
                    row = t * NS + n
                    for k in range(2):
                        elem_dma(FS[:, :, 2 * n + k], sf[row, k, :].rearrange("(c p) -> p c", p=128), "fs", acc=["FS"])
                for hf in range(2):
                    ch0 = hf * NCH
                    for s0 in range(0, NCH, 2):
                        ncol = min(2, NCH - s0) * 128
                        colg = (ch0 + s0) * 128
                        g0 = wnext(w_up, 0, 16, colg, ncol)
                        g1 = wnext(w_up, 2048, 16, colg, ncol)
                        for m in range(ncol // 128):
                            i = ch0 + s0 + m
                            b = next_acc()
                            mm_group([g0, g1], 32, m, xn_rhs, b)
                            P.op(ACT, lambda e, b=b: e.activation(out=half(G32[:, 2:2 + N]), in_=psv(b), func=AF.Copy), writes=[("ps", b), ("ps", b + 1), "G32"])
                            P.op(DVE, lambda e, i=i: e.tensor_copy(out=G32[:, 0:2], in_=FH[:, i, :]), reads=["FH"], writes=["G32"])
                            P.op(ACT, lambda e, i=i: e.activation(out=GCT, in_=G32[:, 2:2 + N], func=AF.Identity, scale=FCW[:, i, 2:3], bias=FCB[:, i, :]),
                                 reads=["G32", "FCW", "FCB"], writes=["GCT"])
                            for k in range(2):
                                P.op(DVE, lambda e, i=i, k=k: e.scalar_tensor_tensor(out=GCT[:, 0:NP], in0=G32[:, k:k + NP], scalar=FCW[:, i, k:k + 1],
                                                                                      in1=GCT[:, 0:NP], op0=ALU.mult, op1=ALU.add),
                                     reads=["G32", "FCW"], writes=["GCT"])
                                P.op(DVE, lambda e, i=i, k=k: e.scalar_tensor_tensor(out=GCT[:, NP:N], in0=FS[:, i, k:16:2], scalar=FCW[:, i, k:k + 1],
                                                                                      in1=GCT[:, NP:N], op0=ALU.mult, op1=ALU.add),
                                     reads=["FS", "FCW"], writes=["GCT"])
                            P.op(DVE, lambda e, i=i: e.tensor_copy(out=FH[:, i, :], in_=G32[:, NP:NP + 2]), reads=["G32"], writes=["FH"])
                            P.op(DVE, lambda e, i=i: e.tensor_copy(out=GS[:, i, :], in_=G32[:, 2 + NP:2 + N]), reads=["G32"], writes=["GS"])
                            P.op(ACT, lambda e, m=m: e.activation(out=SILU[:, m, :], in_=GCT, func=AF.Silu), reads=["GCT"], writes=[("SILU", m)])
                        wrelease(); wrelease()
                        v0 = wnext(w_up, 0, 16, DFF + colg, ncol)
                        v1 = wnext(w_up, 2048, 16, DFF + colg, ncol)
                        for m in range(ncol // 128):
                            il = s0 + m
                            b = next_acc()
                            mm_group([v0, v1], 32, m, xn_rhs, b)
                            P.op(DVE, lambda e, m=m, il=il, b=b: e.tensor_tensor(out=half(H16F[:, il, :]), in0=psv(b), in1=half(SILU[:, m, :]), op=ALU.mult),
                                 reads=[("SILU", m)], writes=[("ps", b), ("ps", b + 1), ("HF", il)])
                        wrelease(); wrelease()

                    def ep_down(q, tb, bk, t=t):
                        rows, c0 = tb_info(tb)
                        sl = xi["i"] % 3
                        xi["i"] += 1
                        P.dma(SP, lambda e: e.dma_start(out=XSL2[:rows, sl, :], in_=XMID[t, c0:c0 + rows, q * 256:(q + 1) * 256]), ("xsl2", sl),
                              reads=[("xmid", tb, q)], writes=[("XSL2", sl)])
                        P.op(DVE, lambda e: e.tensor_tensor(out=XMS2[:rows, sl, :], in0=PS[:rows, bk, 0:256], in1=XSL2[:rows, sl, :], op=ALU.add),
                             reads=[("XSL2", sl)], writes=[("ps", bk), ("XMS2", sl)])
                        P.dma(SP, lambda e: e.dma_start(out=XMID[t, c0:c0 + rows, q * 256:(q + 1) * 256], in_=XMS2[:rows, sl, :]), ("xms2", sl),
                              reads=[("XMS2", sl)], writes=[("xmid", tb, q)])
                    linear_tm(w_down, ch0 * 128, NCH, lambda k: (H16F[:, k, :], ("HF", k)), ep_down)
                P.dma(SP, lambda e, t=t: e.dma_start(out=ffn_s[t * NS:(t + 1) * NS, 0, :], in_=sf[t * NS:(t + 1) * NS, 1, :]), "fso")
                for n in range(NS):
                    elem_dma(ffn_s[t * NS + n, 1, :].rearrange("(c p) -> p c", p=128), GS[:, :, n], "fso", reads=["GS"])
                if t == NT - 1:
                    for k in range(2):
                        elem_dma(ffn_p[k, :].rearrange("(c p) -> p c", p=128), FH[:, :, k], "fpo", reads=["FH"])
                P.barrier()
                P.dma(SP, lambda e: e.dma_start(out=GF, in_=final_norm_g.partition_broadcast(128)), "gf", writes=["GF"])
                for tb in range(5):
                    rows, c0 = tb_info(tb)
                    slot = tb % 2
                    P.dma(SP, lambda e, rows=rows, c0=c0, slot=slot, t=t: e.dma_start(out=XT[:rows, slot, :], in_=XMID[t, c0:c0 + rows, :]), ("xt", slot),
                          reads=[("xmid", tb, q) for q in range(16)], writes=[("XT", slot)])
                    norm_stats(slot, rows)
                    P.op(DVE, lambda e, rows=rows, slot=slot: e.scalar_tensor_tensor(out=XT[:rows, slot, :], in0=XT[:rows, slot, :], scalar=RS[:rows, slot:slot + 1],
                                                                                      in1=GF[:rows, :], op0=ALU.mult, op1=ALU.mult),
                         reads=[("RS", slot), "GF"], writes=[("XT", slot)])
                    P.dma(SP, lambda e, rows=rows, slot=slot, t=t, tb=tb: e.dma_start(out=yrows(t, tb), in_=XT[:rows, slot, :]), ("yo", slot),
                          reads=[("XT", slot)])

        blocks = []
        Pd = Prog(nc, dry=True)
        emit_all(Pd, blocks)
        Pr = Prog(nc, dry=False)
        emit_all(Pr, blocks)
        Pr.emit(st)
    return nc


_NT = 4
_W_KEYS = ["norm_mix_g", "w_in", "conv_w", "conv_b", "ln_g", "ln_b", "w_conv_out", "lam_re", "lam_im", "log_dt",
           "b_re", "b_im", "c_re", "c_im", "d_skip", "w_glu", "w_ssm_out", "w_o", "norm_ffn_g", "w_up",
           "ffn_conv_w", "ffn_conv_b", "w_down"]


def make_in_map(inputs, b, NT, seq0=0):
    f = lambda a: np.ascontiguousarray(np.asarray(a, dtype=np.float32))
    nsa = NT * NS
    m = {}
    m["xp"] = f(inputs["x_prompt"][b, seq0:seq0 + NT * NP])
    m["xs"] = f(inputs["x_sample"][b * nsa:(b + 1) * nsa, 0])
    m["sc"] = f(inputs["state_conv"][0, b * nsa:(b + 1) * nsa])
    m["sr"] = f(inputs["state_ssm_re"][0, b * nsa:(b + 1) * nsa]).reshape(nsa, 8192)
    m["si"] = f(inputs["state_ssm_im"][0, b * nsa:(b + 1) * nsa]).reshape(nsa, 8192)
    m["sf"] = f(inputs["state_ffn_conv"][0, b * nsa:(b + 1) * nsa])
    for k in _W_KEYS:
        m[k] = f(inputs[k][0])
    m["final_norm_g"] = f(inputs["final_norm_g"])
    m["ident"] = np.eye(128, dtype=np.float32)
    return m


def kernel(**inputs):
    NT = _NT
    nc = build_nc(NT)
    maps4 = [make_in_map(inputs, b, NT) for b in range(4)]
    in_maps = [maps4[c % 4] for c in range(8)]
    res = run_bass_kernel_spmd(nc, in_maps, core_ids=list(range(8)))
    r = res.results
    B, S = 4, NT * NP
    y_prompt = np.stack([r[b]["y_p"] for b in range(4)]).astype(np.float32)
    y_sample = np.concatenate([r[b]["y_s"] for b in range(4)])[:, None, :].astype(np.float32)
    conv_prompt = np.stack([r[b]["conv_p"] for b in range(4)])[None].astype(np.float32)
    conv_sample = np.concatenate([r[b]["conv_s"] for b in range(4)])[None].astype(np.float32)
    ssr_p = np.stack([r[b]["ssr_p"].reshape(128, 64) for b in range(4)])[None].astype(np.float32)
    ssi_p = np.stack([r[b]["ssi_p"].reshape(128, 64) for b in range(4)])[None].astype(np.float32)
    ssr_s = np.concatenate([r[b]["ssr_s"].reshape(-1, 128, 64) for b in range(4)])[None].astype(np.float32)
    ssi_s = np.concatenate([r[b]["ssi_s"].reshape(-1, 128, 64) for b in range(4)])[None].astype(np.float32)
    ffn_p = np.stack([r[b]["ffn_p"] for b in range(4)])[None].astype(np.float32)
    ffn_s = np.concatenate([r[b]["ffn_s"] for b in range(4)])[None].astype(np.float32)
    return (y_prompt, y_sample, conv_prompt, conv_sample, ssr_p, ssi_p, ssr_s, ssi_s, ffn_p, ffn_s)
```

```python
import math
import numpy as np
from contextlib import ExitStack
import concourse.bass as bass
import concourse.mybir as mybir
from concourse.bass_utils import run_bass_kernel_spmd

F32 = mybir.dt.float32
BF16 = mybir.dt.bfloat16
AF = mybir.ActivationFunctionType
ALU = mybir.AluOpType
AX = mybir.AxisListType

PE, ACT, DVE, POOL, SP = "pe", "act", "dve", "pool", "sp"

D = 4096
CWD = 2048
SWD = 2048
DFF = 11008
INW = 14336
NP = 512
NS = 8
N = NP + NS
HN = N // 2
PADH = 30
PAD = 256
NSLOT = 6
EPS = 1e-6
NCH = 43


class _Rec:
    __slots__ = ("eng", "fn", "waits", "needs_inc", "dma_sem", "dma_val", "cnt", "seq")

    def __init__(self, eng, fn):
        self.eng = eng
        self.fn = fn
        self.waits = []
        self.needs_inc = False
        self.dma_sem = None
        self.dma_val = 0
        self.cnt = 0
        self.seq = 0


class Prog:
    def __init__(self, nc, dry=False):
        self.nc = nc
        self.dry = dry
        self.ops = {e: [] for e in (PE, ACT, DVE, POOL, SP)}
        self.res = {}
        self.dma_cnt = {}
        self.last_dma = {}
        self.pending = {}

    def _deps(self, rec, reads, writes, acc):
        if self.dry:
            return
        deps = []
        pend = self.pending.get(rec.eng)
        if pend:
            deps.extend(pend)
            self.pending[rec.eng] = None
        for r in reads:
            st = self.res.get(r)
            if st:
                deps.extend(st[0])
        for w in writes:
            st = self.res.get(w)
            if st:
                deps.extend(st[0])
                deps.extend(st[1])
        for w in acc:
            st = self.res.get(w)
            if st:
                deps.extend(st[0])
                deps.extend(st[1])
        best = {}
        for d in deps:
            if d is rec:
                continue
            if d.dma_sem is None and rec.dma_sem is None and d.eng == rec.eng and rec.eng == PE:
                continue
            key = ("d", d.dma_sem) if d.dma_sem is not None else ("e", d.eng)
            val = d.dma_val if d.dma_sem is not None else d.seq
            cur = best.get(key)
            if cur is None or val > cur[0]:
                best[key] = (val, d)
        for _, d in best.values():
            rec.waits.append(d)
            if d.dma_sem is None:
                d.needs_inc = True
        for r in reads:
            self.res.setdefault(r, [[], []])[1].append(rec)
        for w in writes:
            self.res[w] = [[rec], []]
        for w in acc:
            st = self.res.setdefault(w, [[], []])
            st[0].append(rec)
            st[1] = []

    def op(self, eng, fn, reads=(), writes=(), acc=()):
        rec = _Rec(eng, fn)
        rec.seq = len(self.ops[eng]) + 1
        self._deps(rec, reads, writes, acc)
        if not self.dry:
            self.ops[eng].append(rec)
        return rec

    def dma(self, eng, fn, sem, reads=(), writes=(), acc=()):
        rec = _Rec(eng, fn)
        rec.dma_sem = sem
        self.dma_cnt[sem] = self.dma_cnt.get(sem, 0) + 16
        rec.dma_val = self.dma_cnt[sem]
        self._deps(rec, reads, writes, acc)
        if not self.dry:
            self.ops[eng].append(rec)
            self.last_dma[sem] = rec
        return rec

    def barrier(self):
        if self.dry:
            return
        snap = []
        for e, lst in self.ops.items():
            for rec in reversed(lst):
                if rec.dma_sem is None:
                    snap.append(rec)
                    break
        for k, rec in self.last_dma.items():
            if not (isinstance(k, tuple) and k[0] == "w"):
                snap.append(rec)
        for e in (PE, ACT, DVE, SP):
            self.pending[e] = list(snap)

    def emit(self, stack):
        nc = self.nc
        esem = {e: stack.enter_context(nc.semaphore("es_" + e)) for e in (PE, ACT, DVE, POOL)}
        dsem = {}
        for k in self.dma_cnt:
            dsem[k] = stack.enter_context(nc.semaphore("ds_%d" % len(dsem)))
        for e, lst in self.ops.items():
            c = 0
            for rec in lst:
                if rec.dma_sem is None and rec.needs_inc:
                    c += 1
                rec.cnt = c
            assert c < 65000, (e, c)
        for k, v in self.dma_cnt.items():
            assert v < 65000, (k, v)
        block = stack.enter_context(nc.Block())

        def run(e, eng, final=False):
            waited = {}
            for rec in self.ops[e]:
                for d in rec.waits:
                    if d.dma_sem is not None:
                        key, val, sem = ("d", d.dma_sem), d.dma_val, dsem[d.dma_sem]
                    else:
                        key, val, sem = ("e", d.eng), d.cnt, esem[d.eng]
                    if waited.get(key, 0) >= val:
                        continue
                    waited[key] = val
                    eng.wait_ge(sem, val)
                ins = rec.fn(eng)
                if rec.dma_sem is not None:
                    ins.then_inc(dsem[rec.dma_sem], 16)
                elif rec.needs_inc:
                    ins.then_inc(esem[e], 1)
            if final:
                for k, v in self.dma_cnt.items():
                    if waited.get(("d", k), 0) < v:
                        eng.wait_ge(dsem[k], v)

        block.tensor(lambda eng: run(PE, eng))
        block.scalar(lambda eng: run(ACT, eng))
        block.vector(lambda eng: run(DVE, eng))
        block.gpsimd(lambda eng: run(POOL, eng))
        block.sync(lambda eng: run(SP, eng, final=True))


def build_nc(NT):
    nc = bass.Bass("TRN2", target_bir_lowering=False)
    NSA = NT * NS

    def din(name, shape):
        return nc.dram_tensor(name, list(shape), F32, kind="ExternalInput").ap()

    def dout(name, shape):
        return nc.dram_tensor(name, list(shape), F32, kind="ExternalOutput").ap()

    xp = din("xp", [NT * NP, D]); xs = din("xs", [NSA, D])
    sc = din("sc", [NSA, 30, CWD]); sr = din("sr", [NSA, 8192]); si = din("si", [NSA, 8192])
    sf = din("sf", [NSA, 2, DFF])
    norm_mix_g = din("norm_mix_g", [D]); w_in = din("w_in", [D, INW])
    conv_w = din("conv_w", [31, CWD]); conv_b = din("conv_b", [CWD])
    ln_g = din("ln_g", [CWD]); ln_b = din("ln_b", [CWD]); w_conv_out = din("w_conv_out", [CWD, D])
    lam_re = din("lam_re", [128, 64]); lam_im = din("lam_im", [128, 64]); log_dt = din("log_dt", [128])
    b_re = din("b_re", [128, 64, 16]); b_im = din("b_im", [128, 64, 16])
    c_re = din("c_re", [128, 16, 64]); c_im = din("c_im", [128, 16, 64])
    d_skip = din("d_skip", [SWD]); w_glu = din("w_glu", [SWD, SWD]); w_ssm_out = din("w_ssm_out", [SWD, D])
    w_o = din("w_o", [D, D]); norm_ffn_g = din("norm_ffn_g", [D]); w_up = din("w_up", [D, 2 * DFF])
    ffn_conv_w = din("ffn_conv_w", [3, DFF]); ffn_conv_b = din("ffn_conv_b", [DFF])
    w_down = din("w_down", [DFF, D]); final_norm_g = din("final_norm_g", [D])
    ident = din("ident", [128, 128])

    y_p = dout("y_p", [NT * NP, D]); y_s = dout("y_s", [NSA, D])
    conv_p = dout("conv_p", [30, CWD]); conv_s = dout("conv_s", [NSA, 30, CWD])
    ssr_p = dout("ssr_p", [8192]); ssi_p = dout("ssi_p", [8192])
    ssr_s = dout("ssr_s", [NSA, 8192]); ssi_s = dout("ssi_s", [NSA, 8192])
    ffn_p = dout("ffn_p", [2, DFF]); ffn_s = dout("ffn_s", [NSA, 2, DFF])

    XMID = nc.dram_tensor("xmid", [NT, N, D], F32).ap()
    LMD = nc.dram_tensor("lmd", [64, 128, 4, 128], BF16).ap()
    TBL = nc.dram_tensor("tbl", [64, 128, 2, 512], F32).ap()

    st = ExitStack()
    with st:
        def sb(name, shape, dt=F32):
            return st.enter_context(nc.sbuf_tensor(name, list(shape), dt))

        R1 = sb("R1", [128, 32 * N], BF16)
        RA = sb("RA", [128, 34240], BF16)
        R5 = sb("R5", [128, 16 * N], BF16)
        TA = sb("TA", [128, 7680], BF16)
        WR = sb("WR", [128, NSLOT, 16, 256], BF16)
        IDENT = sb("IDENT", [128, 128]); ONES = sb("ONES", [128, 128])
        G1 = sb("G1", [128, 32, 1]); G2 = sb("G2", [128, 32, 1])
        CWT = sb("CWT", [128, 16, 31]); CBS = sb("CBS", [128, 16, 1])
        LNG = sb("LNG", [128, 16, 1]); LNB = sb("LNB", [128, 16, 1])
        FCW = sb("FCW", [128, 86, 3]); FCB = sb("FCB", [128, 86, 1]); DSK = sb("DSK", [128, 16, 1])
        PWR = sb("PWR", [128, 64, 9]); PWI = sb("PWI", [128, 64, 9]); PWN = sb("PWN", [128, 64, 9])
        HALO = sb("HALO", [128, 16, PADH]); FH = sb("FH", [128, 86, 2]); CARRY = sb("CARRY", [128, 64, 2])
        SS = sb("SS", [128, 2]); RS = sb("RS", [128, 2])
        RMAG = sb("RMAG", [128, 64, 1])
        TB = sb("TB", [128, 2, 2, 512])
        PS = st.enter_context(nc.psum_tensor("PS", [128, 8, 512], F32))

        def f32v(t, b0, nbytes):
            return t[:, b0 // 2:(b0 + nbytes) // 2].bitcast(F32)

        def b16v(t, b0, nbytes):
            return t[:, b0 // 2:(b0 + nbytes) // 2]

        XN = R1[:, :].rearrange("p (c n) -> p c n", c=32)
        GF = f32v(R1, 0, 4 * D)
        R2o, R3o = 0, 35200
        GLU = f32v(RA, R2o, 16 * (PADH + N) * 4).rearrange("p (c n) -> p c n", c=16)
        U16 = b16v(RA, R2o, 16 * N * 2).rearrange("p (c n) -> p c n", c=16)
        SG16 = b16v(RA, R2o + 16 * N * 2, 16 * N * 2).rearrange("p (c n) -> p c n", c=16)
        XT = f32v(RA, R2o, 2 * D * 4).rearrange("p (s n) -> p s n", s=2)
        MG16 = b16v(RA, R2o, 32 * N * 2).rearrange("p (c n) -> p c n", c=32)
        CV = f32v(RA, R3o, 16 * N * 4).rearrange("p (c n) -> p c n", c=16)
        SGG16 = b16v(RA, R3o, 16 * N * 2).rearrange("p (c n) -> p c n", c=16)
        XA = f32v(RA, R3o + 16640, 2 * N * 4).rearrange("p (c n) -> p c n", c=2)
        T3B = f32v(RA, R3o + 16640 + 4160, 3 * NP * 4).rearrange("p (c n) -> p c n", c=3)
        HSM = f32v(RA, R3o + 16640 + 4160 + 6144, 2 * NS * 4).rearrange("p (c n) -> p c n", c=2)
        H16S = b16v(RA, R3o + 16640 + 12416, 4 * N * 2).rearrange("p (s r n) -> p s r n", s=2, r=2)
        SQJ = b16v(RA, R3o, D * 2)
        H16F = b16v(RA, 0, NCH * N * 2).rearrange("p (c n) -> p c n", c=NCH)
        FSo = NCH * N * 2
        FS = f32v(RA, FSo, 86 * 16 * 4).rearrange("p (c r) -> p c r", c=86)
        GS = f32v(RA, FSo + 86 * 16 * 4, 86 * 8 * 4).rearrange("p (c r) -> p c r", c=86)
        CB16 = R5[:, :].rearrange("p (c n) -> p c n", c=16)
        SIGT = f32v(TA, 0, 2080)
        SQ = f32v(TA, 2080, 2080); MU = f32v(TA, 4160, 2080); RSTD = f32v(TA, 6240, 2080)
        CSTG = f32v(TA, 8320, 2048).rearrange("p (s b n) -> p s b n", s=2, b=2)
        TMPS = f32v(TA, 10368, 960).rearrange("p (n k) -> p n k", n=8)
        CSS = f32v(TA, 11328, 32)
        YT = f32v(TA, 2080, 2080)
        LMS = b16v(TA, 4160, 2048).rearrange("p (s m n) -> p s m n", s=2, m=4)
        H0 = f32v(TA, 6208, 4096).rearrange("p (s r n) -> p s r n", s=64, r=2)
        HS = f32v(TA, 10304, 4096).rearrange("p (s r n) -> p s r n", s=64, r=2)
        GT1 = f32v(TA, 14400, 480)
        MA = f32v(TA, 2080, 4160).rearrange("p (m n) -> p m n", m=2)
        T2 = f32v(TA, 6240, 2080)
        XSL = f32v(TA, 8320, 3072).rearrange("p (s n) -> p s n", s=3)
        XMS = f32v(TA, 11392, 3072).rearrange("p (s n) -> p s n", s=3)
        G32 = f32v(TA, 0, 2088)
        GCT = f32v(TA, 2088, 2080)
        SILU = f32v(TA, 4168, 4160).rearrange("p (m n) -> p m n", m=2)
        XSL2 = f32v(TA, 8328, 3072).rearrange("p (s n) -> p s n", s=3)
        XMS2 = f32v(TA, 11400, 3072).rearrange("p (s n) -> p s n", s=3)

        CSTG = f32v(TA, 8320, 2048).rearrange("p (s b n) -> p s b n", s=2, b=2)

        def emit_all(P, blocks):
            state = {"wi": 0, "acc": 0, "tm": 0, "issued": 0}

            def wissue(i):
                if i >= len(blocks):
                    return
                W, row0, nk, col0, ncols = blocks[i]
                slot = i % NSLOT
                src = W[row0:row0 + nk * 128, col0:col0 + ncols].rearrange("(k p) n -> p k n", p=128)
                P.dma(POOL, lambda e, slot=slot, src=src, nk=nk, ncols=ncols: e.dma_start(out=WR[:, slot, 0:nk, 0:ncols], in_=src),
                      ("w", slot), writes=[("w", slot)])

            def wnext(W, row0, nk, col0, ncols):
                i = state["wi"]
                state["wi"] += 1
                if P.dry:
                    blocks.append((W, row0, nk, col0, ncols))
                else:
                    assert blocks[i][1:] == (row0, nk, col0, ncols)
                return i % NSLOT

            def wrelease(slot_unused=None):
                if P.dry:
                    return
                wissue(state["issued"])
                state["issued"] += 1

            if not P.dry:
                for i in range(min(NSLOT, len(blocks))):
                    wissue(i)
                state["issued"] = NSLOT

            def next_acc():
                b = state["acc"] * 2
                state["acc"] = (state["acc"] + 1) % 4
                return b

            def psv(b):
                return PS[:, b:b + 2, 0:HN]

            def half(ap2d):
                return ap2d.rearrange("p (h n) -> p h n", h=2)

            def mm_group(slots, K, m, rhs_fn, b):
                for k in range(K):
                    slot = slots[k // 16]
                    rap, rkey = rhs_fn(k)
                    for h in range(2):
                        P.op(PE, lambda e, slot=slot, k=k, m=m, h=h, rap=rap, b=b, K=K: e.matmul(
                            PS[:, b + h, 0:HN], lhsT=WR[:, slot, k % 16, m * 128:(m + 1) * 128],
                            rhs=rap[:, h * HN:(h + 1) * HN], start=(k == 0), stop=(k == K - 1)),
                            reads=[("w", slot), rkey], acc=[("ps", b + h)])

            def linear_fm(W, col0, ncols_total, K, rhs_fn, epilogue, row0=0):
                nblk = (K + 15) // 16
                ci = 0
                for s0 in range(col0, col0 + ncols_total, 256):
                    ncols = min(256, col0 + ncols_total - s0)
                    slots = [wnext(W, row0 + kb * 2048, min(16, K - kb * 16), s0, ncols) for kb in range(nblk)]
                    for m in range(ncols // 128):
                        b = next_acc()
                        mm_group(slots, K, m, rhs_fn, b)
                        epilogue(ci, b)
                        ci += 1
                    for _ in slots:
                        wrelease()

            def tb_info(tb):
                return (128, tb * 128) if tb < 4 else (NS, NP)

            def linear_tm(W, row0, K, lhs_fn, epilogue):
                nblk = (K + 15) // 16
                for q in range(16):
                    banks = []
                    for tb in range(5):
                        banks.append(state["tm"] % 8)
                        state["tm"] += 1
                    for kb in range(nblk):
                        nk = min(16, K - kb * 16)
                        slot = wnext(W, row0 + kb * 2048, nk, q * 256, 256)
                        for tb in range(5):
                            rows, c0 = tb_info(tb)
                            for k in range(nk):
                                kk = kb * 16 + k
                                lap, lkey = lhs_fn(kk)
                                P.op(PE, lambda e, slot=slot, k=k, kk=kk, lap=lap, rows=rows, c0=c0, bk=banks[tb], K=K: e.matmul(
                                    PS[:rows, bk, 0:256], lhsT=lap[:, c0:c0 + rows], rhs=WR[:, slot, k, 0:256],
                                    start=(kk == 0), stop=(kk == K - 1)),
                                    reads=[("w", slot), lkey], acc=[("ps", banks[tb])])
                        wrelease()
                    for tb in range(5):
                        epilogue(q, tb, banks[tb])

            def elem_dma(out_ap, in_ap, sem, reads=(), writes=(), acc=()):
                P.dma(SP, lambda e, o=out_ap, i=in_ap: e.dma_start(out=o, in_=i, allow_slow_non_contiguous=True), sem, reads=reads, writes=writes, acc=acc)

            P.dma(SP, lambda e: e.dma_start(out=IDENT[:], in_=ident), "c0", writes=["IDENT"])
            P.op(DVE, lambda e: e.memset(ONES[:], 1.0), writes=["ONES"])
            P.op(DVE, lambda e: e.memset(HALO[:], 0.0), writes=["HALO"])
            P.op(DVE, lambda e: e.memset(FH[:], 0.0), writes=["FH"])
            P.op(DVE, lambda e: e.memset(CARRY[:], 0.0), writes=["CARRY"])
            elem_dma(G1[:, :, 0], norm_mix_g.rearrange("(c p) -> p c", p=128), "c1", writes=["G1"])
            elem_dma(G2[:, :, 0], norm_ffn_g.rearrange("(c p) -> p c", p=128), "c2", writes=["G2"])
            for k in range(31):
                elem_dma(CWT[:, :, k], conv_w[k].rearrange("(c p) -> p c", p=128), "c3", writes=[], reads=[])
            P.res["CWT"] = [[P.last_dma.get("c3")] if not P.dry else [], []]
            elem_dma(CBS[:, :, 0], conv_b.rearrange("(c p) -> p c", p=128), "c4", writes=["CBS"])
            elem_dma(LNG[:, :, 0], ln_g.rearrange("(c p) -> p c", p=128), "c5", writes=["LNG"])
            elem_dma(LNB[:, :, 0], ln_b.rearrange("(c p) -> p c", p=128), "c6", writes=["LNB"])
            for k in range(3):
                elem_dma(FCW[:, :, k], ffn_conv_w[k].rearrange("(c p) -> p c", p=128), "c7")
            P.res["FCW"] = [[P.last_dma.get("c7")] if not P.dry else [], []]
            elem_dma(FCB[:, :, 0], ffn_conv_b.rearrange("(c p) -> p c", p=128), "c8", writes=["FCB"])
            elem_dma(DSK[:, :, 0], d_skip.rearrange("(c p) -> p c", p=128), "c9", writes=["DSK"])

            def ra32(b0, shape):
                n = int(np.prod(shape))
                v = f32v(RA, b0, n * 4)
                if len(shape) == 2:
                    return v.rearrange("p (a b) -> p a b", a=shape[0])
                if len(shape) == 3:
                    return v.rearrange("p (a b c) -> p a b c", a=shape[0], b=shape[1])
                return v

            o = 32768
            names = ["LR", "LI", "LDT", "DT", "LRD", "LID", "MAG", "YA", "COSV", "SINV", "ABR", "ABI", "DEN", "NR", "T1", "T2s", "FRE", "FIM"]
            V = {}
            for nm in names:
                V[nm] = ra32(o, [64, 1]); o += 256
            BRE = ra32(o, [64, 16]); o += 4096
            BIM = ra32(o, [64, 16]); o += 4096
            BBR = ra32(o, [64, 16]); o += 4096
            BBI = ra32(o, [64, 16]); o += 4096
            TB1 = ra32(o, [64, 16]); o += 4096
            STG = b16v(RA, o, 1024).rearrange("p (s n) -> p s n", s=4); o += 1024
            STG2 = b16v(RA, o, 1024).rearrange("p (s n) -> p s n", s=4); o += 1024
            ZB = ra32(0, [64, 128])
            ZB4 = f32v(RA, 0, 32768).rearrange("p (c j n) -> p c j n", c=16, j=4)

            lam2 = lambda a: a.rearrange("(s two) p -> two p s", two=2)
            for gl in range(2):
                elem_dma(V["LR"][gl * 64:(gl + 1) * 64, :, 0], lam2(lam_re)[gl], "s0")
                elem_dma(V["LI"][gl * 64:(gl + 1) * 64, :, 0], lam2(lam_im)[gl], "s0")
                P.dma(SP, lambda e, gl=gl: e.dma_start(out=V["LDT"][gl * 64:(gl + 1) * 64, :, 0],
                                                       in_=log_dt.rearrange("(s two) -> two s", two=2)[gl].partition_broadcast(64), allow_slow_non_contiguous=True), "s0")
                P.dma(SP, lambda e, gl=gl: e.dma_start(out=BRE[gl * 64:(gl + 1) * 64], in_=b_re.rearrange("(s two) p h -> two p s h", two=2)[gl]), "s0")
                P.dma(SP, lambda e, gl=gl: e.dma_start(out=BIM[gl * 64:(gl + 1) * 64], in_=b_im.rearrange("(s two) p h -> two p s h", two=2)[gl]), "s0")
            if not P.dry:
                P.res["SPRM"] = [[P.last_dma["s0"]], []]

            def dve(fn, r=(), w=()):
                P.op(DVE, fn, reads=r, writes=w)

            def actop(fn, r=(), w=()):
                P.op(ACT, fn, reads=r, writes=w)

            S_ = "SPRM"
            actop(lambda e: e.activation(out=V["DT"][:], in_=V["LDT"][:], func=AF.Exp), [S_], [S_])
            dve(lambda e: e.tensor_tensor(out=V["LRD"][:], in0=V["LR"][:], in1=V["DT"][:], op=ALU.mult), [S_], [S_])
            dve(lambda e: e.tensor_tensor(out=V["LID"][:], in0=V["LI"][:], in1=V["DT"][:], op=ALU.mult), [S_], [S_])
            actop(lambda e: e.activation(out=V["MAG"][:], in_=V["LRD"][:], func=AF.Exp, scale=1.0 / 32), [S_], [S_])
            actop(lambda e: e.activation(out=V["SINV"][:], in_=V["LID"][:], func=AF.Sin, scale=1.0 / 32), [S_], [S_])
            actop(lambda e: e.activation(out=V["COSV"][:], in_=V["LID"][:], func=AF.Sin, scale=1.0 / 32, bias=0.5 * math.pi), [S_], [S_])
            dve(lambda e: e.tensor_tensor(out=V["ABR"][:], in0=V["MAG"][:], in1=V["COSV"][:], op=ALU.mult), [S_], [S_])
            dve(lambda e: e.tensor_tensor(out=V["ABI"][:], in0=V["MAG"][:], in1=V["SINV"][:], op=ALU.mult), [S_], [S_])
            for _sq in range(5):
                dve(lambda e: e.tensor_tensor(out=V["T1"][:], in0=V["ABR"][:], in1=V["ABR"][:], op=ALU.mult), [S_], [S_])
                dve(lambda e: e.tensor_tensor(out=V["T2s"][:], in0=V["ABI"][:], in1=V["ABI"][:], op=ALU.mult), [S_], [S_])
                dve(lambda e: e.tensor_tensor(out=V["YA"][:], in0=V["ABR"][:], in1=V["ABI"][:], op=ALU.mult), [S_], [S_])
                dve(lambda e: e.tensor_tensor(out=V["ABR"][:], in0=V["T1"][:], in1=V["T2s"][:], op=ALU.subtract), [S_], [S_])
                dve(lambda e: e.tensor_scalar(out=V["ABI"][:], in0=V["YA"][:], scalar1=2.0, scalar2=None, op0=ALU.mult), [S_], [S_])
            dve(lambda e: e.tensor_tensor(out=V["DEN"][:], in0=V["LR"][:], in1=V["LR"][:], op=ALU.mult), [S_], [S_])
            dve(lambda e: e.tensor_tensor(out=V["T1"][:], in0=V["LI"][:], in1=V["LI"][:], op=ALU.mult), [S_], [S_])
            dve(lambda e: e.tensor_tensor(out=V["DEN"][:], in0=V["DEN"][:], in1=V["T1"][:], op=ALU.add), [S_], [S_])
            dve(lambda e: e.reciprocal(out=V["DEN"][:], in_=V["DEN"][:]), [S_], [S_])
            dve(lambda e: e.tensor_scalar(out=V["NR"][:], in0=V["ABR"][:], scalar1=-1.0, scalar2=None, op0=ALU.add), [S_], [S_])
            dve(lambda e: e.tensor_tensor(out=V["T1"][:], in0=V["NR"][:], in1=V["LR"][:], op=ALU.mult), [S_], [S_])
            dve(lambda e: e.tensor_tensor(out=V["T2s"][:], in0=V["ABI"][:], in1=V["LI"][:], op=ALU.mult), [S_], [S_])
            dve(lambda e: e.tensor_tensor(out=V["T1"][:], in0=V["T1"][:], in1=V["T2s"][:], op=ALU.add), [S_], [S_])
            dve(lambda e: e.tensor_tensor(out=V["FRE"][:], in0=V["T1"][:], in1=V["DEN"][:], op=ALU.mult), [S_], [S_])
            dve(lambda e: e.tensor_tensor(out=V["T1"][:], in0=V["ABI"][:], in1=V["LR"][:], op=ALU.mult), [S_], [S_])
            dve(lambda e: e.tensor_tensor(out=V["T2s"][:], in0=V["NR"][:], in1=V["LI"][:], op=ALU.mult), [S_], [S_])
            dve(lambda e: e.tensor_tensor(out=V["T1"][:], in0=V["T1"][:], in1=V["T2s"][:], op=ALU.subtract), [S_], [S_])
            dve(lambda e: e.tensor_tensor(out=V["FIM"][:], in0=V["T1"][:], in1=V["DEN"][:], op=ALU.mult), [S_], [S_])
            dve(lambda e: e.tensor_copy(out=PWR[:, :, 0:1], in_=V["ABR"][:]), [S_], ["PW"])
            dve(lambda e: e.tensor_copy(out=PWI[:, :, 0:1], in_=V["ABI"][:]), [S_], ["PW"])
            for j in range(8):
                dve(lambda e, j=j: e.tensor_tensor(out=V["T1"][:], in0=PWR[:, :, j:j + 1], in1=PWR[:, :, j:j + 1], op=ALU.mult), ["PW", S_], [S_])
                dve(lambda e, j=j: e.tensor_tensor(out=V["T2s"][:], in0=PWI[:, :, j:j + 1], in1=PWI[:, :, j:j + 1], op=ALU.mult), ["PW", S_], [S_])
                dve(lambda e, j=j: e.tensor_tensor(out=PWR[:, :, j + 1:j + 2], in0=V["T1"][:], in1=V["T2s"][:], op=ALU.subtract), [S_], ["PW"])
                dve(lambda e, j=j: e.tensor_tensor(out=V["T1"][:], in0=PWR[:, :, j:j + 1], in1=PWI[:, :, j:j + 1], op=ALU.mult), ["PW", S_], [S_])
                dve(lambda e, j=j: e.tensor_scalar(out=PWI[:, :, j + 1:j + 2], in0=V["T1"][:], scalar1=2.0, scalar2=None, op0=ALU.mult), [S_], ["PW"])
            dve(lambda e: e.tensor_scalar(out=PWN[:], in0=PWI[:], scalar1=-1.0, scalar2=None, op0=ALU.mult), ["PW"], ["PW"])
            bc = lambda a: a.to_broadcast([128, 64, 16])
            dve(lambda e: e.tensor_tensor(out=BBR[:], in0=BRE[:], in1=bc(V["FRE"][:]), op=ALU.mult), [S_], [S_])
            dve(lambda e: e.tensor_tensor(out=TB1[:], in0=BIM[:], in1=bc(V["FIM"][:]), op=ALU.mult), [S_], [S_])
            dve(lambda e: e.tensor_tensor(out=BBR[:], in0=BBR[:], in1=TB1[:], op=ALU.subtract), [S_], [S_])
            dve(lambda e: e.tensor_tensor(out=BBI[:], in0=BIM[:], in1=bc(V["FRE"][:]), op=ALU.mult), [S_], [S_])
            dve(lambda e: e.tensor_tensor(out=TB1[:], in0=BRE[:], in1=bc(V["FIM"][:]), op=ALU.mult), [S_], [S_])
            dve(lambda e: e.tensor_tensor(out=BBI[:], in0=BBI[:], in1=TB1[:], op=ALU.add), [S_], [S_])

            def transpose_store(mi, neg):
                for s4 in range(16):
                    bk = s4 % 2
                    for j in range(4):
                        s = s4 * 4 + j
                        P.op(PE, lambda e, s=s, j=j, bk=bk: e.transpose(out=PS[:, bk, j * 128:(j + 1) * 128], in_=ZB[:, s, :], identity=IDENT[:]),
                             reads=["ZB", "IDENT"], acc=[("ps", bk)])
                    stg = STG if s4 % 2 == 0 else STG2
                    skey = "STG%d" % (s4 % 2)
                    P.op(ACT, lambda e, bk=bk, stg=stg, neg=neg: e.activation(out=stg[:], in_=PS[:, bk, :].rearrange("p (j n) -> p j n", j=4),
                                                                             func=AF.Copy, scale=(-1.0 if neg else 1.0)),
                         writes=[("ps", bk), skey])
                    P.dma(SP, lambda e, s4=s4, stg=stg, mi=mi: e.dma_start(out=LMD[s4 * 4:(s4 + 1) * 4, :, mi, :].rearrange("s p n -> p s n"), in_=stg[:]),
                          "lm%d" % (s4 % 2), reads=[skey], writes=[("LMD", mi, s4)])

            dve(lambda e: e.memset(ZB[:], 0.0), [], ["ZB"])
            for mi, BB in ((0, BBR), (1, BBI)):
                BB4 = BB[:].rearrange("p (c j) h -> p c j h", j=4)
                for gl in range(2):
                    for j in range(4):
                        c0 = 32 * j + 16 * gl
                        dve(lambda e, gl=gl, j=j, c0=c0, BB4=BB4: e.tensor_copy(out=ZB4[gl * 64:(gl + 1) * 64, :, j, c0:c0 + 16],
                                                                               in_=BB4[gl * 64:(gl + 1) * 64, :, j, :]), [S_], ["ZB"])
                transpose_store(mi, False)
            dve(lambda e: e.memset(ZB[:], 0.0), [], ["ZB"])
            for mi, CC in ((2, c_re), (3, c_im)):
                CCv = CC.rearrange("(c r) h p -> r h c p", r=8)
                first = True
                for j in range(4):
                    for gl in range(2):
                        p0 = 32 * j + 16 * gl
                        P.dma(SP, lambda e, j=j, gl=gl, p0=p0, CCv=CCv: e.dma_start(out=ZB4[p0:p0 + 16, :, j, 64 * gl:64 * gl + 64], in_=CCv[2 * j + gl]),
                              "zc", reads=[], acc=["ZB"])
                transpose_store(mi, mi == 3)
            P.barrier()
            actop(lambda e: e.activation(out=RMAG[:], in_=V["LRD"][:], func=AF.Exp), [S_], ["RMAG"])
            WPR = ra32(32768 + 18 * 256 + 5 * 4096 + 2048, [64, 9])
            WPI = ra32(32768 + 18 * 256 + 5 * 4096 + 2048 + 2304, [64, 9])
            TGo = 32768 + 18 * 256
            TG = f32v(RA, TGo, 8 * 256 * 4).rearrange("p (s n) -> p s n", s=8)
            assert TGo + 8192 <= 68480
            EG = f32v(RA, 0, 8 * 2 * 512 * 4).rearrange("p (s r n) -> p s r n", s=8, r=2)
            dve(lambda e: e.tensor_copy(out=V["T1"][:], in_=V["COSV"][:]), [S_], [S_])
            dve(lambda e: e.tensor_copy(out=V["T2s"][:], in_=V["SINV"][:]), [S_], [S_])
            for _sq in range(5):
                dve(lambda e: e.tensor_tensor(out=V["NR"][:], in0=V["T1"][:], in1=V["T1"][:], op=ALU.mult), [S_], [S_])
                dve(lambda e: e.tensor_tensor(out=V["DEN"][:], in0=V["T2s"][:], in1=V["T2s"][:], op=ALU.mult), [S_], [S_])
                dve(lambda e: e.tensor_tensor(out=V["YA"][:], in0=V["T1"][:], in1=V["T2s"][:], op=ALU.mult), [S_], [S_])
                dve(lambda e: e.tensor_tensor(out=V["T1"][:], in0=V["NR"][:], in1=V["DEN"][:], op=ALU.subtract), [S_], [S_])
                dve(lambda e: e.tensor_scalar(out=V["T2s"][:], in0=V["YA"][:], scalar1=2.0, scalar2=None, op0=ALU.mult), [S_], [S_])
            dve(lambda e: e.tensor_copy(out=WPR[:, :, 0:1], in_=V["T1"][:]), [S_], ["WP"])
            dve(lambda e: e.tensor_copy(out=WPI[:, :, 0:1], in_=V["T2s"][:]), [S_], ["WP"])
            for j in range(8):
                dve(lambda e, j=j: e.tensor_tensor(out=V["T1"][:], in0=WPR[:, :, j:j + 1], in1=WPR[:, :, j:j + 1], op=ALU.mult), ["WP", S_], [S_])
                dve(lambda e, j=j: e.tensor_tensor(out=V["T2s"][:], in0=WPI[:, :, j:j + 1], in1=WPI[:, :, j:j + 1], op=ALU.mult), ["WP", S_], [S_])
                dve(lambda e, j=j: e.tensor_tensor(out=WPR[:, :, j + 1:j + 2], in0=V["T1"][:], in1=V["T2s"][:], op=ALU.subtract), [S_], ["WP"])
                dve(lambda e, j=j: e.tensor_tensor(out=V["T1"][:], in0=WPR[:, :, j:j + 1], in1=WPI[:, :, j:j + 1], op=ALU.mult), ["WP", S_], [S_])
                dve(lambda e, j=j: e.tensor_scalar(out=WPI[:, :, j + 1:j + 2], in0=V["T1"][:], scalar1=2.0, scalar2=None, op0=ALU.mult), [S_], ["WP"])
            for g8 in range(8):
                s0 = g8 * 8
                dve(lambda e: e.memset(EG[:, :, 0, 0:1], 1.0), [], ["EG"])
                dve(lambda e: e.memset(EG[:, :, 1, 0:1], 0.0), [], ["EG"])
                for j in range(9):
                    n = 1 << j
                    pr = WPR[:, s0:s0 + 8, j:j + 1].to_broadcast([128, 8, n])
                    pi_ = WPI[:, s0:s0 + 8, j:j + 1].to_broadcast([128, 8, n])
                    sr_, si_ = EG[:, :, 0, 0:n], EG[:, :, 1, 0:n]
                    dr_, di_ = EG[:, :, 0, n:2 * n], EG[:, :, 1, n:2 * n]
                    tg = TG[:, :, 0:n]
                    dve(lambda e, dr_=dr_, sr_=sr_, pr=pr: e.tensor_tensor(out=dr_, in0=sr_, in1=pr, op=ALU.mult), ["WP"], ["EG"])
                    dve(lambda e, tg=tg, si_=si_, pi_=pi_: e.tensor_tensor(out=tg, in0=si_, in1=pi_, op=ALU.mult), ["WP", "EG"], ["TG"])
                    dve(lambda e, dr_=dr_, tg=tg: e.tensor_tensor(out=dr_, in0=dr_, in1=tg, op=ALU.subtract), ["TG"], ["EG"])
                    dve(lambda e, di_=di_, sr_=sr_, pi_=pi_: e.tensor_tensor(out=di_, in0=sr_, in1=pi_, op=ALU.mult), ["WP"], ["EG"])
                    dve(lambda e, tg=tg, si_=si_, pr=pr: e.tensor_tensor(out=tg, in0=si_, in1=pr, op=ALU.mult), ["WP", "EG"], ["TG"])
                    dve(lambda e, di_=di_, tg=tg: e.tensor_tensor(out=di_, in0=di_, in1=tg, op=ALU.add), ["TG"], ["EG"])
                P.dma(SP, lambda e, s0=s0: e.dma_start(out=TBL[s0:s0 + 8].rearrange("s p r t -> p s (r t)"),
                                                        in_=EG.rearrange("p s r t -> p s (r t)")), "tblw", reads=["EG"], writes=[("TBL", g8)])
            P.barrier()

            def xrows(t, tb):
                rows, c0 = tb_info(tb)
                if tb < 4:
                    return xp[t * NP + tb * 128: t * NP + (tb + 1) * 128, :], rows, c0
                return xs[t * NS:(t + 1) * NS, :], rows, c0

            def yrows(t, tb):
                if tb < 4:
                    return y_p[t * NP + tb * 128: t * NP + (tb + 1) * 128, :]
                return y_s[t * NS:(t + 1) * NS, :]

            def norm_stats(slot, rows):
                P.op(ACT, lambda e: e.activation(out=SQJ[:rows, :], in_=XT[:rows, slot, :], func=AF.Square, accum_out=SS[:rows, slot:slot + 1]),
                     reads=[("XT", slot)], writes=["SQJ", ("SS", slot)])
                P.op(ACT, lambda e: e.activation(out=RS[:rows, slot:slot + 1], in_=SS[:rows, slot:slot + 1], func=AF.Sqrt, scale=1.0 / D, bias=EPS),
                     reads=[("SS", slot)], writes=[("RS", slot)])
                P.op(DVE, lambda e: e.reciprocal(out=RS[:rows, slot:slot + 1], in_=RS[:rows, slot:slot + 1]), writes=[("RS", slot)])

            def norm_T(t, src_fn, G, gkey):
                for tb in range(5):
                    src, rows, c0 = src_fn(tb)
                    slot = tb % 2
                    P.dma(SP, lambda e, src=src, rows=rows, slot=slot: e.dma_start(out=XT[:rows, slot, :], in_=src), ("xt", slot),
                          writes=[("XT", slot)])
                    norm_stats(slot, rows)
                    P.op(ACT, lambda e, rows=rows, slot=slot: e.activation(out=XT[:rows, slot, :], in_=XT[:rows, slot, :], func=AF.Identity,
                                                                            scale=RS[:rows, slot:slot + 1]),
                         reads=[("RS", slot)], writes=[("XT", slot)])
                    for c4 in range(8):
                        bk = c4 % 2
                        for j in range(4):
                            c = c4 * 4 + j
                            P.op(PE, lambda e, rows=rows, slot=slot, c=c, j=j, bk=bk: e.transpose(
                                out=PS[:, bk, j * 128:j * 128 + rows], in_=XT[:rows, slot, c * 128:(c + 1) * 128], identity=IDENT[:rows, :rows]),
                                reads=[("XT", slot), "IDENT"], acc=[("ps", bk)])
                        P.op(DVE, lambda e, rows=rows, c4=c4, bk=bk, c0=c0, G=G: e.tensor_tensor(
                            out=XN[:, c4 * 4:(c4 + 1) * 4, c0:c0 + rows],
                            in0=PS[:, bk, :].rearrange("p (j n) -> p j n", j=4)[:, :, 0:rows],
                            in1=G[:, c4 * 4:(c4 + 1) * 4, :].to_broadcast([128, 4, rows]), op=ALU.mult),
                            reads=[gkey], writes=[("ps", bk)], acc=[("XN", c4 * 4 + j) for j in range(4)])

            def xn_rhs(k):
                return XN[:, k, :], ("XN", k)

            for t in range(NT):
                P.barrier()
                norm_T(t, lambda tb, t=t: xrows(t, tb), G1, "G1")
                P.barrier()
                def ep_ain(ci, b):
                    P.op(ACT, lambda e, ci=ci, b=b: e.activation(out=half(GLU[:, ci, PADH:PADH + N]), in_=psv(b), func=AF.Copy),
                         writes=[("ps", b), ("ps", b + 1), ("GLU", ci)])
                linear_fm(w_in, 0, CWD, 32, xn_rhs, ep_ain)

                def ep_agate(ci, b):
                    P.op(ACT, lambda e, b=b: e.activation(out=half(SIGT), in_=psv(b), func=AF.Sigmoid),
                         writes=[("ps", b), ("ps", b + 1), "SIGT"])
                    P.op(DVE, lambda e, ci=ci: e.tensor_tensor(out=GLU[:, ci, PADH:PADH + N], in0=GLU[:, ci, PADH:PADH + N], in1=SIGT, op=ALU.mult),
                         reads=["SIGT"], writes=[("GLU", ci)])
                linear_fm(w_in, CWD, CWD, 32, xn_rhs, ep_agate)
                P.op(DVE, lambda e: e.tensor_copy(out=GLU[:, :, 0:PADH], in_=HALO[:]), reads=["HALO"], acc=[("GLU", c) for c in range(16)])
                scv = sc[t * NS:(t + 1) * NS].rearrange("n k f -> (n k) f")
                for c in range(16):
                    cs = c % 2
                    P.dma(SP, lambda e, c=c, cs=cs, scv=scv: e.dma_start(out=CSTG[:120, cs, :, :],
                                                                  in_=scv[:, c * 128:(c + 1) * 128].rearrange("(b r) f -> r b f", b=2)),
                          ("cstg", cs), writes=[("CSTG", cs)])
                    P.op(ACT, lambda e, c=c: e.activation(out=CV[:, c, 0:NP], in_=GLU[:, c, PADH:PADH + NP], func=AF.Identity,
                                                           scale=CWT[:, c, 30:31], bias=CBS[:, c, :]),
                         reads=[("GLU", c), "CWT", "CBS"], writes=[("CV", c)])
                    for k in range(30):
                        P.op(DVE, lambda e, c=c, k=k: e.scalar_tensor_tensor(out=CV[:, c, 0:NP], in0=GLU[:, c, k:k + NP], scalar=CWT[:, c, k:k + 1],
                                                                              in1=CV[:, c, 0:NP], op0=ALU.mult, op1=ALU.add),
                             reads=[("GLU", c), "CWT"], writes=[("CV", c)])
                    bk = 4 + cs
                    for b2 in range(2):
                        P.op(PE, lambda e, cs=cs, b2=b2, bk=bk: e.transpose(out=PS[:, bk, b2 * 120:(b2 + 1) * 120], in_=CSTG[:120, cs, b2, :],
                                                                             identity=IDENT[:120, :120]),
                             reads=[("CSTG", cs), "IDENT"], acc=[("ps", bk)])
                    P.op(DVE, lambda e, c=c, bk=bk: e.tensor_tensor(out=TMPS[:], in0=PS[:, bk, 0:240].rearrange("p (n k) -> p n k", n=8),
                                                                    in1=CWT[:, c:c + 1, 0:30].to_broadcast([128, 8, 30]), op=ALU.mult),
                         reads=["CWT"], writes=[("ps", bk), "TMPS"])
                    P.op(DVE, lambda e: e.tensor_reduce(out=CSS, in_=TMPS[:], axis=AX.X, op=ALU.add), reads=["TMPS"], writes=["CSS"])
                    P.op(DVE, lambda e, c=c: e.scalar_tensor_tensor(out=CSS, in0=GLU[:, c, PADH + NP:PADH + N], scalar=CWT[:, c, 30:31], in1=CSS,
                                                                     op0=ALU.mult, op1=ALU.add), reads=[("GLU", c), "CWT"], writes=["CSS"])
                    P.op(DVE, lambda e, c=c: e.tensor_scalar(out=CV[:, c, NP:N], in0=CSS, scalar1=CBS[:, c, :], scalar2=None, op0=ALU.add),
                         reads=["CSS", "CBS"], writes=[("CV", c)])
                    P.op(ACT, lambda e, c=c: e.activation(out=SQ, in_=CV[:, c, :], func=AF.Square), reads=[("CV", c)], writes=["SQ"])
                    for h in range(2):
                        P.op(PE, lambda e, c=c, h=h: e.matmul(PS[:, 0 + h, 0:HN], lhsT=ONES[:], rhs=CV[:, c, h * HN:(h + 1) * HN], start=(c == 0), stop=(c == 15)),
                             reads=["ONES", ("CV", c)], acc=[("ps", 0 + h)])
                        P.op(PE, lambda e, c=c, h=h: e.matmul(PS[:, 2 + h, 0:HN], lhsT=ONES[:], rhs=SQ[:, h * HN:(h + 1) * HN], start=(c == 0), stop=(c == 15)),
                             reads=["ONES", "SQ"], acc=[("ps", 2 + h)])
                P.op(DVE, lambda e: e.tensor_copy(out=HALO[:], in_=GLU[:, :, NP:NP + PADH]), reads=[("GLU", c) for c in range(16)], writes=["HALO"])
                P.dma(SP, lambda e, t=t: e.dma_start(out=conv_s[t * NS:(t + 1) * NS, 0:29, :], in_=sc[t * NS:(t + 1) * NS, 1:30, :]), "cso")
                for n in range(NS):
                    elem_dma(conv_s[t * NS + n, 29, :].rearrange("(c p) -> p c", p=128), GLU[:, :, PADH + NP + n], "cso",
                             reads=[("GLU", c) for c in range(16)])
                if t == NT - 1:
                    for c in range(16):
                        elem_dma(conv_p[:, c * 128:(c + 1) * 128].rearrange("k p -> p k"), HALO[:, c, :], "cpo", reads=["HALO"])
                P.op(DVE, lambda e: e.tensor_scalar(out=half(MU), in0=psv(0), scalar1=1.0 / CWD, scalar2=None, op0=ALU.mult),
                     writes=[("ps", 0), ("ps", 1), "MU"])
                P.op(DVE, lambda e: e.tensor_tensor(out=SQ, in0=MU, in1=MU, op=ALU.mult), reads=["MU"], writes=["SQ"])
                P.op(DVE, lambda e: e.scalar_tensor_tensor(out=half(RSTD), in0=psv(2), scalar=1.0 / CWD, in1=half(SQ), op0=ALU.mult, op1=ALU.subtract),
                     reads=["SQ"], writes=[("ps", 2), ("ps", 3), "RSTD"])
                P.op(ACT, lambda e: e.activation(out=RSTD, in_=RSTD, func=AF.Sqrt, bias=EPS), writes=["RSTD"])
                P.op(DVE, lambda e: e.reciprocal(out=RSTD, in_=RSTD), writes=["RSTD"])
                for c in range(16):
                    P.op(DVE, lambda e, c=c: e.tensor_tensor(out=CV[:, c, :], in0=CV[:, c, :], in1=MU, op=ALU.subtract), reads=["MU"], writes=[("CV", c)])
                    P.op(DVE, lambda e, c=c: e.tensor_tensor(out=CV[:, c, :], in0=CV[:, c, :], in1=RSTD, op=ALU.mult), reads=["RSTD"], writes=[("CV", c)])
                    P.op(ACT, lambda e, c=c: e.activation(out=CB16[:, c, :], in_=CV[:, c, :], func=AF.Silu, scale=LNG[:, c, :], bias=LNB[:, c, :]),
                         reads=[("CV", c), "LNG", "LNB"], writes=[("CB", c)])
                P.barrier()
                def ep_u(ci, b):
                    P.op(ACT, lambda e, ci=ci, b=b: e.activation(out=half(U16[:, ci, :]), in_=psv(b), func=AF.Copy),
                         writes=[("ps", b), ("ps", b + 1), ("U", ci)])
                linear_fm(w_in, 2 * CWD, SWD, 32, xn_rhs, ep_u)
                for n in range(NS):
                    row = t * NS + n
                    elem_dma(H0[:, :, 0, n], sr[row, :].rearrange("(s p) -> p s", p=128), "h0", acc=["H0"])
                    elem_dma(H0[:, :, 1, n], si[row, :].rearrange("(s p) -> p s", p=128), "h0", acc=["H0"])

                def bproj(s):
                    c = s // 4
                    sl = s % 2
                    P.dma(SP, lambda e, s=s, sl=sl: e.dma_start(out=LMS[:, sl], in_=LMD[s]), ("lms", sl),
                          reads=[("LMD", mi, s // 4) for mi in range(4)], writes=[("LMS", sl)])
                    P.dma(SP, lambda e, s=s, sl=sl: e.dma_start(out=TB[:, sl], in_=TBL[s]), ("tb", sl),
                          reads=[("TBL", s // 8)], writes=[("TB", sl)])
                    for ri in range(2):
                        for h in range(2):
                            P.op(PE, lambda e, sl=sl, c=c, ri=ri, h=h: e.matmul(PS[:, 2 * ri + h, 0:HN], lhsT=LMS[:, sl, ri, :], rhs=U16[:, c, h * HN:(h + 1) * HN],
                                                                             start=True, stop=True),
                                 reads=[("LMS", sl), ("U", c)], acc=[("ps", 2 * ri + h)])

                def evac(s):
                    for ri in range(2):
                        P.op(ACT, lambda e, ri=ri: e.activation(out=half(XA[:, ri, :]), in_=psv(2 * ri), func=AF.Copy),
                             writes=[("ps", 2 * ri), ("ps", 2 * ri + 1), ("XA", ri)])

                def scan(s):
                    ar, ai, an = PWR[:, s, 0:1], PWI[:, s, 0:1], PWN[:, s, 0:1]
                    sl = s % 2
                    c0 = slice(0, 1)
                    sm = slice(NP, N)
                    pc = slice(0, NP)
                    CT, ST = TB[:, sl, 0, :], TB[:, sl, 1, :]
                    tbk = ("TB", sl)

                    def stt(out, in0, scalar, in1, r=(), w=()):
                        P.op(DVE, lambda e: e.scalar_tensor_tensor(out=out, in0=in0, scalar=scalar, in1=in1, op0=ALU.mult, op1=ALU.add),
                             reads=["PW"] + list(r), writes=list(w))

                    def tt(out, in0, in1, op, r=(), w=()):
                        P.op(DVE, lambda e: e.tensor_tensor(out=out, in0=in0, in1=in1, op=op), reads=list(r), writes=list(w))
                    stt(HSM[:, 0, :], H0[:, s, 0, :], ar, XA[:, 0, sm], ["H0", ("XA", 0)], ["HSM0"])
                    stt(HSM[:, 0, :], H0[:, s, 1, :], an, HSM[:, 0, :], ["H0"], ["HSM0"])
                    stt(HSM[:, 1, :], H0[:, s, 1, :], ar, XA[:, 1, sm], ["H0", ("XA", 1)], ["HSM1"])
                    stt(HSM[:, 1, :], H0[:, s, 0, :], ai, HSM[:, 1, :], ["H0"], ["HSM1"])
                    stt(XA[:, 0, c0], CARRY[:, s, 0:1], ar, XA[:, 0, c0], ["CARRY"], [("XA", 0)])
                    stt(XA[:, 0, c0], CARRY[:, s, 1:2], an, XA[:, 0, c0], ["CARRY"], [("XA", 0)])
                    stt(XA[:, 1, c0], CARRY[:, s, 1:2], ar, XA[:, 1, c0], ["CARRY"], [("XA", 1)])
                    stt(XA[:, 1, c0], CARRY[:, s, 0:1], ai, XA[:, 1, c0], ["CARRY"], [("XA", 1)])
                    tt(T3B[:, 0, :], XA[:, 0, pc], CT, ALU.mult, [("XA", 0), tbk], ["T0"])
                    tt(T3B[:, 1, :], XA[:, 1, pc], ST, ALU.mult, [("XA", 1), tbk], ["T1"])
                    tt(T3B[:, 0, :], T3B[:, 0, :], T3B[:, 1, :], ALU.add, ["T1"], ["T0"])
                    tt(T3B[:, 1, :], XA[:, 1, pc], CT, ALU.mult, [("XA", 1), tbk], ["T1"])
                    tt(T3B[:, 2, :], XA[:, 0, pc], ST, ALU.mult, [("XA", 0), tbk], ["T2"])
                    tt(T3B[:, 1, :], T3B[:, 1, :], T3B[:, 2, :], ALU.subtract, ["T2"], ["T1"])
                    rb = RMAG[:, s, :].to_broadcast([128, NP])
                    P.op(DVE, lambda e: e.tensor_tensor_scan(out=XA[:, 0, pc], data0=rb, data1=T3B[:, 0, :], initial=0.0, op0=ALU.mult, op1=ALU.add),
                         reads=["T0", "RMAG"], writes=[("XA", 0)])
                    P.op(DVE, lambda e: e.tensor_tensor_scan(out=XA[:, 1, pc], data0=rb, data1=T3B[:, 1, :], initial=0.0, op0=ALU.mult, op1=ALU.add),
                         reads=["T1", "RMAG"], writes=[("XA", 1)])
                    tt(T3B[:, 0, :], XA[:, 0, pc], CT, ALU.mult, [("XA", 0), tbk], ["T0"])
                    tt(T3B[:, 2, :], XA[:, 1, pc], ST, ALU.mult, [("XA", 1), tbk], ["T2"])
                    tt(T3B[:, 0, :], T3B[:, 0, :], T3B[:, 2, :], ALU.subtract, ["T2"], ["T0"])
                    tt(T3B[:, 1, :], XA[:, 0, pc], ST, ALU.mult, [("XA", 0), tbk], ["T1"])
                    tt(T3B[:, 2, :], XA[:, 1, pc], CT, ALU.mult, [("XA", 1), tbk], ["T2"])
                    tt(T3B[:, 1, :], T3B[:, 1, :], T3B[:, 2, :], ALU.add, ["T2"], ["T1"])
                    return 0

                def cast(s, fb):
                    hs = s % 2
                    for ri in range(2):
                        tk = "T%d" % ri
                        hk = "HSM%d" % ri
                        P.op(ACT, lambda e, hs=hs, ri=ri: e.activation(out=H16S[:, hs, ri, 0:NP], in_=T3B[:, ri, :], func=AF.Copy),
                             reads=[tk], acc=[("H16S", hs)])
                        P.op(ACT, lambda e, hs=hs, ri=ri: e.activation(out=H16S[:, hs, ri, NP:N], in_=HSM[:, ri, :], func=AF.Copy),
                             reads=[hk], acc=[("H16S", hs)])
                        P.op(ACT, lambda e, s=s, ri=ri: e.activation(out=CARRY[:, s, ri:ri + 1], in_=T3B[:, ri, NP - 1:NP], func=AF.Copy),
                             reads=[tk], writes=["CARRY"])
                        P.op(ACT, lambda e, s=s, ri=ri: e.activation(out=HS[:, s, ri, :], in_=HSM[:, ri, :], func=AF.Copy),
                             reads=[hk], writes=["HS"])

                def cproj(s):
                    c, j = s // 4, s % 4
                    sl, hs = s % 2, s % 2
                    by = 4 + 2 * (c % 2)
                    for ri in range(2):
                        for h in range(2):
                            P.op(PE, lambda e, sl=sl, hs=hs, ri=ri, h=h, by=by, j=j: e.matmul(
                                PS[:, by + h, 0:HN], lhsT=LMS[:, sl, 2 + ri, :], rhs=H16S[:, hs, ri, h * HN:(h + 1) * HN],
                                start=(j == 0 and ri == 0), stop=(j == 3 and ri == 1)),
                                reads=[("LMS", sl), ("H16S", hs)], acc=[("ps", by + h)])
                    if j == 3:
                        P.op(DVE, lambda e, c=c, by=by: e.scalar_tensor_tensor(out=half(YT), in0=half(U16[:, c, :]), scalar=DSK[:, c, :], in1=psv(by),
                                                                             op0=ALU.mult, op1=ALU.add),
                             reads=[("U", c), "DSK"], writes=[("ps", by), ("ps", by + 1), "YT"])
                        P.op(ACT, lambda e: e.activation(out=SIGT, in_=YT, func=AF.Square), reads=["YT"], writes=["SIGT"])
                        P.op(DVE, lambda e: e.tensor_scalar(out=SIGT, in0=SIGT, scalar1=0.044715, scalar2=1.0, op0=ALU.mult, op1=ALU.add), writes=["SIGT"])
                        P.op(DVE, lambda e: e.tensor_tensor(out=SIGT, in0=SIGT, in1=YT, op=ALU.mult), reads=["YT"], writes=["SIGT"])
                        P.op(ACT, lambda e: e.activation(out=SIGT, in_=SIGT, func=AF.Sigmoid, scale=2.0 * math.sqrt(2.0 / math.pi)), writes=["SIGT"])
                        P.op(DVE, lambda e, c=c: e.tensor_tensor(out=SG16[:, c, :], in0=YT, in1=SIGT, op=ALU.mult), reads=["YT", "SIGT"], writes=[("SG", c)])

                bproj(0)
                evac(0)
                for s in range(64):
                    if s + 1 < 64:
                        bproj(s + 1)
                    fb = scan(s)
                    cast(s, fb)
                    if s + 1 < 64:
                        evac(s + 1)
                    cproj(s)
                for n in range(NS):
                    row = t * NS + n
                    elem_dma(ssr_s[row, :].rearrange("(s p) -> p s", p=128), HS[:, :, 0, n], "hso", reads=["HS"])
                    elem_dma(ssi_s[row, :].rearrange("(s p) -> p s", p=128), HS[:, :, 1, n], "hso", reads=["HS"])
                if t == NT - 1:
                    elem_dma(ssr_p.rearrange("(s p) -> p s", p=128), CARRY[:, :, 0], "hpo", reads=["CARRY"])
                    elem_dma(ssi_p.rearrange("(s p) -> p s", p=128), CARRY[:, :, 1], "hpo", reads=["CARRY"])
                def ep_glu(ci, b):
                    P.op(ACT, lambda e, b=b: e.activation(out=half(SIGT), in_=psv(b), func=AF.Sigmoid), writes=[("ps", b), ("ps", b + 1), "SIGT"])
                    P.op(DVE, lambda e, ci=ci: e.tensor_tensor(out=SGG16[:, ci, :], in0=SG16[:, ci, :], in1=SIGT, op=ALU.mult),
                         reads=[("SG", ci), "SIGT"], writes=[("SGG", ci)])
                linear_fm(w_glu, 0, SWD, 16, lambda k: (SG16[:, k, :], ("SG", k)), ep_glu)
                P.barrier()
                for q in range(16):
                    co = wnext(w_conv_out, 0, 16, q * 256, 256)
                    ga0 = wnext(w_in, 0, 16, 3 * CWD + q * 256, 256)
                    ga1 = wnext(w_in, 2048, 16, 3 * CWD + q * 256, 256)
                    for m in range(2):
                        b1 = next_acc()
                        mm_group([co], 16, m, lambda k: (CB16[:, k, :], ("CB", k)), b1)
                        b2 = next_acc()
                        mm_group([ga0, ga1], 32, m, xn_rhs, b2)
                        P.op(ACT, lambda e, b2=b2: e.activation(out=half(SIGT), in_=psv(b2), func=AF.Sigmoid), writes=[("ps", b2), ("ps", b2 + 1), "SIGT"])
                        P.op(DVE, lambda e, m=m, b1=b1: e.tensor_tensor(out=half(MA[:, m, :]), in0=psv(b1), in1=half(SIGT), op=ALU.mult),
                             reads=["SIGT"], writes=[("ps", b1), ("ps", b1 + 1), ("MA", m)])
                    wrelease(); wrelease(); wrelease()
                    so = wnext(w_ssm_out, 0, 16, q * 256, 256)
                    gb0 = wnext(w_in, 0, 16, 3 * CWD + D + q * 256, 256)
                    gb1 = wnext(w_in, 2048, 16, 3 * CWD + D + q * 256, 256)
                    for m in range(2):
                        b1 = next_acc()
                        mm_group([so], 16, m, lambda k: (SGG16[:, k, :], ("SGG", k)), b1)
                        b2 = next_acc()
                        mm_group([gb0, gb1], 32, m, xn_rhs, b2)
                        P.op(ACT, lambda e, b2=b2: e.activation(out=half(SIGT), in_=psv(b2), func=AF.Sigmoid), writes=[("ps", b2), ("ps", b2 + 1), "SIGT"])
                        P.op(DVE, lambda e, b1=b1: e.tensor_tensor(out=half(T2), in0=psv(b1), in1=half(SIGT), op=ALU.mult),
                             reads=["SIGT"], writes=[("ps", b1), ("ps", b1 + 1), "T2"])
                        P.op(DVE, lambda e, m=m, q=q: e.tensor_tensor(out=MG16[:, 2 * q + m, :], in0=MA[:, m, :], in1=T2, op=ALU.add),
                             reads=[("MA", m), "T2"], writes=[("MG", 2 * q + m)])
                    wrelease(); wrelease(); wrelease()
                xi = {"i": 0}

                def ep_wo(q, tb, bk, t=t):
                    src, rows, c0 = xrows(t, tb)
                    sl = xi["i"] % 3
                    xi["i"] += 1
                    P.dma(SP, lambda e: e.dma_start(out=XSL[:rows, sl, :], in_=src[:, q * 256:(q + 1) * 256]), ("xsl", sl), writes=[("XSL", sl)])
                    P.op(DVE, lambda e: e.tensor_tensor(out=XMS[:rows, sl, :], in0=PS[:rows, bk, 0:256], in1=XSL[:rows, sl, :], op=ALU.add),
                         reads=[("XSL", sl)], writes=[("ps", bk), ("XMS", sl)])
                    P.dma(SP, lambda e: e.dma_start(out=XMID[t, c0:c0 + rows, q * 256:(q + 1) * 256], in_=XMS[:rows, sl, :]), ("xms", sl),
                          reads=[("XMS", sl)], writes=[("xmid", tb, q)])
                linear_tm(w_o, 0, 32, lambda k: (MG16[:, k, :], ("MG", k)), ep_wo)
                P.barrier()
                def xmid_rows(tb, t=t):
                    rows, c0 = tb_info(tb)
                    return XMID[t, c0:c0 + rows, :], rows, c0
                norm_T(t, xmid_rows, G2, "G2")
                P.barrier()
                for n in range(NS):
                    row = t * NS + n
                    for k in range(2):
                        elem_dma(FS[:, :, 2 * n + k], sf[row, k, :].rearrange("(c p) -> p c", p=128), "fs", acc=["FS"])
                for hf in range(2):
                    ch0 = hf * NCH
                    for s0 in range(0, NCH, 2):
                        ncol = min(2, NCH - s0) * 128
                        colg = (ch0 + s0) * 128
                        g0 = wnext(w_up, 0, 16, colg, ncol)
                        g1 = wnext(w_up, 2048, 16, colg, ncol)
                        for m in range(ncol // 128):
                            i = ch0 + s0 + m
                            b = next_acc()
                            mm_group([g0, g1], 32, m, xn_rhs, b)
                            P.op(ACT, lambda e, b=b: e.activation(out=half(G32[:, 2:2 + N]), in_=psv(b), func=AF.Copy), writes=[("ps", b), ("ps", b + 1), "G32"])
                            P.op(DVE, lambda e, i=i: e.tensor_copy(out=G32[:, 0:2], in_=FH[:, i, :]), reads=["FH"], writes=["G32"])
                            P.op(ACT, lambda e, i=i: e.activation(out=GCT, in_=G32[:, 2:2 + N], func=AF.Identity, scale=FCW[:, i, 2:3], bias=FCB[:, i, :]),
                                 reads=["G32", "FCW", "FCB"], writes=["GCT"])
                            for k in range(2):
                                P.op(DVE, lambda e, i=i, k=k: e.scalar_tensor_tensor(out=GCT[:, 0:NP], in0=G32[:, k:k + NP], scalar=FCW[:, i, k:k + 1],
                                                                                      in1=GCT[:, 0:NP], op0=ALU.mult, op1=ALU.add),
                                     reads=["G32", "FCW"], writes=["GCT"])
                                P.op(DVE, lambda e, i=i, k=k: e.scalar_tensor_tensor(out=GCT[:, NP:N], in0=FS[:, i, k:16:2], scalar=FCW[:, i, k:k + 1],
                                                                                      in1=GCT[:, NP:N], op0=ALU.mult, op1=ALU.add),
                                     reads=["FS", "FCW"], writes=["GCT"])
                            P.op(DVE, lambda e, i=i: e.tensor_copy(out=FH[:, i, :], in_=G32[:, NP:NP + 2]), reads=["G32"], writes=["FH"])
                            P.op(DVE, lambda e, i=i: e.tensor_copy(out=GS[:, i, :], in_=G32[:, 2 + NP:2 + N]), reads=["G32"], writes=["GS"])
                            P.op(ACT, lambda e, m=m: e.activation(out=SILU[:, m, :], in_=GCT, func=AF.Silu), reads=["GCT"], writes=[("SILU", m)])
                        wrelease(); wrelease()
                        v0 = wnext(w_up, 0, 16, DFF + colg, ncol)
                        v1 = wnext(w_up, 2048, 16, DFF + colg, ncol)
                        for m in range(ncol // 128):
                            il = s0 + m
                            b = next_acc()
                            mm_group([v0, v1], 32, m, xn_rhs, b)
                            P.op(DVE, lambda e, m=m, il=il, b=b: e.tensor_tensor(out=half(H16F[:, il, :]), in0=psv(b), in1=half(SILU[:, m, :]), op=ALU.mult),
                                 reads=[("SILU", m)], writes=[("ps", b), ("ps", b + 1), ("HF", il)])
                        wrelease(); wrelease()

                    def ep_down(q, tb, bk, t=t):
                        rows, c0 = tb_info(tb)
                        sl = xi["i"] % 3
                        xi["i"] += 1
                        P.dma(SP, lambda e: e.dma_start(out=XSL2[:rows, sl, :], in_=XMID[t, c0:c0 + rows, q * 256:(q + 1) * 256]), ("xsl2", sl),
                              reads=[("xmid", tb, q)], writes=[("XSL2", sl)])
                        P.op(DVE, lambda e: e.tensor_tensor(out=XMS2[:rows, sl, :], in0=PS[:rows, bk, 0:256], in1=XSL2[:rows, sl, :], op=ALU.add),
                             reads=[("XSL2", sl)], writes=[("ps", bk), ("XMS2", sl)])
                        P.dma(SP, lambda e: e.dma_start(out=XMID[t, c0:c0 + rows, q * 256:(q + 1) * 256], in_=XMS2[:rows, sl, :]), ("xms2", sl),
                              reads=[("XMS2", sl)], writes=[("xmid", tb, q)])
                    linear_tm(w_down, ch0 * 128, NCH, lambda k: (H16F[:, k, :], ("HF", k)), ep_down)
                P.dma(SP, lambda e, t=t: e.dma_start(out=ffn_s[t * NS:(t + 1) * NS, 0, :], in_=sf[t * NS:(t + 1) * NS, 1, :]), "fso")
                for n in range(NS):
                    elem_dma(ffn_s[t * NS + n, 1, :].rearrange("(c p) -> p c", p=128), GS[:, :, n], "fso", reads=["GS"])
                if t == NT - 1:
                    for k in range(2):
                        elem_dma(ffn_p[k, :].rearrange("(c p) -> p c", p=128), FH[:, :, k], "fpo", reads=["FH"])
                P.barrier()
                P.dma(SP, lambda e: e.dma_start(out=GF, in_=final_norm_g.partition_broadcast(128)), "gf", writes=["GF"])
                for tb in range(5):
                    rows, c0 = tb_info(tb)
                    slot = tb % 2
                    P.dma(SP, lambda e, rows=rows, c0=c0, slot=slot, t=t: e.dma_start(out=XT[:rows, slot, :], in_=XMID[t, c0:c0 + rows, :]), ("xt", slot),
                          reads=[("xmid", tb, q) for q in range(16)], writes=[("XT", slot)])
                    norm_stats(slot, rows)
                    P.op(DVE, lambda e, rows=rows, slot=slot: e.scalar_tensor_tensor(out=XT[:rows, slot, :], in0=XT[:rows, slot, :], scalar=RS[:rows, slot:slot + 1],
                                                                                      in1=GF[:rows, :], op0=ALU.mult, op1=ALU.mult),
                         reads=[("RS", slot), "GF"], writes=[("XT", slot)])
                    P.dma(SP, lambda e, rows=rows, slot=slot, t=t, tb=tb: e.dma_start(out=yrows(t, tb), in_=XT[:rows, slot, :]), ("yo", slot),
                          reads=[("XT", slot)])

        blocks = []
        Pd = Prog(nc, dry=True)
        emit_all(Pd, blocks)
        Pr = Prog(nc, dry=False)
        emit_all(Pr, blocks)
        Pr.emit(st)
    return nc


_NT = 4
_W_KEYS = ["norm_mix_g", "w_in", "conv_w", "conv_b", "ln_g", "ln_b", "w_conv_out", "lam_re", "lam_im", "log_dt",
           "b_re", "b_im", "c_re", "c_im", "d_skip", "w_glu", "w_ssm_out", "w_o", "norm_ffn_g", "w_up",
           "ffn_conv_w", "ffn_conv_b", "w_down"]


def make_in_map(inputs, b, NT, seq0=0):
    f = lambda a: np.ascontiguousarray(np.asarray(a, dtype=np.float32))
    nsa = NT * NS
    m = {}
    m["xp"] = f(inputs["x_prompt"][b, seq0:seq0 + NT * NP])
    m["xs"] = f(inputs["x_sample"][b * nsa:(b + 1) * nsa, 0])
    m["sc"] = f(inputs["state_conv"][0, b * nsa:(b + 1) * nsa])
    m["sr"] = f(inputs["state_ssm_re"][0, b * nsa:(b + 1) * nsa]).reshape(nsa, 8192)
    m["si"] = f(inputs["state_ssm_im"][0, b * nsa:(b + 1) * nsa]).reshape(nsa, 8192)
    m["sf"] = f(inputs["state_ffn_conv"][0, b * nsa:(b + 1) * nsa])
    for k in _W_KEYS:
        m[k] = f(inputs[k][0])
    m["final_norm_g"] = f(inputs["final_norm_g"])
    m["ident"] = np.eye(128, dtype=np.float32)
    return m


def kernel(**inputs):
    NT = _NT
    nc = build_nc(NT)
    maps4 = [make_in_map(inputs, b, NT) for b in range(4)]
    in_maps = [maps4[c % 4] for c in range(8)]
    res = run_bass_kernel_spmd(nc, in_maps, core_ids=list(range(8)))
    r = res.results
    B, S = 4, NT * NP
    y_prompt = np.stack([r[b]["y_p"] for b in range(4)]).astype(np.float32)
    y_sample = np.concatenate([r[b]["y_s"] for b in range(4)])[:, None, :].astype(np.float32)
    conv_prompt = np.stack([r[b]["conv_p"] for b in range(4)])[None].astype(np.float32)
    conv_sample = np.concatenate([r[b]["conv_s"] for b in range(4)])[None].astype(np.float32)
    ssr_p = np.stack([r[b]["ssr_p"].reshape(128, 64) for b in range(4)])[None].astype(np.float32)
    ssi_p = np.stack([r[b]["ssi_p"].reshape(128, 64) for b in range(4)])[None].astype(np.float32)
    ssr_s = np.concatenate([r[b]["ssr_s"].reshape(-1, 128, 64) for b in range(4)])[None].astype(np.float32)
    ssi_s = np.concatenate([r[b]["ssi_s"].reshape(-1, 128, 64) for b in range(4)])[None].astype(np.float32)
    ffn_p = np.stack([r[b]["ffn_p"] for b in range(4)])[None].astype(np.float32)
    ffn_s = np.concatenate([r[b]["ffn_s"] for b in range(4)])[None].astype(np.float32)
    return (y_prompt, y_sample, conv_prompt, conv_sample, ssr_p, ssi_p, ssr_s, ssi_s, ffn_p, ffn_s)
```

```python
import math
import numpy as np
from contextlib import ExitStack
import concourse.bass as bass
import concourse.mybir as mybir
from concourse.bass_utils import run_bass_kernel_spmd

F32 = mybir.dt.float32
BF16 = mybir.dt.bfloat16
AF = mybir.ActivationFunctionType
ALU = mybir.AluOpType
AX = mybir.AxisListType

PE, ACT, DVE, POOL, SP = "pe", "act", "dve", "pool", "sp"

D = 4096
CWD = 2048
SWD = 2048
DFF = 11008
INW = 14336
NP = 512
NS = 8
N = NP + NS
HN = N // 2
PADH = 30
PAD = 256
NSLOT = 6
EPS = 1e-6
NCH = 43


class _Rec:
    __slots__ = ("eng", "fn", "waits", "needs_inc", "dma_sem", "dma_val", "cnt", "seq")

    def __init__(self, eng, fn):
        self.eng = eng
        self.fn = fn
        self.waits = []
        self.needs_inc = False
        self.dma_sem = None
        self.dma_val = 0
        self.cnt = 0
        self.seq = 0


class Prog:
    def __init__(self, nc, dry=False):
        self.nc = nc
        self.dry = dry
        self.ops = {e: [] for e in (PE, ACT, DVE, POOL, SP)}
        self.res = {}
        self.dma_cnt = {}
        self.last_dma = {}
        self.pending = {}

    def _deps(self, rec, reads, writes, acc):
        if self.dry:
            return
        deps = []
        pend = self.pending.get(rec.eng)
        if pend:
            deps.extend(pend)
            self.pending[rec.eng] = None
        for r in reads:
            st = self.res.get(r)
            if st:
                deps.extend(st[0])
        for w in writes:
            st = self.res.get(w)
            if st:
                deps.extend(st[0])
                deps.extend(st[1])
        for w in acc:
            st = self.res.get(w)
            if st:
                deps.extend(st[0])
                deps.extend(st[1])
        best = {}
        for d in deps:
            if d is rec:
                continue
            if d.dma_sem is None and rec.dma_sem is None and d.eng == rec.eng and rec.eng == PE:
                continue
            key = ("d", d.dma_sem) if d.dma_sem is not None else ("e", d.eng)
            val = d.dma_val if d.dma_sem is not None else d.seq
            cur = best.get(key)
            if cur is None or val > cur[0]:
                best[key] = (val, d)
        for _, d in best.values():
            rec.waits.append(d)
            if d.dma_sem is None:
                d.needs_inc = True
        for r in reads:
            self.res.setdefault(r, [[], []])[1].append(rec)
        for w in writes:
            self.res[w] = [[rec], []]
        for w in acc:
            st = self.res.setdefault(w, [[], []])
            st[0].append(rec)
            st[1] = []

    def op(self, eng, fn, reads=(), writes=(), acc=()):
        rec = _Rec(eng, fn)
        rec.seq = len(self.ops[eng]) + 1
        self._deps(rec, reads, writes, acc)
        if not self.dry:
            self.ops[eng].append(rec)
        return rec

    def dma(self, eng, fn, sem, reads=(), writes=(), acc=()):
        rec = _Rec(eng, fn)
        rec.dma_sem = sem
        self.dma_cnt[sem] = self.dma_cnt.get(sem, 0) + 16
        rec.dma_val = self.dma_cnt[sem]
        self._deps(rec, reads, writes, acc)
        if not self.dry:
            self.ops[eng].append(rec)
            self.last_dma[sem] = rec
        return rec

    def barrier(self):
        if self.dry:
            return
        snap = []
        for e, lst in self.ops.items():
            for rec in reversed(lst):
                if rec.dma_sem is None:
                    snap.append(rec)
                    break
        for k, rec in self.last_dma.items():
            if not (isinstance(k, tuple) and k[0] == "w"):
                snap.append(rec)
        for e in (PE, ACT, DVE, SP):
            self.pending[e] = list(snap)

    def emit(self, stack):
        nc = self.nc
        esem = {e: stack.enter_context(nc.semaphore("es_" + e)) for e in (PE, ACT, DVE, POOL)}
        dsem = {}
        for k in self.dma_cnt:
            dsem[k] = stack.enter_context(nc.semaphore("ds_%d" % len(dsem)))
        for e, lst in self.ops.items():
            c = 0
            for rec in lst:
                if rec.dma_sem is None and rec.needs_inc:
                    c += 1
                rec.cnt = c
            assert c < 65000, (e, c)
        for k, v in self.dma_cnt.items():
            assert v < 65000, (k, v)
        block = stack.enter_context(nc.Block())

        def run(e, eng, final=False):
            waited = {}
            for rec in self.ops[e]:
                for d in rec.waits:
                    if d.dma_sem is not None:
                        key, val, sem = ("d", d.dma_sem), d.dma_val, dsem[d.dma_sem]
                    else:
                        key, val, sem = ("e", d.eng), d.cnt, esem[d.eng]
                    if waited.get(key, 0) >= val:
                        continue
                    waited[key] = val
                    eng.wait_ge(sem, val)
                ins = rec.fn(eng)
                if rec.dma_sem is not None:
                    ins.then_inc(dsem[rec.dma_sem], 16)
                elif rec.needs_inc:
                    ins.then_inc(esem[e], 1)
            if final:
                for k, v in self.dma_cnt.items():
                    if waited.get(("d", k), 0) < v:
                        eng.wait_ge(dsem[k], v)

        block.tensor(lambda eng: run(PE, eng))
        block.scalar(lambda eng: run(ACT, eng))
        block.vector(lambda eng: run(DVE, eng))
        block.gpsimd(lambda eng: run(POOL, eng))
        block.sync(lambda eng: run(SP, eng, final=True))


class Geo:
    def __init__(self, np_, ns):
        self.np, self.ns, self.n = np_, ns, np_ + ns
        if self.n > 512:
            h = self.n // 2
            self.parts = [(0, h), (h, self.n - h)]
        else:
            self.parts = [(0, self.n)]
        self.tbs = [(min(128, np_ - r), r) for r in range(0, np_, 128)] + ([(ns, np_)] if ns else [])


def build_nc(NT, PRO=0):
    nc = bass.Bass("TRN2", target_bir_lowering=False)
    NSA = NT * NS

    def din(name, shape):
        return nc.dram_tensor(name, list(shape), F32, kind="ExternalInput").ap()

    def dout(name, shape):
        return nc.dram_tensor(name, list(shape), F32, kind="ExternalOutput").ap()

    xp = din("xp", [NT * NP, D]); xs = din("xs", [NSA, D])
    if PRO:
        xprev = din("xprev", [PRO, D]); hmask = din("hmask", [128, 1])
    sc = din("sc", [NSA, 30, CWD]); sr = din("sr", [NSA, 8192]); si = din("si", [NSA, 8192])
    sf = din("sf", [NSA, 2, DFF])
    norm_mix_g = din("norm_mix_g", [D]); w_in = din("w_in", [D, INW])
    conv_w = din("conv_w", [31, CWD]); conv_b = din("conv_b", [CWD])
    ln_g = din("ln_g", [CWD]); ln_b = din("ln_b", [CWD]); w_conv_out = din("w_conv_out", [CWD, D])
    lam_re = din("lam_re", [128, 64]); lam_im = din("lam_im", [128, 64]); log_dt = din("log_dt", [128])
    b_re = din("b_re", [128, 64, 16]); b_im = din("b_im", [128, 64, 16])
    c_re = din("c_re", [128, 16, 64]); c_im = din("c_im", [128, 16, 64])
    d_skip = din("d_skip", [SWD]); w_glu = din("w_glu", [SWD, SWD]); w_ssm_out = din("w_ssm_out", [SWD, D])
    w_o = din("w_o", [D, D]); norm_ffn_g = din("norm_ffn_g", [D]); w_up = din("w_up", [D, 2 * DFF])
    ffn_conv_w = din("ffn_conv_w", [3, DFF]); ffn_conv_b = din("ffn_conv_b", [DFF])
    w_down = din("w_down", [DFF, D]); final_norm_g = din("final_norm_g", [D])
    ident = din("ident", [128, 128])

    y_p = dout("y_p", [NT * NP, D]); y_s = dout("y_s", [NSA, D])
    conv_p = dout("conv_p", [30, CWD]); conv_s = dout("conv_s", [NSA, 30, CWD])
    ssr_p = dout("ssr_p", [8192]); ssi_p = dout("ssi_p", [8192])
    ssr_s = dout("ssr_s", [NSA, 8192]); ssi_s = dout("ssi_s", [NSA, 8192])
    ffn_p = dout("ffn_p", [2, DFF]); ffn_s = dout("ffn_s", [NSA, 2, DFF])

    XMID = nc.dram_tensor("xmid", [NT + 1, N, D], F32).ap()
    LMD = nc.dram_tensor("lmd", [64, 128, 4, 128], BF16).ap()
    TBL = nc.dram_tensor("tbl", [64, 128, 2, 512], F32).ap()

    st = ExitStack()
    with st:
        def sb(name, shape, dt=F32):
            return st.enter_context(nc.sbuf_tensor(name, list(shape), dt))

        R1 = sb("R1", [128, 32 * N], BF16)
        RA = sb("RA", [128, 34240], BF16)
        R5 = sb("R5", [128, 16 * N], BF16)
        TA = sb("TA", [128, 7680], BF16)
        WR = sb("WR", [128, NSLOT, 16, 256], BF16)
        IDENT = sb("IDENT", [128, 128]); ONES = sb("ONES", [128, 128])
        G1 = sb("G1", [128, 32, 1]); G2 = sb("G2", [128, 32, 1])
        CWT = sb("CWT", [128, 16, 31]); CBS = sb("CBS", [128, 16, 1])
        LNG = sb("LNG", [128, 16, 1]); LNB = sb("LNB", [128, 16, 1])
        FCW = sb("FCW", [128, 86, 3]); FCB = sb("FCB", [128, 86, 1]); DSK = sb("DSK", [128, 16, 1])
        PWR = sb("PWR", [128, 64, 9]); PWI = sb("PWI", [128, 64, 9]); PWN = sb("PWN", [128, 64, 9])
        HALO = sb("HALO", [128, 16, PADH]); FH = sb("FH", [128, 86, 2]); CARRY = sb("CARRY", [128, 64, 2])
        SS = sb("SS", [128, 2]); RS = sb("RS", [128, 2])
        RMAG = sb("RMAG", [128, 64, 1])
        HMASK = sb("HMASK", [128, 1])
        TB = sb("TB", [128, 2, 2, 512])
        PS = st.enter_context(nc.psum_tensor("PS", [128, 8, 512], F32))

        def f32v(t, b0, nbytes):
            return t[:, b0 // 2:(b0 + nbytes) // 2].bitcast(F32)

        def b16v(t, b0, nbytes):
            return t[:, b0 // 2:(b0 + nbytes) // 2]

        XN = R1[:, :].rearrange("p (c n) -> p c n", c=32)
        GF = f32v(R1, 0, 4 * D)
        R2o, R3o = 0, 35200
        GLU = f32v(RA, R2o, 16 * (PADH + N) * 4).rearrange("p (c n) -> p c n", c=16)
        U16 = b16v(RA, R2o, 16 * N * 2).rearrange("p (c n) -> p c n", c=16)
        SG16 = b16v(RA, R2o + 16 * N * 2, 16 * N * 2).rearrange("p (c n) -> p c n", c=16)
        XT = f32v(RA, R2o, 2 * D * 4).rearrange("p (s n) -> p s n", s=2)
        MG16 = b16v(RA, R2o, 32 * N * 2).rearrange("p (c n) -> p c n", c=32)
        CV = f32v(RA, R3o, 16 * N * 4).rearrange("p (c n) -> p c n", c=16)
        SGG16 = b16v(RA, R3o, 16 * N * 2).rearrange("p (c n) -> p c n", c=16)
        XA = f32v(RA, R3o + 16640, 2 * N * 4).rearrange("p (c n) -> p c n", c=2)
        T3B = f32v(RA, R3o + 16640 + 4160, 3 * NP * 4).rearrange("p (c n) -> p c n", c=3)
        HSM = f32v(RA, R3o + 16640 + 4160 + 6144, 2 * NS * 4).rearrange("p (c n) -> p c n", c=2)
        H16S = b16v(RA, R3o + 16640 + 12416, 4 * N * 2).rearrange("p (s r n) -> p s r n", s=2, r=2)
        SQJ = b16v(RA, R3o, D * 2)
        H16F = b16v(RA, 0, NCH * N * 2).rearrange("p (c n) -> p c n", c=NCH)
        FSo = NCH * N * 2
        FS = f32v(RA, FSo, 86 * 16 * 4).rearrange("p (c r) -> p c r", c=86)
        GS = f32v(RA, FSo + 86 * 16 * 4, 86 * 8 * 4).rearrange("p (c r) -> p c r", c=86)
        CB16 = R5[:, :].rearrange("p (c n) -> p c n", c=16)
        SIGT = f32v(TA, 0, 2080)
        SQ = f32v(TA, 2080, 2080); MU = f32v(TA, 4160, 2080); RSTD = f32v(TA, 6240, 2080)
        CSTG = f32v(TA, 8320, 2048).rearrange("p (s b n) -> p s b n", s=2, b=2)
        TMPS = f32v(TA, 10368, 960).rearrange("p (n k) -> p n k", n=8)
        CSS = f32v(TA, 11328, 32)
        YT = f32v(TA, 2080, 2080)
        LMS = b16v(TA, 4160, 2048).rearrange("p (s m n) -> p s m n", s=2, m=4)
        H0 = f32v(TA, 6208, 4096).rearrange("p (s r n) -> p s r n", s=64, r=2)
        HS = f32v(TA, 10304, 4096).rearrange("p (s r n) -> p s r n", s=64, r=2)
        GT1 = f32v(TA, 14400, 480)
        MA = f32v(TA, 2080, 4160).rearrange("p (m n) -> p m n", m=2)
        T2 = f32v(TA, 6240, 2080)
        XSL = f32v(TA, 8320, 3072).rearrange("p (s n) -> p s n", s=3)
        XMS = f32v(TA, 11392, 3072).rearrange("p (s n) -> p s n", s=3)
        G32 = f32v(TA, 0, 2088)
        GCT = f32v(TA, 2088, 2080)
        SILU = f32v(TA, 4168, 4160).rearrange("p (m n) -> p m n", m=2)
        XSL2 = f32v(TA, 8328, 3072).rearrange("p (s n) -> p s n", s=3)
        XMS2 = f32v(TA, 11400, 3072).rearrange("p (s n) -> p s n", s=3)

        CSTG = f32v(TA, 8320, 2048).rearrange("p (s b n) -> p s b n", s=2, b=2)

        def emit_all(P, blocks):
            state = {"wi": 0, "acc": 0, "tm": 0, "issued": 0}

            def wissue(i):
                if i >= len(blocks):
                    return
                W, row0, nk, col0, ncols = blocks[i]
                slot = i % NSLOT
                src = W[row0:row0 + nk * 128, col0:col0 + ncols].rearrange("(k p) n -> p k n", p=128)
                P.dma(POOL, lambda e, slot=slot, src=src, nk=nk, ncols=ncols: e.dma_start(out=WR[:, slot, 0:nk, 0:ncols], in_=src),
                      ("w", slot), writes=[("w", slot)])

            def wnext(W, row0, nk, col0, ncols):
                i = state["wi"]
                state["wi"] += 1
                if P.dry:
                    blocks.append((W, row0, nk, col0, ncols))
                else:
                    assert blocks[i][1:] == (row0, nk, col0, ncols)
                return i % NSLOT

            def wrelease(slot_unused=None):
                if P.dry:
                    return
                wissue(state["issued"])
                state["issued"] += 1

            if not P.dry:
                for i in range(min(NSLOT, len(blocks))):
                    wissue(i)
                state["issued"] = NSLOT

            def next_acc():
                b = state["acc"] * 2
                state["acc"] = (state["acc"] + 1) % 4
                return b

            def pv(g, b):
                if len(g.parts) == 2:
                    return PS[:, b:b + 2, 0:g.parts[0][1]]
                return PS[:, b, 0:g.n]

            def sv(g, ap2d):
                a_ = ap2d[:, 0:g.n]
                if len(g.parts) == 2:
                    return a_.rearrange("p (h n) -> p h n", h=2)
                return a_

            def pk(g, b):
                return [("ps", b + i) for i in range(len(g.parts))]

            def mm_group(g, slots, K, m, rhs_fn, b):
                for k in range(K):
                    slot = slots[k // 16]
                    rap, rkey = rhs_fn(k)
                    for i, (c0, n) in enumerate(g.parts):
                        P.op(PE, lambda e, slot=slot, k=k, m=m, i=i, c0=c0, n=n, rap=rap, b=b, K=K: e.matmul(
                            PS[:, b + i, 0:n], lhsT=WR[:, slot, k % 16, m * 128:(m + 1) * 128],
                            rhs=rap[:, c0:c0 + n], start=(k == 0), stop=(k == K - 1)),
                            reads=[("w", slot), rkey], acc=[("ps", b + i)])

            def linear_fm(g, W, col0, ncols_total, K, rhs_fn, epilogue, row0=0):
                nblk = (K + 15) // 16
                ci = 0
                for s0 in range(col0, col0 + ncols_total, 256):
                    ncols = min(256, col0 + ncols_total - s0)
                    slots = [wnext(W, row0 + kb * 2048, min(16, K - kb * 16), s0, ncols) for kb in range(nblk)]
                    for m in range(ncols // 128):
                        b = next_acc()
                        mm_group(g, slots, K, m, rhs_fn, b)
                        epilogue(ci, b)
                        ci += 1
                    for _ in slots:
                        wrelease()

            def linear_tm(g, W, row0, K, lhs_fn, epilogue):
                nblk = (K + 15) // 16
                ntb = len(g.tbs)
                for q in range(16):
                    banks = []
                    for tb in range(ntb):
                        banks.append(state["tm"] % 8)
                        state["tm"] += 1
                    for kb in range(nblk):
                        nk = min(16, K - kb * 16)
                        slot = wnext(W, row0 + kb * 2048, nk, q * 256, 256)
                        for tb in range(ntb):
                            rows, c0 = g.tbs[tb]
                            for k in range(nk):
                                kk = kb * 16 + k
                                lap, lkey = lhs_fn(kk)
                                P.op(PE, lambda e, slot=slot, k=k, kk=kk, lap=lap, rows=rows, c0=c0, bk=banks[tb], K=K: e.matmul(
                                    PS[:rows, bk, 0:256], lhsT=lap[:, c0:c0 + rows], rhs=WR[:, slot, k, 0:256],
                                    start=(kk == 0), stop=(kk == K - 1)),
                                    reads=[("w", slot), lkey], acc=[("ps", banks[tb])])
                        wrelease()
                    for tb in range(ntb):
                        epilogue(q, tb, banks[tb])

            def elem_dma(out_ap, in_ap, sem, reads=(), writes=(), acc=()):
                P.dma(SP, lambda e, o=out_ap, i=in_ap: e.dma_start(out=o, in_=i, allow_slow_non_contiguous=True), sem, reads=reads, writes=writes, acc=acc)

            P.dma(SP, lambda e: e.dma_start(out=IDENT[:], in_=ident), "c0", writes=["IDENT"])
            if PRO:
                P.dma(SP, lambda e: e.dma_start(out=HMASK[:], in_=hmask), "c0b", writes=["HMASK"])
            P.op(DVE, lambda e: e.memset(ONES[:], 1.0), writes=["ONES"])
            P.op(DVE, lambda e: e.memset(HALO[:], 0.0), writes=["HALO"])
            P.op(DVE, lambda e: e.memset(FH[:], 0.0), writes=["FH"])
            P.op(DVE, lambda e: e.memset(CARRY[:], 0.0), writes=["CARRY"])
            elem_dma(G1[:, :, 0], norm_mix_g.rearrange("(c p) -> p c", p=128), "c1", writes=["G1"])
            elem_dma(G2[:, :, 0], norm_ffn_g.rearrange("(c p) -> p c", p=128), "c2", writes=["G2"])
            for k in range(31):
                elem_dma(CWT[:, :, k], conv_w[k].rearrange("(c p) -> p c", p=128), "c3", writes=[], reads=[])
            P.res["CWT"] = [[P.last_dma.get("c3")] if not P.dry else [], []]
            elem_dma(CBS[:, :, 0], conv_b.rearrange("(c p) -> p c", p=128), "c4", writes=["CBS"])
            elem_dma(LNG[:, :, 0], ln_g.rearrange("(c p) -> p c", p=128), "c5", writes=["LNG"])
            elem_dma(LNB[:, :, 0], ln_b.rearrange("(c p) -> p c", p=128), "c6", writes=["LNB"])
            for k in range(3):
                elem_dma(FCW[:, :, k], ffn_conv_w[k].rearrange("(c p) -> p c", p=128), "c7")
            P.res["FCW"] = [[P.last_dma.get("c7")] if not P.dry else [], []]
            elem_dma(FCB[:, :, 0], ffn_conv_b.rearrange("(c p) -> p c", p=128), "c8", writes=["FCB"])
            elem_dma(DSK[:, :, 0], d_skip.rearrange("(c p) -> p c", p=128), "c9", writes=["DSK"])

            def ra32(b0, shape):
                n = int(np.prod(shape))
                v = f32v(RA, b0, n * 4)
                if len(shape) == 2:
                    return v.rearrange("p (a b) -> p a b", a=shape[0])
                if len(shape) == 3:
                    return v.rearrange("p (a b c) -> p a b c", a=shape[0], b=shape[1])
                return v

            o = 32768
            names = ["LR", "LI", "LDT", "DT", "LRD", "LID", "MAG", "YA", "COSV", "SINV", "ABR", "ABI", "DEN", "NR", "T1", "T2s", "FRE", "FIM"]
            V = {}
            for nm in names:
                V[nm] = ra32(o, [64, 1]); o += 256
            BRE = ra32(o, [64, 16]); o += 4096
            BIM = ra32(o, [64, 16]); o += 4096
            BBR = ra32(o, [64, 16]); o += 4096
            BBI = ra32(o, [64, 16]); o += 4096
            TB1 = ra32(o, [64, 16]); o += 4096
            STG = b16v(RA, o, 1024).rearrange("p (s n) -> p s n", s=4); o += 1024
            STG2 = b16v(RA, o, 1024).rearrange("p (s n) -> p s n", s=4); o += 1024
            ZB = ra32(0, [64, 128])
            ZB4 = f32v(RA, 0, 32768).rearrange("p (c j n) -> p c j n", c=16, j=4)

            lam2 = lambda a: a.rearrange("(s two) p -> two p s", two=2)
            for gl in range(2):
                elem_dma(V["LR"][gl * 64:(gl + 1) * 64, :, 0], lam2(lam_re)[gl], "s0")
                elem_dma(V["LI"][gl * 64:(gl + 1) * 64, :, 0], lam2(lam_im)[gl], "s0")
                P.dma(SP, lambda e, gl=gl: e.dma_start(out=V["LDT"][gl * 64:(gl + 1) * 64, :, 0],
                                                       in_=log_dt.rearrange("(s two) -> two s", two=2)[gl].partition_broadcast(64), allow_slow_non_contiguous=True), "s0")
                P.dma(SP, lambda e, gl=gl: e.dma_start(out=BRE[gl * 64:(gl + 1) * 64], in_=b_re.rearrange("(s two) p h -> two p s h", two=2)[gl]), "s0")
                P.dma(SP, lambda e, gl=gl: e.dma_start(out=BIM[gl * 64:(gl + 1) * 64], in_=b_im.rearrange("(s two) p h -> two p s h", two=2)[gl]), "s0")
            if not P.dry:
                P.res["SPRM"] = [[P.last_dma["s0"]], []]

            def dve(fn, r=(), w=()):
                P.op(DVE, fn, reads=r, writes=w)

            def actop(fn, r=(), w=()):
                P.op(ACT, fn, reads=r, writes=w)

            S_ = "SPRM"
            actop(lambda e: e.activation(out=V["DT"][:], in_=V["LDT"][:], func=AF.Exp), [S_], [S_])
            dve(lambda e: e.tensor_tensor(out=V["LRD"][:], in0=V["LR"][:], in1=V["DT"][:], op=ALU.mult), [S_], [S_])
            dve(lambda e: e.tensor_tensor(out=V["LID"][:], in0=V["LI"][:], in1=V["DT"][:], op=ALU.mult), [S_], [S_])
            actop(lambda e: e.activation(out=V["MAG"][:], in_=V["LRD"][:], func=AF.Exp, scale=1.0 / 32), [S_], [S_])
            actop(lambda e: e.activation(out=V["SINV"][:], in_=V["LID"][:], func=AF.Sin, scale=1.0 / 32), [S_], [S_])
            actop(lambda e: e.activation(out=V["COSV"][:], in_=V["LID"][:], func=AF.Sin, scale=1.0 / 32, bias=0.5 * math.pi), [S_], [S_])
            dve(lambda e: e.tensor_tensor(out=V["ABR"][:], in0=V["MAG"][:], in1=V["COSV"][:], op=ALU.mult), [S_], [S_])
            dve(lambda e: e.tensor_tensor(out=V["ABI"][:], in0=V["MAG"][:], in1=V["SINV"][:], op=ALU.mult), [S_], [S_])
            for _sq in range(5):
                dve(lambda e: e.tensor_tensor(out=V["T1"][:], in0=V["ABR"][:], in1=V["ABR"][:], op=ALU.mult), [S_], [S_])
                dve(lambda e: e.tensor_tensor(out=V["T2s"][:], in0=V["ABI"][:], in1=V["ABI"][:], op=ALU.mult), [S_], [S_])
                dve(lambda e: e.tensor_tensor(out=V["YA"][:], in0=V["ABR"][:], in1=V["ABI"][:], op=ALU.mult), [S_], [S_])
                dve(lambda e: e.tensor_tensor(out=V["ABR"][:], in0=V["T1"][:], in1=V["T2s"][:], op=ALU.subtract), [S_], [S_])
                dve(lambda e: e.tensor_scalar(out=V["ABI"][:], in0=V["YA"][:], scalar1=2.0, scalar2=None, op0=ALU.mult), [S_], [S_])
            dve(lambda e: e.tensor_tensor(out=V["DEN"][:], in0=V["LR"][:], in1=V["LR"][:], op=ALU.mult), [S_], [S_])
            dve(lambda e: e.tensor_tensor(out=V["T1"][:], in0=V["LI"][:], in1=V["LI"][:], op=ALU.mult), [S_], [S_])
            dve(lambda e: e.tensor_tensor(out=V["DEN"][:], in0=V["DEN"][:], in1=V["T1"][:], op=ALU.add), [S_], [S_])
            dve(lambda e: e.reciprocal(out=V["DEN"][:], in_=V["DEN"][:]), [S_], [S_])
            dve(lambda e: e.tensor_scalar(out=V["NR"][:], in0=V["ABR"][:], scalar1=-1.0, scalar2=None, op0=ALU.add), [S_], [S_])
            dve(lambda e: e.tensor_tensor(out=V["T1"][:], in0=V["NR"][:], in1=V["LR"][:], op=ALU.mult), [S_], [S_])
            dve(lambda e: e.tensor_tensor(out=V["T2s"][:], in0=V["ABI"][:], in1=V["LI"][:], op=ALU.mult), [S_], [S_])
            dve(lambda e: e.tensor_tensor(out=V["T1"][:], in0=V["T1"][:], in1=V["T2s"][:], op=ALU.add), [S_], [S_])
            dve(lambda e: e.tensor_tensor(out=V["FRE"][:], in0=V["T1"][:], in1=V["DEN"][:], op=ALU.mult), [S_], [S_])
            dve(lambda e: e.tensor_tensor(out=V["T1"][:], in0=V["ABI"][:], in1=V["LR"][:], op=ALU.mult), [S_], [S_])
            dve(lambda e: e.tensor_tensor(out=V["T2s"][:], in0=V["NR"][:], in1=V["LI"][:], op=ALU.mult), [S_], [S_])
            dve(lambda e: e.tensor_tensor(out=V["T1"][:], in0=V["T1"][:], in1=V["T2s"][:], op=ALU.subtract), [S_], [S_])
            dve(lambda e: e.tensor_tensor(out=V["FIM"][:], in0=V["T1"][:], in1=V["DEN"][:], op=ALU.mult), [S_], [S_])
            dve(lambda e: e.tensor_copy(out=PWR[:, :, 0:1], in_=V["ABR"][:]), [S_], ["PW"])
            dve(lambda e: e.tensor_copy(out=PWI[:, :, 0:1], in_=V["ABI"][:]), [S_], ["PW"])
            for j in range(8):
                dve(lambda e, j=j: e.tensor_tensor(out=V["T1"][:], in0=PWR[:, :, j:j + 1], in1=PWR[:, :, j:j + 1], op=ALU.mult), ["PW", S_], [S_])
                dve(lambda e, j=j: e.tensor_tensor(out=V["T2s"][:], in0=PWI[:, :, j:j + 1], in1=PWI[:, :, j:j + 1], op=ALU.mult), ["PW", S_], [S_])
                dve(lambda e, j=j: e.tensor_tensor(out=PWR[:, :, j + 1:j + 2], in0=V["T1"][:], in1=V["T2s"][:], op=ALU.subtract), [S_], ["PW"])
                dve(lambda e, j=j: e.tensor_tensor(out=V["T1"][:], in0=PWR[:, :, j:j + 1], in1=PWI[:, :, j:j + 1], op=ALU.mult), ["PW", S_], [S_])
                dve(lambda e, j=j: e.tensor_scalar(out=PWI[:, :, j + 1:j + 2], in0=V["T1"][:], scalar1=2.0, scalar2=None, op0=ALU.mult), [S_], ["PW"])
            dve(lambda e: e.tensor_scalar(out=PWN[:], in0=PWI[:], scalar1=-1.0, scalar2=None, op0=ALU.mult), ["PW"], ["PW"])
            bc = lambda a: a.to_broadcast([128, 64, 16])
            dve(lambda e: e.tensor_tensor(out=BBR[:], in0=BRE[:], in1=bc(V["FRE"][:]), op=ALU.mult), [S_], [S_])
            dve(lambda e: e.tensor_tensor(out=TB1[:], in0=BIM[:], in1=bc(V["FIM"][:]), op=ALU.mult), [S_], [S_])
            dve(lambda e: e.tensor_tensor(out=BBR[:], in0=BBR[:], in1=TB1[:], op=ALU.subtract), [S_], [S_])
            dve(lambda e: e.tensor_tensor(out=BBI[:], in0=BIM[:], in1=bc(V["FRE"][:]), op=ALU.mult), [S_], [S_])
            dve(lambda e: e.tensor_tensor(out=TB1[:], in0=BRE[:], in1=bc(V["FIM"][:]), op=ALU.mult), [S_], [S_])
            dve(lambda e: e.tensor_tensor(out=BBI[:], in0=BBI[:], in1=TB1[:], op=ALU.add), [S_], [S_])

            def transpose_store(mi, neg):
                for s4 in range(16):
                    bk = s4 % 2
                    for j in range(4):
                        s = s4 * 4 + j
                        P.op(PE, lambda e, s=s, j=j, bk=bk: e.transpose(out=PS[:, bk, j * 128:(j + 1) * 128], in_=ZB[:, s, :], identity=IDENT[:]),
                             reads=["ZB", "IDENT"], acc=[("ps", bk)])
                    stg = STG if s4 % 2 == 0 else STG2
                    skey = "STG%d" % (s4 % 2)
                    P.op(ACT, lambda e, bk=bk, stg=stg, neg=neg: e.activation(out=stg[:], in_=PS[:, bk, :].rearrange("p (j n) -> p j n", j=4),
                                                                             func=AF.Copy, scale=(-1.0 if neg else 1.0)),
                         writes=[("ps", bk), skey])
                    P.dma(SP, lambda e, s4=s4, stg=stg, mi=mi: e.dma_start(out=LMD[s4 * 4:(s4 + 1) * 4, :, mi, :].rearrange("s p n -> p s n"), in_=stg[:]),
                          "lm%d" % (s4 % 2), reads=[skey], writes=[("LMD", mi, s4)])

            dve(lambda e: e.memset(ZB[:], 0.0), [], ["ZB"])
            for mi, BB in ((0, BBR), (1, BBI)):
                BB4 = BB[:].rearrange("p (c j) h -> p c j h", j=4)
                for gl in range(2):
                    for j in range(4):
                        c0 = 32 * j + 16 * gl
                        dve(lambda e, gl=gl, j=j, c0=c0, BB4=BB4: e.tensor_copy(out=ZB4[gl * 64:(gl + 1) * 64, :, j, c0:c0 + 16],
                                                                               in_=BB4[gl * 64:(gl + 1) * 64, :, j, :]), [S_], ["ZB"])
                transpose_store(mi, False)
            dve(lambda e: e.memset(ZB[:], 0.0), [], ["ZB"])
            for mi, CC in ((2, c_re), (3, c_im)):
                CCv = CC.rearrange("(c r) h p -> r h c p", r=8)
                first = True
                for j in range(4):
                    for gl in range(2):
                        p0 = 32 * j + 16 * gl
                        P.dma(SP, lambda e, j=j, gl=gl, p0=p0, CCv=CCv: e.dma_start(out=ZB4[p0:p0 + 16, :, j, 64 * gl:64 * gl + 64], in_=CCv[2 * j + gl]),
                              "zc", reads=[], acc=["ZB"])
                transpose_store(mi, mi == 3)
            P.barrier()
            actop(lambda e: e.activation(out=RMAG[:], in_=V["LRD"][:], func=AF.Exp), [S_], ["RMAG"])
            WPR = ra32(32768 + 18 * 256 + 5 * 4096 + 2048, [64, 9])
            WPI = ra32(32768 + 18 * 256 + 5 * 4096 + 2048 + 2304, [64, 9])
            TGo = 32768 + 18 * 256
            TG = f32v(RA, TGo, 8 * 256 * 4).rearrange("p (s n) -> p s n", s=8)
            assert TGo + 8192 <= 68480
            EG = f32v(RA, 0, 8 * 2 * 512 * 4).rearrange("p (s r n) -> p s r n", s=8, r=2)
            dve(lambda e: e.tensor_copy(out=V["T1"][:], in_=V["COSV"][:]), [S_], [S_])
            dve(lambda e: e.tensor_copy(out=V["T2s"][:], in_=V["SINV"][:]), [S_], [S_])
            for _sq in range(5):
                dve(lambda e: e.tensor_tensor(out=V["NR"][:], in0=V["T1"][:], in1=V["T1"][:], op=ALU.mult), [S_], [S_])
                dve(lambda e: e.tensor_tensor(out=V["DEN"][:], in0=V["T2s"][:], in1=V["T2s"][:], op=ALU.mult), [S_], [S_])
                dve(lambda e: e.tensor_tensor(out=V["YA"][:], in0=V["T1"][:], in1=V["T2s"][:], op=ALU.mult), [S_], [S_])
                dve(lambda e: e.tensor_tensor(out=V["T1"][:], in0=V["NR"][:], in1=V["DEN"][:], op=ALU.subtract), [S_], [S_])
                dve(lambda e: e.tensor_scalar(out=V["T2s"][:], in0=V["YA"][:], scalar1=2.0, scalar2=None, op0=ALU.mult), [S_], [S_])
            dve(lambda e: e.tensor_copy(out=WPR[:, :, 0:1], in_=V["T1"][:]), [S_], ["WP"])
            dve(lambda e: e.tensor_copy(out=WPI[:, :, 0:1], in_=V["T2s"][:]), [S_], ["WP"])
            for j in range(8):
                dve(lambda e, j=j: e.tensor_tensor(out=V["T1"][:], in0=WPR[:, :, j:j + 1], in1=WPR[:, :, j:j + 1], op=ALU.mult), ["WP", S_], [S_])
                dve(lambda e, j=j: e.tensor_tensor(out=V["T2s"][:], in0=WPI[:, :, j:j + 1], in1=WPI[:, :, j:j + 1], op=ALU.mult), ["WP", S_], [S_])
                dve(lambda e, j=j: e.tensor_tensor(out=WPR[:, :, j + 1:j + 2], in0=V["T1"][:], in1=V["T2s"][:], op=ALU.subtract), [S_], ["WP"])
                dve(lambda e, j=j: e.tensor_tensor(out=V["T1"][:], in0=WPR[:, :, j:j + 1], in1=WPI[:, :, j:j + 1], op=ALU.mult), ["WP", S_], [S_])
                dve(lambda e, j=j: e.tensor_scalar(out=WPI[:, :, j + 1:j + 2], in0=V["T1"][:], scalar1=2.0, scalar2=None, op0=ALU.mult), [S_], ["WP"])
            for g8 in range(8):
                s0 = g8 * 8
                dve(lambda e: e.memset(EG[:, :, 0, 0:1], 1.0), [], ["EG"])
                dve(lambda e: e.memset(EG[:, :, 1, 0:1], 0.0), [], ["EG"])
                for j in range(9):
                    n = 1 << j
                    pr = WPR[:, s0:s0 + 8, j:j + 1].to_broadcast([128, 8, n])
                    pi_ = WPI[:, s0:s0 + 8, j:j + 1].to_broadcast([128, 8, n])
                    sr_, si_ = EG[:, :, 0, 0:n], EG[:, :, 1, 0:n]
                    dr_, di_ = EG[:, :, 0, n:2 * n], EG[:, :, 1, n:2 * n]
                    tg = TG[:, :, 0:n]
                    dve(lambda e, dr_=dr_, sr_=sr_, pr=pr: e.tensor_tensor(out=dr_, in0=sr_, in1=pr, op=ALU.mult), ["WP"], ["EG"])
                    dve(lambda e, tg=tg, si_=si_, pi_=pi_: e.tensor_tensor(out=tg, in0=si_, in1=pi_, op=ALU.mult), ["WP", "EG"], ["TG"])
                    dve(lambda e, dr_=dr_, tg=tg: e.tensor_tensor(out=dr_, in0=dr_, in1=tg, op=ALU.subtract), ["TG"], ["EG"])
                    dve(lambda e, di_=di_, sr_=sr_, pi_=pi_: e.tensor_tensor(out=di_, in0=sr_, in1=pi_, op=ALU.mult), ["WP"], ["EG"])
                    dve(lambda e, tg=tg, si_=si_, pr=pr: e.tensor_tensor(out=tg, in0=si_, in1=pr, op=ALU.mult), ["WP", "EG"], ["TG"])
                    dve(lambda e, di_=di_, tg=tg: e.tensor_tensor(out=di_, in0=di_, in1=tg, op=ALU.add), ["TG"], ["EG"])
                P.dma(SP, lambda e, s0=s0: e.dma_start(out=TBL[s0:s0 + 8].rearrange("s p r t -> p s (r t)"),
                                                        in_=EG.rearrange("p s r t -> p s (r t)")), "tblw", reads=["EG"], writes=[("TBL", g8)])
            P.barrier()

            def norm_stats(slot, rows):
                P.op(ACT, lambda e: e.activation(out=SQJ[:rows, :], in_=XT[:rows, slot, :], func=AF.Square, accum_out=SS[:rows, slot:slot + 1]),
                     reads=[("XT", slot)], writes=["SQJ", ("SS", slot)])
                P.op(ACT, lambda e: e.activation(out=RS[:rows, slot:slot + 1], in_=SS[:rows, slot:slot + 1], func=AF.Sqrt, scale=1.0 / D, bias=EPS),
                     reads=[("SS", slot)], writes=[("RS", slot)])
                P.op(DVE, lambda e: e.reciprocal(out=RS[:rows, slot:slot + 1], in_=RS[:rows, slot:slot + 1]), writes=[("RS", slot)])

            def norm_T(g, src_fn, G, gkey):
                for tb, (rows, c0) in enumerate(g.tbs):
                    src = src_fn(tb)
                    slot = tb % 2
                    P.dma(SP, lambda e, src=src, rows=rows, slot=slot: e.dma_start(out=XT[:rows, slot, :], in_=src), ("xt", slot),
                          writes=[("XT", slot)])
                    norm_stats(slot, rows)
                    P.op(ACT, lambda e, rows=rows, slot=slot: e.activation(out=XT[:rows, slot, :], in_=XT[:rows, slot, :], func=AF.Identity,
                                                                            scale=RS[:rows, slot:slot + 1]),
                         reads=[("RS", slot)], writes=[("XT", slot)])
                    for c4 in range(8):
                        bk = c4 % 2
                        for j in range(4):
                            c = c4 * 4 + j
                            P.op(PE, lambda e, rows=rows, slot=slot, c=c, j=j, bk=bk: e.transpose(
                                out=PS[:, bk, j * 128:j * 128 + rows], in_=XT[:rows, slot, c * 128:(c + 1) * 128], identity=IDENT[:rows, :rows]),
                                reads=[("XT", slot), "IDENT"], acc=[("ps", bk)])
                        P.op(DVE, lambda e, rows=rows, c4=c4, bk=bk, c0=c0, G=G: e.tensor_tensor(
                            out=XN[:, c4 * 4:(c4 + 1) * 4, c0:c0 + rows],
                            in0=PS[:, bk, :].rearrange("p (j n) -> p j n", j=4)[:, :, 0:rows],
                            in1=G[:, c4 * 4:(c4 + 1) * 4, :].to_broadcast([128, 4, rows]), op=ALU.mult),
                            reads=[gkey], writes=[("ps", bk)], acc=[("XN", c4 * 4 + j) for j in range(4)])

            def xn_rhs(k):
                return XN[:, k, :], ("XN", k)

            xi = {"i": 0}

            def run_tile(g, mode, xsrc_fn, xm, samp0, outp):
                full = mode == "full"
                npc, nsc, n = g.np, g.ns, g.n
                P.barrier()
                norm_T(g, xsrc_fn, G1, "G1")
                P.barrier()
                if mode != "ssm":
                    def ep_ain(ci, b):
                        P.op(ACT, lambda e, ci=ci, b=b: e.activation(out=sv(g, GLU[:, ci, PADH:PADH + n]), in_=pv(g, b), func=AF.Copy),
                             writes=pk(g, b) + [("GLU", ci)])
                    linear_fm(g, w_in, 0, CWD, 32, xn_rhs, ep_ain)

                    def ep_agate(ci, b):
                        P.op(ACT, lambda e, b=b: e.activation(out=sv(g, SIGT), in_=pv(g, b), func=AF.Sigmoid), writes=pk(g, b) + ["SIGT"])
                        P.op(DVE, lambda e, ci=ci: e.tensor_tensor(out=GLU[:, ci, PADH:PADH + n], in0=GLU[:, ci, PADH:PADH + n], in1=SIGT[:, 0:n], op=ALU.mult),
                             reads=["SIGT"], writes=[("GLU", ci)])
                    linear_fm(g, w_in, CWD, CWD, 32, xn_rhs, ep_agate)
                    P.op(DVE, lambda e: e.tensor_copy(out=GLU[:, :, 0:PADH], in_=HALO[:]), reads=["HALO"], acc=[("GLU", c) for c in range(16)])
                    if nsc:
                        scv = sc[samp0:samp0 + NS].rearrange("n k f -> (n k) f")
                    for c in range(16):
                        cs = c % 2
                        P.op(ACT, lambda e, c=c: e.activation(out=CV[:, c, 0:npc], in_=GLU[:, c, PADH:PADH + npc], func=AF.Identity,
                                                               scale=CWT[:, c, 30:31], bias=CBS[:, c, :]),
                             reads=[("GLU", c), "CWT", "CBS"], writes=[("CV", c)])
                        for k in range(30):
                            P.op(DVE, lambda e, c=c, k=k: e.scalar_tensor_tensor(out=CV[:, c, 0:npc], in0=GLU[:, c, k:k + npc], scalar=CWT[:, c, k:k + 1],
                                                                                  in1=CV[:, c, 0:npc], op0=ALU.mult, op1=ALU.add),
                                 reads=[("GLU", c), "CWT"], writes=[("CV", c)])
                        if nsc:
                            P.dma(SP, lambda e, c=c, cs=cs, scv=scv: e.dma_start(out=CSTG[:120, cs, :, :],
                                                                                   in_=scv[:, c * 128:(c + 1) * 128].rearrange("(b r) f -> r b f", b=2)),
                                  ("cstg", cs), writes=[("CSTG", cs)])
                            bk = 4 + cs
                            for b2 in range(2):
                                P.op(PE, lambda e, cs=cs, b2=b2, bk=bk: e.transpose(out=PS[:, bk, b2 * 120:(b2 + 1) * 120], in_=CSTG[:120, cs, b2, :],
                                                                                     identity=IDENT[:120, :120]),
                                     reads=[("CSTG", cs), "IDENT"], acc=[("ps", bk)])
                            P.op(DVE, lambda e, c=c, bk=bk: e.tensor_tensor(out=TMPS[:], in0=PS[:, bk, 0:240].rearrange("p (n k) -> p n k", n=8),
                                                                            in1=CWT[:, c:c + 1, 0:30].to_broadcast([128, 8, 30]), op=ALU.mult),
                                 reads=["CWT"], writes=[("ps", bk), "TMPS"])
                            P.op(DVE, lambda e: e.tensor_reduce(out=CSS, in_=TMPS[:], axis=AX.X, op=ALU.add), reads=["TMPS"], writes=["CSS"])
                            P.op(DVE, lambda e, c=c: e.scalar_tensor_tensor(out=CSS, in0=GLU[:, c, PADH + npc:PADH + n], scalar=CWT[:, c, 30:31], in1=CSS,
                                                                             op0=ALU.mult, op1=ALU.add), reads=[("GLU", c), "CWT"], writes=["CSS"])
                            P.op(DVE, lambda e, c=c: e.tensor_scalar(out=CV[:, c, npc:n], in0=CSS, scalar1=CBS[:, c, :], scalar2=None, op0=ALU.add),
                                 reads=["CSS", "CBS"], writes=[("CV", c)])
                        P.op(ACT, lambda e, c=c: e.activation(out=SQ[:, 0:n], in_=CV[:, c, 0:n], func=AF.Square), reads=[("CV", c)], writes=["SQ"])
                        for i, (p0, pn) in enumerate(g.parts):
                            P.op(PE, lambda e, c=c, i=i, p0=p0, pn=pn: e.matmul(PS[:, 0 + i, 0:pn], lhsT=ONES[:], rhs=CV[:, c, p0:p0 + pn], start=(c == 0), stop=(c == 15)),
                                 reads=["ONES", ("CV", c)], acc=[("ps", 0 + i)])
                            P.op(PE, lambda e, c=c, i=i, p0=p0, pn=pn: e.matmul(PS[:, 2 + i, 0:pn], lhsT=ONES[:], rhs=SQ[:, p0:p0 + pn], start=(c == 0), stop=(c == 15)),
                                 reads=["ONES", "SQ"], acc=[("ps", 2 + i)])
                    P.op(DVE, lambda e: e.tensor_copy(out=HALO[:], in_=GLU[:, :, npc:npc + PADH]), reads=[("GLU", c) for c in range(16)], writes=["HALO"])
                    if full and nsc:
                        P.dma(SP, lambda e: e.dma_start(out=conv_s[samp0:samp0 + NS, 0:29, :], in_=sc[samp0:samp0 + NS, 1:30, :]), "cso")
                        for n_ in range(NS):
                            elem_dma(conv_s[samp0 + n_, 29, :].rearrange("(c p) -> p c", p=128), GLU[:, :, PADH + npc + n_], "cso",
                                     reads=[("GLU", c) for c in range(16)])
                    P.op(DVE, lambda e: e.tensor_scalar(out=sv(g, MU), in0=pv(g, 0), scalar1=1.0 / CWD, scalar2=None, op0=ALU.mult),
                         writes=pk(g, 0) + ["MU"])
                    P.op(DVE, lambda e: e.tensor_tensor(out=SQ[:, 0:n], in0=MU[:, 0:n], in1=MU[:, 0:n], op=ALU.mult), reads=["MU"], writes=["SQ"])
                    P.op(DVE, lambda e: e.scalar_tensor_tensor(out=sv(g, RSTD), in0=pv(g, 2), scalar=1.0 / CWD, in1=sv(g, SQ), op0=ALU.mult, op1=ALU.subtract),
                         reads=["SQ"], writes=pk(g, 2) + ["RSTD"])
                    P.op(ACT, lambda e: e.activation(out=RSTD[:, 0:n], in_=RSTD[:, 0:n], func=AF.Sqrt, bias=EPS), writes=["RSTD"])
                    P.op(DVE, lambda e: e.reciprocal(out=RSTD[:, 0:n], in_=RSTD[:, 0:n]), writes=["RSTD"])
                    for c in range(16):
                        P.op(DVE, lambda e, c=c: e.tensor_tensor(out=CV[:, c, 0:n], in0=CV[:, c, 0:n], in1=MU[:, 0:n], op=ALU.subtract), reads=["MU"], writes=[("CV", c)])
                        P.op(DVE, lambda e, c=c: e.tensor_tensor(out=CV[:, c, 0:n], in0=CV[:, c, 0:n], in1=RSTD[:, 0:n], op=ALU.mult), reads=["RSTD"], writes=[("CV", c)])
                        P.op(ACT, lambda e, c=c: e.activation(out=CB16[:, c, 0:n], in_=CV[:, c, 0:n], func=AF.Silu, scale=LNG[:, c, :], bias=LNB[:, c, :]),
                             reads=[("CV", c), "LNG", "LNB"], writes=[("CB", c)])
                    P.barrier()
                def ep_u(ci, b):
                    P.op(ACT, lambda e, ci=ci, b=b: e.activation(out=sv(g, U16[:, ci, :]), in_=pv(g, b), func=AF.Copy),
                         writes=pk(g, b) + [("U", ci)])
                linear_fm(g, w_in, 2 * CWD, SWD, 32, xn_rhs, ep_u)
                if nsc:
                    for n_ in range(NS):
                        row = samp0 + n_
                        elem_dma(H0[:, :, 0, n_], sr[row, :].rearrange("(s p) -> p s", p=128), "h0", acc=["H0"])
                        elem_dma(H0[:, :, 1, n_], si[row, :].rearrange("(s p) -> p s", p=128), "h0", acc=["H0"])

                def bproj(s):
                    c = s // 4
                    sl = s % 2
                    P.dma(SP, lambda e, s=s, sl=sl: e.dma_start(out=LMS[:, sl], in_=LMD[s]), ("lms", sl),
                          reads=[("LMD", mi, s // 4) for mi in range(4)], writes=[("LMS", sl)])
                    P.dma(SP, lambda e, s=s, sl=sl: e.dma_start(out=TB[:, sl], in_=TBL[s]), ("tb", sl),
                          reads=[("TBL", s // 8)], writes=[("TB", sl)])
                    for ri in range(2):
                        for i, (p0, pn) in enumerate(g.parts):
                            P.op(PE, lambda e, sl=sl, c=c, ri=ri, i=i, p0=p0, pn=pn: e.matmul(PS[:, 2 * ri + i, 0:pn], lhsT=LMS[:, sl, ri, :], rhs=U16[:, c, p0:p0 + pn],
                                                                                         start=True, stop=True),
                                 reads=[("LMS", sl), ("U", c)], acc=[("ps", 2 * ri + i)])

                def evac(s):
                    for ri in range(2):
                        P.op(ACT, lambda e, ri=ri: e.activation(out=sv(g, XA[:, ri, :]), in_=pv(g, 2 * ri), func=AF.Copy),
                             writes=pk(g, 2 * ri) + [("XA", ri)])

                def scan(s):
                    ar, ai, an = PWR[:, s, 0:1], PWI[:, s, 0:1], PWN[:, s, 0:1]
                    sl = s % 2
                    c0 = slice(0, 1)
                    sm = slice(npc, n)
                    pc = slice(0, npc)
                    CT, ST = TB[:, sl, 0, 0:npc], TB[:, sl, 1, 0:npc]
                    tbk = ("TB", sl)
                    T0, T1, T2 = T3B[:, 0, 0:npc], T3B[:, 1, 0:npc], T3B[:, 2, 0:npc]

                    def stt(out, in0, scalar, in1, r=(), w=()):
                        P.op(DVE, lambda e: e.scalar_tensor_tensor(out=out, in0=in0, scalar=scalar, in1=in1, op0=ALU.mult, op1=ALU.add),
                             reads=["PW"] + list(r), writes=list(w))

                    def tt(out, in0, in1, op, r=(), w=()):
                        P.op(DVE, lambda e: e.tensor_tensor(out=out, in0=in0, in1=in1, op=op), reads=list(r), writes=list(w))
                    if nsc:
                        stt(HSM[:, 0, :], H0[:, s, 0, :], ar, XA[:, 0, sm], ["H0", ("XA", 0)], ["HSM0"])
                        stt(HSM[:, 0, :], H0[:, s, 1, :], an, HSM[:, 0, :], ["H0"], ["HSM0"])
                        stt(HSM[:, 1, :], H0[:, s, 1, :], ar, XA[:, 1, sm], ["H0", ("XA", 1)], ["HSM1"])
                        stt(HSM[:, 1, :], H0[:, s, 0, :], ai, HSM[:, 1, :], ["H0"], ["HSM1"])
                    stt(XA[:, 0, c0], CARRY[:, s, 0:1], ar, XA[:, 0, c0], ["CARRY"], [("XA", 0)])
                    stt(XA[:, 0, c0], CARRY[:, s, 1:2], an, XA[:, 0, c0], ["CARRY"], [("XA", 0)])
                    stt(XA[:, 1, c0], CARRY[:, s, 1:2], ar, XA[:, 1, c0], ["CARRY"], [("XA", 1)])
                    stt(XA[:, 1, c0], CARRY[:, s, 0:1], ai, XA[:, 1, c0], ["CARRY"], [("XA", 1)])
                    tt(T0, XA[:, 0, pc], CT, ALU.mult, [("XA", 0), tbk], ["T0"])
                    tt(T1, XA[:, 1, pc], ST, ALU.mult, [("XA", 1), tbk], ["T1"])
                    tt(T0, T0, T1, ALU.add, ["T1"], ["T0"])
                    tt(T1, XA[:, 1, pc], CT, ALU.mult, [("XA", 1), tbk], ["T1"])
                    tt(T2, XA[:, 0, pc], ST, ALU.mult, [("XA", 0), tbk], ["T2"])
                    tt(T1, T1, T2, ALU.subtract, ["T2"], ["T1"])
                    rb = RMAG[:, s, :].to_broadcast([128, npc])
                    P.op(DVE, lambda e: e.tensor_tensor_scan(out=XA[:, 0, pc], data0=rb, data1=T0, initial=0.0, op0=ALU.mult, op1=ALU.add),
                         reads=["T0", "RMAG"], writes=[("XA", 0)])
                    P.op(DVE, lambda e: e.tensor_tensor_scan(out=XA[:, 1, pc], data0=rb, data1=T1, initial=0.0, op0=ALU.mult, op1=ALU.add),
                         reads=["T1", "RMAG"], writes=[("XA", 1)])
                    tt(T0, XA[:, 0, pc], CT, ALU.mult, [("XA", 0), tbk], ["T0"])
                    tt(T2, XA[:, 1, pc], ST, ALU.mult, [("XA", 1), tbk], ["T2"])
                    tt(T0, T0, T2, ALU.subtract, ["T2"], ["T0"])
                    tt(T1, XA[:, 0, pc], ST, ALU.mult, [("XA", 0), tbk], ["T1"])
                    tt(T2, XA[:, 1, pc], CT, ALU.mult, [("XA", 1), tbk], ["T2"])
                    tt(T1, T1, T2, ALU.add, ["T2"], ["T1"])

                def cast(s):
                    hs = s % 2
                    for ri in range(2):
                        tk = "T%d" % ri
                        hk = "HSM%d" % ri
                        if mode != "ssm":
                            P.op(ACT, lambda e, hs=hs, ri=ri: e.activation(out=H16S[:, hs, ri, 0:npc], in_=T3B[:, ri, 0:npc], func=AF.Copy),
                                 reads=[tk], acc=[("H16S", hs)])
                        P.op(ACT, lambda e, s=s, ri=ri: e.activation(out=CARRY[:, s, ri:ri + 1], in_=T3B[:, ri, npc - 1:npc], func=AF.Copy),
                             reads=[tk], writes=["CARRY"])
                        if nsc:
                            P.op(ACT, lambda e, hs=hs, ri=ri: e.activation(out=H16S[:, hs, ri, npc:n], in_=HSM[:, ri, :], func=AF.Copy),
                                 reads=[hk], acc=[("H16S", hs)])
                            P.op(ACT, lambda e, s=s, ri=ri: e.activation(out=HS[:, s, ri, :], in_=HSM[:, ri, :], func=AF.Copy),
                                 reads=[hk], writes=["HS"])

                def cproj(s):
                    c, j = s // 4, s % 4
                    sl, hs = s % 2, s % 2
                    by = 4 + 2 * (c % 2)
                    for ri in range(2):
                        for i, (p0, pn) in enumerate(g.parts):
                            P.op(PE, lambda e, sl=sl, hs=hs, ri=ri, i=i, p0=p0, pn=pn, by=by, j=j: e.matmul(
                                PS[:, by + i, 0:pn], lhsT=LMS[:, sl, 2 + ri, :], rhs=H16S[:, hs, ri, p0:p0 + pn],
                                start=(j == 0 and ri == 0), stop=(j == 3 and ri == 1)),
                                reads=[("LMS", sl), ("H16S", hs)], acc=[("ps", by + i)])
                    if j == 3:
                        P.op(DVE, lambda e, c=c, by=by: e.scalar_tensor_tensor(out=sv(g, YT), in0=sv(g, U16[:, c, :]), scalar=DSK[:, c, :], in1=pv(g, by),
                                                                             op0=ALU.mult, op1=ALU.add),
                             reads=[("U", c), "DSK"], writes=pk(g, by) + ["YT"])
                        P.op(ACT, lambda e: e.activation(out=SIGT[:, 0:n], in_=YT[:, 0:n], func=AF.Square), reads=["YT"], writes=["SIGT"])
                        P.op(DVE, lambda e: e.tensor_scalar(out=SIGT[:, 0:n], in0=SIGT[:, 0:n], scalar1=0.044715, scalar2=1.0, op0=ALU.mult, op1=ALU.add), writes=["SIGT"])
                        P.op(DVE, lambda e: e.tensor_tensor(out=SIGT[:, 0:n], in0=SIGT[:, 0:n], in1=YT[:, 0:n], op=ALU.mult), reads=["YT"], writes=["SIGT"])
                        P.op(ACT, lambda e: e.activation(out=SIGT[:, 0:n], in_=SIGT[:, 0:n], func=AF.Sigmoid, scale=2.0 * math.sqrt(2.0 / math.pi)), writes=["SIGT"])
                        P.op(DVE, lambda e, c=c: e.tensor_tensor(out=SG16[:, c, 0:n], in0=YT[:, 0:n], in1=SIGT[:, 0:n], op=ALU.mult), reads=["YT", "SIGT"], writes=[("SG", c)])

                bproj(0)
                evac(0)
                for s in range(64):
                    if s + 1 < 64:
                        bproj(s + 1)
                    scan(s)
                    cast(s)
                    if s + 1 < 64:
                        evac(s + 1)
                    if mode != "ssm":
                        cproj(s)
                if mode == "ssm":
                    return
                if full and nsc:
                    for n_ in range(NS):
                        row = samp0 + n_
                        elem_dma(ssr_s[row, :].rearrange("(s p) -> p s", p=128), HS[:, :, 0, n_], "hso", reads=["HS"])
                        elem_dma(ssi_s[row, :].rearrange("(s p) -> p s", p=128), HS[:, :, 1, n_], "hso", reads=["HS"])
                def ep_glu(ci, b):
                    P.op(ACT, lambda e, b=b: e.activation(out=sv(g, SIGT), in_=pv(g, b), func=AF.Sigmoid), writes=pk(g, b) + ["SIGT"])
                    P.op(DVE, lambda e, ci=ci: e.tensor_tensor(out=SGG16[:, ci, 0:n], in0=SG16[:, ci, 0:n], in1=SIGT[:, 0:n], op=ALU.mult),
                         reads=[("SG", ci), "SIGT"], writes=[("SGG", ci)])
                linear_fm(g, w_glu, 0, SWD, 16, lambda k: (SG16[:, k, :], ("SG", k)), ep_glu)
                P.barrier()
                for q in range(16):
                    co = wnext(w_conv_out, 0, 16, q * 256, 256)
                    ga0 = wnext(w_in, 0, 16, 3 * CWD + q * 256, 256)
                    ga1 = wnext(w_in, 2048, 16, 3 * CWD + q * 256, 256)
                    for m in range(2):
                        b1 = next_acc()
                        mm_group(g, [co], 16, m, lambda k: (CB16[:, k, :], ("CB", k)), b1)
                        b2 = next_acc()
                        mm_group(g, [ga0, ga1], 32, m, xn_rhs, b2)
                        P.op(ACT, lambda e, b2=b2: e.activation(out=sv(g, SIGT), in_=pv(g, b2), func=AF.Sigmoid), writes=pk(g, b2) + ["SIGT"])
                        P.op(DVE, lambda e, m=m, b1=b1: e.tensor_tensor(out=sv(g, MA[:, m, :]), in0=pv(g, b1), in1=sv(g, SIGT), op=ALU.mult),
                             reads=["SIGT"], writes=pk(g, b1) + [("MA", m)])
                    wrelease(); wrelease(); wrelease()
                    so = wnext(w_ssm_out, 0, 16, q * 256, 256)
                    gb0 = wnext(w_in, 0, 16, 3 * CWD + D + q * 256, 256)
                    gb1 = wnext(w_in, 2048, 16, 3 * CWD + D + q * 256, 256)
                    for m in range(2):
                        b1 = next_acc()
                        mm_group(g, [so], 16, m, lambda k: (SGG16[:, k, :], ("SGG", k)), b1)
                        b2 = next_acc()
                        mm_group(g, [gb0, gb1], 32, m, xn_rhs, b2)
                        P.op(ACT, lambda e, b2=b2: e.activation(out=sv(g, SIGT), in_=pv(g, b2), func=AF.Sigmoid), writes=pk(g, b2) + ["SIGT"])
                        P.op(DVE, lambda e, b1=b1: e.tensor_tensor(out=sv(g, T2), in0=pv(g, b1), in1=sv(g, SIGT), op=ALU.mult),
                             reads=["SIGT"], writes=pk(g, b1) + ["T2"])
                        P.op(DVE, lambda e, m=m, q=q: e.tensor_tensor(out=MG16[:, 2 * q + m, 0:n], in0=MA[:, m, 0:n], in1=T2[:, 0:n], op=ALU.add),
                             reads=[("MA", m), "T2"], writes=[("MG", 2 * q + m)])
                    wrelease(); wrelease(); wrelease()
                def ep_wo(q, tb, bk):
                    rows, c0 = g.tbs[tb]
                    src = xsrc_fn(tb)
                    sl = xi["i"] % 3
                    xi["i"] += 1
                    P.dma(SP, lambda e: e.dma_start(out=XSL[:rows, sl, :], in_=src[:, q * 256:(q + 1) * 256]), ("xsl", sl), writes=[("XSL", sl)])
                    P.op(DVE, lambda e: e.tensor_tensor(out=XMS[:rows, sl, :], in0=PS[:rows, bk, 0:256], in1=XSL[:rows, sl, :], op=ALU.add),
                         reads=[("XSL", sl)], writes=[("ps", bk), ("XMS", sl)])
                    P.dma(SP, lambda e: e.dma_start(out=XMID[xm, c0:c0 + rows, q * 256:(q + 1) * 256], in_=XMS[:rows, sl, :]), ("xms", sl),
                          reads=[("XMS", sl)], writes=[("xmid", tb, q)])
                linear_tm(g, w_o, 0, 32, lambda k: (MG16[:, k, :], ("MG", k)), ep_wo)
                P.barrier()
                norm_T(g, lambda tb: XMID[xm, g.tbs[tb][1]:g.tbs[tb][1] + g.tbs[tb][0], :], G2, "G2")
                P.barrier()
                if nsc:
                    for n_ in range(NS):
                        row = samp0 + n_
                        for k in range(2):
                            elem_dma(FS[:, :, 2 * n_ + k], sf[row, k, :].rearrange("(c p) -> p c", p=128), "fs", acc=["FS"])
                for hf in range(2):
                    ch0 = hf * NCH
                    for s0 in range(0, NCH, 2):
                        ncol = min(2, NCH - s0) * 128
                        colg = (ch0 + s0) * 128
                        g0 = wnext(w_up, 0, 16, colg, ncol)
                        g1 = wnext(w_up, 2048, 16, colg, ncol)
                        for m in range(ncol // 128):
                            i = ch0 + s0 + m
                            b = next_acc()
                            mm_group(g, [g0, g1], 32, m, xn_rhs, b)
                            P.op(ACT, lambda e, b=b: e.activation(out=sv(g, G32[:, 2:2 + n]), in_=pv(g, b), func=AF.Copy), writes=pk(g, b) + ["G32"])
                            if full:
                                P.op(DVE, lambda e, i=i: e.tensor_copy(out=G32[:, 0:2], in_=FH[:, i, :]), reads=["FH"], writes=["G32"])
                                P.op(ACT, lambda e, i=i: e.activation(out=GCT[:, 0:n], in_=G32[:, 2:2 + n], func=AF.Identity, scale=FCW[:, i, 2:3], bias=FCB[:, i, :]),
                                     reads=["G32", "FCW", "FCB"], writes=["GCT"])
                                for k in range(2):
                                    P.op(DVE, lambda e, i=i, k=k: e.scalar_tensor_tensor(out=GCT[:, 0:npc], in0=G32[:, k:k + npc], scalar=FCW[:, i, k:k + 1],
                                                                                          in1=GCT[:, 0:npc], op0=ALU.mult, op1=ALU.add),
                                         reads=["G32", "FCW"], writes=["GCT"])
                                    if nsc:
                                        P.op(DVE, lambda e, i=i, k=k: e.scalar_tensor_tensor(out=GCT[:, npc:n], in0=FS[:, i, k:16:2], scalar=FCW[:, i, k:k + 1],
                                                                                              in1=GCT[:, npc:n], op0=ALU.mult, op1=ALU.add),
                                             reads=["FS", "FCW"], writes=["GCT"])
                            P.op(DVE, lambda e, i=i: e.tensor_copy(out=FH[:, i, :], in_=G32[:, npc:npc + 2]), reads=["G32"], writes=["FH"])
                            if full:
                                if nsc:
                                    P.op(DVE, lambda e, i=i: e.tensor_copy(out=GS[:, i, :], in_=G32[:, 2 + npc:2 + n]), reads=["G32"], writes=["GS"])
                                P.op(ACT, lambda e, m=m: e.activation(out=SILU[:, m, 0:n], in_=GCT[:, 0:n], func=AF.Silu), reads=["GCT"], writes=[("SILU", m)])
                        wrelease(); wrelease()
                        if not full:
                            continue
                        v0 = wnext(w_up, 0, 16, DFF + colg, ncol)
                        v1 = wnext(w_up, 2048, 16, DFF + colg, ncol)
                        for m in range(ncol // 128):
                            il = s0 + m
                            b = next_acc()
                            mm_group(g, [v0, v1], 32, m, xn_rhs, b)
                            P.op(DVE, lambda e, m=m, il=il, b=b: e.tensor_tensor(out=sv(g, H16F[:, il, :]), in0=pv(g, b), in1=sv(g, SILU[:, m, :]), op=ALU.mult),
                                 reads=[("SILU", m)], writes=pk(g, b) + [("HF", il)])
                        wrelease(); wrelease()
                    if not full:
                        continue

                    def ep_down(q, tb, bk):
                        rows, c0 = g.tbs[tb]
                        sl = xi["i"] % 3
                        xi["i"] += 1
                        P.dma(SP, lambda e: e.dma_start(out=XSL2[:rows, sl, :], in_=XMID[xm, c0:c0 + rows, q * 256:(q + 1) * 256]), ("xsl2", sl),
                              reads=[("xmid", tb, q)], writes=[("XSL2", sl)])
                        P.op(DVE, lambda e: e.tensor_tensor(out=XMS2[:rows, sl, :], in0=PS[:rows, bk, 0:256], in1=XSL2[:rows, sl, :], op=ALU.add),
                             reads=[("XSL2", sl)], writes=[("ps", bk), ("XMS2", sl)])
                        P.dma(SP, lambda e: e.dma_start(out=XMID[xm, c0:c0 + rows, q * 256:(q + 1) * 256], in_=XMS2[:rows, sl, :]), ("xms2", sl),
                              reads=[("XMS2", sl)], writes=[("xmid", tb, q)])
                    linear_tm(g, w_down, ch0 * 128, NCH, lambda k: (H16F[:, k, :], ("HF", k)), ep_down)
                if not full:
                    return
                if nsc:
                    P.dma(SP, lambda e: e.dma_start(out=ffn_s[samp0:samp0 + NS, 0, :], in_=sf[samp0:samp0 + NS, 1, :]), "fso")
                    for n_ in range(NS):
                        elem_dma(ffn_s[samp0 + n_, 1, :].rearrange("(c p) -> p c", p=128), GS[:, :, n_], "fso", reads=["GS"])
                P.barrier()
                P.dma(SP, lambda e: e.dma_start(out=GF, in_=final_norm_g.partition_broadcast(128)), "gf", writes=["GF"])
                for tb, (rows, c0) in enumerate(g.tbs):
                    slot = tb % 2
                    P.dma(SP, lambda e, rows=rows, c0=c0, slot=slot: e.dma_start(out=XT[:rows, slot, :], in_=XMID[xm, c0:c0 + rows, :]), ("xt", slot),
                          reads=[("xmid", tb, q) for q in range(16)], writes=[("XT", slot)])
                    norm_stats(slot, rows)
                    P.op(DVE, lambda e, rows=rows, slot=slot: e.scalar_tensor_tensor(out=XT[:rows, slot, :], in0=XT[:rows, slot, :], scalar=RS[:rows, slot:slot + 1],
                                                                                      in1=GF[:rows, :], op0=ALU.mult, op1=ALU.mult),
                         reads=[("RS", slot), "GF"], writes=[("XT", slot)])
                    P.dma(SP, lambda e, rows=rows, slot=slot, tb=tb: e.dma_start(out=outp(tb), in_=XT[:rows, slot, :]), ("yo", slot),
                          reads=[("XT", slot)])

            GM = Geo(NP, NS)
            if PRO:
                done = 0
                rest = PRO - 32
                while done < rest:
                    cnt = min(512, rest - done)
                    gp = Geo(cnt, 0)
                    run_tile(gp, "ssm", lambda tb, done=done, gp=gp: xprev[done + gp.tbs[tb][1]: done + gp.tbs[tb][1] + gp.tbs[tb][0], :], NT, 0, None)
                    done += cnt
                gm = Geo(32, 0)
                run_tile(gm, "mini", lambda tb: xprev[PRO - 32:PRO, :], NT, 0, None)
                P.op(DVE, lambda e: e.tensor_scalar(out=FH[:], in0=FH[:], scalar1=HMASK[:, 0:1], scalar2=None, op0=ALU.mult), reads=["HMASK"], writes=["FH"])
                P.op(DVE, lambda e: e.tensor_scalar(out=HALO[:], in0=HALO[:], scalar1=HMASK[:, 0:1], scalar2=None, op0=ALU.mult), reads=["HMASK"], writes=["HALO"])
                P.op(DVE, lambda e: e.tensor_scalar(out=CARRY[:], in0=CARRY[:], scalar1=HMASK[:, 0:1], scalar2=None, op0=ALU.mult), reads=["HMASK"], writes=["CARRY"])
            for t in range(NT):
                def xsrc(tb, t=t):
                    rows, c0 = GM.tbs[tb]
                    if c0 < NP:
                        return xp[t * NP + c0: t * NP + c0 + rows, :]
                    return xs[t * NS:(t + 1) * NS, :]

                def ydst(tb, t=t):
                    rows, c0 = GM.tbs[tb]
                    if c0 < NP:
                        return y_p[t * NP + c0: t * NP + c0 + rows, :]
                    return y_s[t * NS:(t + 1) * NS, :]
                run_tile(GM, "full", xsrc, t, t * NS, ydst)
            for c in range(16):
                elem_dma(conv_p[:, c * 128:(c + 1) * 128].rearrange("k p -> p k"), HALO[:, c, :], "cpo", reads=["HALO"])
            elem_dma(ssr_p.rearrange("(s p) -> p s", p=128), CARRY[:, :, 0], "hpo", reads=["CARRY"])
            elem_dma(ssi_p.rearrange("(s p) -> p s", p=128), CARRY[:, :, 1], "hpo", reads=["CARRY"])
            for k in range(2):
                elem_dma(ffn_p[k, :].rearrange("(c p) -> p c", p=128), FH[:, :, k], "fpo", reads=["FH"])

        blocks = []
        Pd = Prog(nc, dry=True)
        emit_all(Pd, blocks)
        Pr = Prog(nc, dry=False)
        emit_all(Pr, blocks)
        Pr.emit(st)
    return nc


_NT = 2
_PRO = 1024
_W_KEYS = ["norm_mix_g", "w_in", "conv_w", "conv_b", "ln_g", "ln_b", "w_conv_out", "lam_re", "lam_im", "log_dt",
           "b_re", "b_im", "c_re", "c_im", "d_skip", "w_glu", "w_ssm_out", "w_o", "norm_ffn_g", "w_up",
           "ffn_conv_w", "ffn_conv_b", "w_down"]


def make_in_map(inputs, b, tok0, samp0, NT, PRO=0):
    f = lambda a: np.ascontiguousarray(np.asarray(a, dtype=np.float32))
    nsa = NT * NS
    m = {}
    m["xp"] = f(inputs["x_prompt"][b, tok0:tok0 + NT * NP])
    if PRO:
        if tok0 >= PRO:
            m["xprev"] = f(inputs["x_prompt"][b, tok0 - PRO:tok0])
            m["hmask"] = np.ones((128, 1), np.float32)
        else:
            assert tok0 == 0
            m["xprev"] = np.zeros((PRO, D), np.float32)
            m["hmask"] = np.zeros((128, 1), np.float32)
    m["xs"] = f(inputs["x_sample"][samp0:samp0 + nsa, 0])
    m["sc"] = f(inputs["state_conv"][0, samp0:samp0 + nsa])
    m["sr"] = f(inputs["state_ssm_re"][0, samp0:samp0 + nsa]).reshape(nsa, 8192)
    m["si"] = f(inputs["state_ssm_im"][0, samp0:samp0 + nsa]).reshape(nsa, 8192)
    m["sf"] = f(inputs["state_ffn_conv"][0, samp0:samp0 + nsa])
    for k in _W_KEYS:
        m[k] = f(inputs[k][0])
    m["final_norm_g"] = f(inputs["final_norm_g"])
    m["ident"] = np.eye(128, dtype=np.float32)
    return m


def kernel(**inputs):
    NT, PRO = _NT, _PRO
    nc = build_nc(NT, PRO)
    nsa = NT * NS
    ntok = NT * NP
    in_maps = []
    for c in range(8):
        b, hf = c // 2, c % 2
        in_maps.append(make_in_map(inputs, b, hf * ntok, c * nsa, NT, PRO))
    res = run_bass_kernel_spmd(nc, in_maps, core_ids=list(range(8)))
    r = res.results
    y_prompt = np.zeros((4, 2 * ntok, D), np.float32)
    for c in range(8):
        y_prompt[c // 2, (c % 2) * ntok:(c % 2 + 1) * ntok] = r[c]["y_p"]
    y_sample = np.concatenate([r[c]["y_s"] for c in range(8)])[:, None, :].astype(np.float32)
    last = [2 * b + 1 for b in range(4)]
    conv_prompt = np.stack([r[c]["conv_p"] for c in last])[None].astype(np.float32)
    conv_sample = np.concatenate([r[c]["conv_s"] for c in range(8)])[None].astype(np.float32)
    ssr_p = np.stack([r[c]["ssr_p"].reshape(128, 64) for c in last])[None].astype(np.float32)
    ssi_p = np.stack([r[c]["ssi_p"].reshape(128, 64) for c in last])[None].astype(np.float32)
    ssr_s = np.concatenate([r[c]["ssr_s"].reshape(-1, 128, 64) for c in range(8)])[None].astype(np.float32)
    ssi_s = np.concatenate([r[c]["ssi_s"].reshape(-1, 128, 64) for c in range(8)])[None].astype(np.float32)
    ffn_p = np.stack([r[c]["ffn_p"] for c in last])[None].astype(np.float32)
    ffn_s = np.concatenate([r[c]["ffn_s"] for c in range(8)])[None].astype(np.float32)
    return (y_prompt, y_sample, conv_prompt, conv_sample, ssr_p, ssi_p, ssr_s, ssi_s, ffn_p, ffn_s)
```

```python
import math
import numpy as np
from contextlib import ExitStack
import concourse.bass as bass
import concourse.mybir as mybir
from concourse.bass_utils import run_bass_kernel_spmd

F32 = mybir.dt.float32
BF16 = mybir.dt.bfloat16
AF = mybir.ActivationFunctionType
ALU = mybir.AluOpType
AX = mybir.AxisListType

PE, ACT, DVE, POOL, SP = "pe", "act", "dve", "pool", "sp"

D = 4096
CWD = 2048
SWD = 2048
DFF = 11008
INW = 14336
NP = 512
NS = 8
N = NP + NS
HN = N // 2
PADH = 30
PAD = 256
NSLOT = 5
LEAD = 32
NPX = NP + LEAD
NX = NPX + NS
EPS = 1e-6
NCH = 43


class _Rec:
    __slots__ = ("eng", "fn", "waits", "needs_inc", "dma_sem", "dma_val", "cnt", "seq")

    def __init__(self, eng, fn):
        self.eng = eng
        self.fn = fn
        self.waits = []
        self.needs_inc = False
        self.dma_sem = None
        self.dma_val = 0
        self.cnt = 0
        self.seq = 0


class Prog:
    def __init__(self, nc, dry=False):
        self.nc = nc
        self.dry = dry
        self.ops = {e: [] for e in (PE, ACT, DVE, POOL, SP)}
        self.res = {}
        self.dma_cnt = {}
        self.last_dma = {}
        self.pending = {}

    def _deps(self, rec, reads, writes, acc):
        if self.dry:
            return
        deps = []
        pend = self.pending.get(rec.eng)
        if pend:
            deps.extend(pend)
            self.pending[rec.eng] = None
        for r in reads:
            st = self.res.get(r)
            if st:
                deps.extend(st[0])
        for w in writes:
            st = self.res.get(w)
            if st:
                deps.extend(st[0])
                deps.extend(st[1])
        for w in acc:
            st = self.res.get(w)
            if st:
                deps.extend(st[0])
                deps.extend(st[1])
        best = {}
        for d in deps:
            if d is rec:
                continue
            if d.dma_sem is None and rec.dma_sem is None and d.eng == rec.eng and rec.eng == PE:
                continue
            key = ("d", d.dma_sem) if d.dma_sem is not None else ("e", d.eng)
            val = d.dma_val if d.dma_sem is not None else d.seq
            cur = best.get(key)
            if cur is None or val > cur[0]:
                best[key] = (val, d)
        for _, d in best.values():
            rec.waits.append(d)
            if d.dma_sem is None:
                d.needs_inc = True
        for r in reads:
            self.res.setdefault(r, [[], []])[1].append(rec)
        for w in writes:
            self.res[w] = [[rec], []]
        for w in acc:
            st = self.res.setdefault(w, [[], []])
            st[0].append(rec)
            st[1] = []

    def op(self, eng, fn, reads=(), writes=(), acc=()):
        rec = _Rec(eng, fn)
        rec.seq = len(self.ops[eng]) + 1
        self._deps(rec, reads, writes, acc)
        if not self.dry:
            self.ops[eng].append(rec)
        return rec

    def dma(self, eng, fn, sem, reads=(), writes=(), acc=()):
        rec = _Rec(eng, fn)
        rec.dma_sem = sem
        self.dma_cnt[sem] = self.dma_cnt.get(sem, 0) + 16
        rec.dma_val = self.dma_cnt[sem]
        self._deps(rec, reads, writes, acc)
        if not self.dry:
            self.ops[eng].append(rec)
            self.last_dma[sem] = rec
        return rec

    def barrier(self):
        if self.dry:
            return
        snap = []
        for e, lst in self.ops.items():
            for rec in reversed(lst):
                if rec.dma_sem is None:
                    snap.append(rec)
                    break
        for k, rec in self.last_dma.items():
            if not (isinstance(k, tuple) and k[0] == "w"):
                snap.append(rec)
        for e in (PE, ACT, DVE, SP):
            self.pending[e] = list(snap)

    def emit(self, stack):
        nc = self.nc
        esem = {e: stack.enter_context(nc.semaphore("es_" + e)) for e in (PE, ACT, DVE, POOL)}
        dsem = {}
        for k in self.dma_cnt:
            dsem[k] = stack.enter_context(nc.semaphore("ds_%d" % len(dsem)))
        for e, lst in self.ops.items():
            c = 0
            for rec in lst:
                if rec.dma_sem is None and rec.needs_inc:
                    c += 1
                rec.cnt = c
            assert c < 65000, (e, c)
        for k, v in self.dma_cnt.items():
            assert v < 65000, (k, v)
        block = stack.enter_context(nc.Block())

        def run(e, eng, final=False):
            waited = {}
            for rec in self.ops[e]:
                for d in rec.waits:
                    if d.dma_sem is not None:
                        key, val, sem = ("d", d.dma_sem), d.dma_val, dsem[d.dma_sem]
                    else:
                        key, val, sem = ("e", d.eng), d.cnt, esem[d.eng]
                    if waited.get(key, 0) >= val:
                        continue
                    waited[key] = val
                    eng.wait_ge(sem, val)
                ins = rec.fn(eng)
                if rec.dma_sem is not None:
                    ins.then_inc(dsem[rec.dma_sem], 16)
                elif rec.needs_inc:
                    ins.then_inc(esem[e], 1)
            if final:
                for k, v in self.dma_cnt.items():
                    if waited.get(("d", k), 0) < v:
                        eng.wait_ge(dsem[k], v)

        block.tensor(lambda eng: run(PE, eng))
        block.scalar(lambda eng: run(ACT, eng))
        block.vector(lambda eng: run(DVE, eng))
        block.gpsimd(lambda eng: run(POOL, eng))
        block.sync(lambda eng: run(SP, eng, final=True))


class Geo:
    def __init__(self, np_, ns, lead=0):
        self.np, self.ns, self.n, self.lead = np_, ns, np_ + ns, lead
        if self.n > 512:
            h = self.n // 2
            self.parts = [(0, h), (h, self.n - h)]
        else:
            self.parts = [(0, self.n)]
        self.tbs = ([(lead, 0)] if lead else []) + [(min(128, np_ - r), r) for r in range(lead, np_, 128)] + ([(ns, np_)] if ns else [])


def build_nc(NT, PRO=0):
    nc = bass.Bass("TRN2", target_bir_lowering=False)
    NSA = NT * NS

    def din(name, shape):
        return nc.dram_tensor(name, list(shape), F32, kind="ExternalInput").ap()

    def dout(name, shape):
        return nc.dram_tensor(name, list(shape), F32, kind="ExternalOutput").ap()

    xp = din("xp", [NT * NP, D]); xs = din("xs", [NSA, D])
    if PRO:
        xprev = din("xprev", [PRO, D]); hmask = din("hmask", [128, 1])
    sc = din("sc", [NSA, 30, CWD]); sr = din("sr", [NSA, 8192]); si = din("si", [NSA, 8192])
    sf = din("sf", [NSA, 2, DFF])
    norm_mix_g = din("norm_mix_g", [D]); w_in = din("w_in", [D, INW])
    conv_w = din("conv_w", [31, CWD]); conv_b = din("conv_b", [CWD])
    ln_g = din("ln_g", [CWD]); ln_b = din("ln_b", [CWD]); w_conv_out = din("w_conv_out", [CWD, D])
    lam_re = din("lam_re", [128, 64]); lam_im = din("lam_im", [128, 64]); log_dt = din("log_dt", [128])
    b_re = din("b_re", [128, 64, 16]); b_im = din("b_im", [128, 64, 16])
    c_re = din("c_re", [128, 16, 64]); c_im = din("c_im", [128, 16, 64])
    d_skip = din("d_skip", [SWD]); w_glu = din("w_glu", [SWD, SWD]); w_ssm_out = din("w_ssm_out", [SWD, D])
    w_o = din("w_o", [D, D]); norm_ffn_g = din("norm_ffn_g", [D]); w_up = din("w_up", [D, 2 * DFF])
    ffn_conv_w = din("ffn_conv_w", [3, DFF]); ffn_conv_b = din("ffn_conv_b", [DFF])
    w_down = din("w_down", [DFF, D]); final_norm_g = din("final_norm_g", [D])
    ident = din("ident", [128, 128])

    y_p = dout("y_p", [NT * NP, D]); y_s = dout("y_s", [NSA, D])
    conv_p = dout("conv_p", [30, CWD]); conv_s = dout("conv_s", [NSA, 30, CWD])
    ssr_p = dout("ssr_p", [8192]); ssi_p = dout("ssi_p", [8192])
    ssr_s = dout("ssr_s", [NSA, 8192]); ssi_s = dout("ssi_s", [NSA, 8192])
    ffn_p = dout("ffn_p", [2, DFF]); ffn_s = dout("ffn_s", [NSA, 2, DFF])

    XMID = nc.dram_tensor("xmid", [NT + 1, NX, D], F32).ap()
    LMD = nc.dram_tensor("lmd", [64, 128, 4, 128], BF16).ap()
    TBL = nc.dram_tensor("tbl", [64, 128, 2, NPX], F32).ap()

    st = ExitStack()
    with st:
        def sb(name, shape, dt=F32):
            return st.enter_context(nc.sbuf_tensor(name, list(shape), dt))

        R1 = sb("R1", [128, 32 * NX], BF16)
        RA = sb("RA", [128, 36288], BF16)
        R5 = sb("R5", [128, 16 * NX], BF16)
        TA = sb("TA", [128, 7680], BF16)
        WR = sb("WR", [128, NSLOT, 16, 256], BF16)
        IDENT = sb("IDENT", [128, 128]); ONES = sb("ONES", [128, 128])
        G1 = sb("G1", [128, 32, 1]); G2 = sb("G2", [128, 32, 1])
        CWT = sb("CWT", [128, 16, 31]); CBS = sb("CBS", [128, 16, 1])
        LNG = sb("LNG", [128, 16, 1]); LNB = sb("LNB", [128, 16, 1])
        FCW = sb("FCW", [128, 86, 3]); FCB = sb("FCB", [128, 86, 1]); DSK = sb("DSK", [128, 16, 1])
        PWR = sb("PWR", [128, 64, 9]); PWI = sb("PWI", [128, 64, 9]); PWN = sb("PWN", [128, 64, 9])
        HALO = sb("HALO", [128, 16, PADH]); FH = sb("FH", [128, 86, 2]); CARRY = sb("CARRY", [128, 64, 2])
        SS = sb("SS", [128, 2]); RS = sb("RS", [128, 2])
        RMAG = sb("RMAG", [128, 64, 1])
        HMASK = sb("HMASK", [128, 1])
        TB = sb("TB", [128, 2, 2, NPX])
        PS = st.enter_context(nc.psum_tensor("PS", [128, 8, 512], F32))

        def f32v(t, b0, nbytes):
            return t[:, b0 // 2:(b0 + nbytes) // 2].bitcast(F32)

        def b16v(t, b0, nbytes):
            return t[:, b0 // 2:(b0 + nbytes) // 2]

        XN = R1[:, :].rearrange("p (c n) -> p c n", c=32)
        GF = f32v(R1, 0, 4 * D)
        R2o, R3o = 0, 37248
        GLU = f32v(RA, R2o, 16 * (PADH + NX) * 4).rearrange("p (c n) -> p c n", c=16)
        U16 = b16v(RA, R2o, 16 * NX * 2).rearrange("p (c n) -> p c n", c=16)
        SG16 = b16v(RA, R2o + 16 * NX * 2, 16 * NX * 2).rearrange("p (c n) -> p c n", c=16)
        XT = f32v(RA, R2o, 2 * D * 4).rearrange("p (s n) -> p s n", s=2)
        MG16 = b16v(RA, R2o, 32 * NX * 2).rearrange("p (c n) -> p c n", c=32)
        CV = f32v(RA, R3o, 16 * NX * 4).rearrange("p (c n) -> p c n", c=16)
        SGG16 = b16v(RA, R3o, 16 * NX * 2).rearrange("p (c n) -> p c n", c=16)
        XAo = R3o + 16 * NX * 2
        XA = f32v(RA, XAo, 2 * NX * 4).rearrange("p (c n) -> p c n", c=2)
        T3B = f32v(RA, XAo + 4416, 3 * NPX * 4).rearrange("p (c n) -> p c n", c=3)
        HSM = f32v(RA, XAo + 4416 + 6528, 2 * NS * 4).rearrange("p (c n) -> p c n", c=2)
        H16S = b16v(RA, XAo + 11072, 4 * NX * 2).rearrange("p (s r n) -> p s r n", s=2, r=2)
        assert XAo + 11072 + 4 * NX * 2 <= 72576
        SQJ = b16v(RA, R3o, D * 2)
        H16F = b16v(RA, 0, NCH * NX * 2).rearrange("p (c n) -> p c n", c=NCH)
        FSo = NCH * NX * 2
        FS = f32v(RA, FSo, 86 * 16 * 4).rearrange("p (c r) -> p c r", c=86)
        GS = f32v(RA, FSo + 86 * 16 * 4, 86 * 8 * 4).rearrange("p (c r) -> p c r", c=86)
        CB16 = R5[:, :].rearrange("p (c n) -> p c n", c=16)
        SIGT = f32v(TA, 0, 2208)
        SQ = f32v(TA, 2208, 2208); MU = f32v(TA, 4416, 2208); RSTD = f32v(TA, 6624, 2208)
        TMPS = f32v(TA, 10880, 960).rearrange("p (n k) -> p n k", n=8)
        CSS = f32v(TA, 11840, 32)
        YT = f32v(TA, 2208, 2208)
        LMS = b16v(TA, 4416, 2048).rearrange("p (s m n) -> p s m n", s=2, m=4)
        H0 = f32v(TA, 6464, 4096).rearrange("p (s r n) -> p s r n", s=64, r=2)
        HS = f32v(TA, 10560, 4096).rearrange("p (s r n) -> p s r n", s=64, r=2)
        MA = f32v(TA, 2208, 4416).rearrange("p (m n) -> p m n", m=2)
        T2 = f32v(TA, 6624, 2208)
        XSL = f32v(TA, 8832, 3072).rearrange("p (s n) -> p s n", s=3)
        XMS = f32v(TA, 11904, 3072).rearrange("p (s n) -> p s n", s=3)
        G32 = f32v(TA, 0, 2216)
        GCT = f32v(TA, 2216, 2208)
        SILU = f32v(TA, 4424, 4416).rearrange("p (m n) -> p m n", m=2)
        XSL2 = f32v(TA, 8840, 3072).rearrange("p (s n) -> p s n", s=3)
        XMS2 = f32v(TA, 11912, 3072).rearrange("p (s n) -> p s n", s=3)

        CSTG = f32v(TA, 8832, 2048).rearrange("p (s b n) -> p s b n", s=2, b=2)

        def emit_all(P, blocks):
            state = {"wi": 0, "acc": 0, "tm": 0, "issued": 0}

            def wissue(i):
                if i >= len(blocks):
                    return
                W, row0, nk, col0, ncols = blocks[i]
                slot = i % NSLOT
                src = W[row0:row0 + nk * 128, col0:col0 + ncols].rearrange("(k p) n -> p k n", p=128)
                P.dma(POOL, lambda e, slot=slot, src=src, nk=nk, ncols=ncols: e.dma_start(out=WR[:, slot, 0:nk, 0:ncols], in_=src),
                      ("w", slot), writes=[("w", slot)])

            def wnext(W, row0, nk, col0, ncols):
                i = state["wi"]
                state["wi"] += 1
                if P.dry:
                    blocks.append((W, row0, nk, col0, ncols))
                else:
                    assert blocks[i][1:] == (row0, nk, col0, ncols)
                return i % NSLOT

            def wrelease(slot_unused=None):
                if P.dry:
                    return
                wissue(state["issued"])
                state["issued"] += 1

            if not P.dry:
                for i in range(min(NSLOT, len(blocks))):
                    wissue(i)
                state["issued"] = NSLOT

            def next_acc():
                b = state["acc"] * 2
                state["acc"] = (state["acc"] + 1) % 4
                return b

            def pv(g, b):
                if len(g.parts) == 2:
                    return PS[:, b:b + 2, 0:g.parts[0][1]]
                return PS[:, b, 0:g.n]

            def sv(g, ap2d):
                a_ = ap2d[:, 0:g.n]
                if len(g.parts) == 2:
                    return a_.rearrange("p (h n) -> p h n", h=2)
                return a_

            def pk(g, b):
                return [("ps", b + i) for i in range(len(g.parts))]

            def mm_group(g, slots, K, m, rhs_fn, b):
                for k in range(K):
                    slot = slots[k // 16]
                    rap, rkey = rhs_fn(k)
                    for i, (c0, n) in enumerate(g.parts):
                        P.op(PE, lambda e, slot=slot, k=k, m=m, i=i, c0=c0, n=n, rap=rap, b=b, K=K: e.matmul(
                            PS[:, b + i, 0:n], lhsT=WR[:, slot, k % 16, m * 128:(m + 1) * 128],
                            rhs=rap[:, c0:c0 + n], start=(k == 0), stop=(k == K - 1)),
                            reads=[("w", slot), rkey], acc=[("ps", b + i)])

            def linear_fm(g, W, col0, ncols_total, K, rhs_fn, epilogue, row0=0):
                nblk = (K + 15) // 16
                ci = 0
                for s0 in range(col0, col0 + ncols_total, 256):
                    ncols = min(256, col0 + ncols_total - s0)
                    slots = [wnext(W, row0 + kb * 2048, min(16, K - kb * 16), s0, ncols) for kb in range(nblk)]
                    for m in range(ncols // 128):
                        b = next_acc()
                        mm_group(g, slots, K, m, rhs_fn, b)
                        epilogue(ci, b)
                        ci += 1
                    for _ in slots:
                        wrelease()

            def linear_tm(g, W, row0, K, lhs_fn, epilogue):
                nblk = (K + 15) // 16
                ntb = len(g.tbs)
                for q in range(16):
                    banks = []
                    for tb in range(ntb):
                        banks.append(state["tm"] % 8)
                        state["tm"] += 1
                    for kb in range(nblk):
                        nk = min(16, K - kb * 16)
                        slot = wnext(W, row0 + kb * 2048, nk, q * 256, 256)
                        for tb in range(ntb):
                            rows, c0 = g.tbs[tb]
                            for k in range(nk):
                                kk = kb * 16 + k
                                lap, lkey = lhs_fn(kk)
                                P.op(PE, lambda e, slot=slot, k=k, kk=kk, lap=lap, rows=rows, c0=c0, bk=banks[tb], K=K: e.matmul(
                                    PS[:rows, bk, 0:256], lhsT=lap[:, c0:c0 + rows], rhs=WR[:, slot, k, 0:256],
                                    start=(kk == 0), stop=(kk == K - 1)),
                                    reads=[("w", slot), lkey], acc=[("ps", banks[tb])])
                        wrelease()
                    for tb in range(ntb):
                        epilogue(q, tb, banks[tb])

            def elem_dma(out_ap, in_ap, sem, reads=(), writes=(), acc=()):
                P.dma(SP, lambda e, o=out_ap, i=in_ap: e.dma_start(out=o, in_=i, allow_slow_non_contiguous=True), sem, reads=reads, writes=writes, acc=acc)

            P.dma(SP, lambda e: e.dma_start(out=IDENT[:], in_=ident), "c0", writes=["IDENT"])
            if PRO:
                P.dma(SP, lambda e: e.dma_start(out=HMASK[:], in_=hmask), "c0b", writes=["HMASK"])
            P.op(DVE, lambda e: e.memset(ONES[:], 1.0), writes=["ONES"])
            P.op(DVE, lambda e: e.memset(HALO[:], 0.0), writes=["HALO"])
            P.op(DVE, lambda e: e.memset(FH[:], 0.0), writes=["FH"])
            P.op(DVE, lambda e: e.memset(CARRY[:], 0.0), writes=["CARRY"])
            elem_dma(G1[:, :, 0], norm_mix_g.rearrange("(c p) -> p c", p=128), "c1", writes=["G1"])
            elem_dma(G2[:, :, 0], norm_ffn_g.rearrange("(c p) -> p c", p=128), "c2", writes=["G2"])
            for k in range(31):
                elem_dma(CWT[:, :, k], conv_w[k].rearrange("(c p) -> p c", p=128), "c3", writes=[], reads=[])
            P.res["CWT"] = [[P.last_dma.get("c3")] if not P.dry else [], []]
            elem_dma(CBS[:, :, 0], conv_b.rearrange("(c p) -> p c", p=128), "c4", writes=["CBS"])
            elem_dma(LNG[:, :, 0], ln_g.rearrange("(c p) -> p c", p=128), "c5", writes=["LNG"])
            elem_dma(LNB[:, :, 0], ln_b.rearrange("(c p) -> p c", p=128), "c6", writes=["LNB"])
            for k in range(3):
                elem_dma(FCW[:, :, k], ffn_conv_w[k].rearrange("(c p) -> p c", p=128), "c7")
            P.res["FCW"] = [[P.last_dma.get("c7")] if not P.dry else [], []]
            elem_dma(FCB[:, :, 0], ffn_conv_b.rearrange("(c p) -> p c", p=128), "c8", writes=["FCB"])
            elem_dma(DSK[:, :, 0], d_skip.rearrange("(c p) -> p c", p=128), "c9", writes=["DSK"])

            def ra32(b0, shape):
                n = int(np.prod(shape))
                v = f32v(RA, b0, n * 4)
                if len(shape) == 2:
                    return v.rearrange("p (a b) -> p a b", a=shape[0])
                if len(shape) == 3:
                    return v.rearrange("p (a b c) -> p a b c", a=shape[0], b=shape[1])
                return v

            o = 36864
            names = ["LR", "LI", "LDT", "DT", "LRD", "LID", "MAG", "YA", "COSV", "SINV", "ABR", "ABI", "DEN", "NR", "T1", "T2s", "FRE", "FIM"]
            V = {}
            for nm in names:
                V[nm] = ra32(o, [64, 1]); o += 256
            BRE = ra32(o, [64, 16]); o += 4096
            BIM = ra32(o, [64, 16]); o += 4096
            BBR = ra32(o, [64, 16]); o += 4096
            BBI = ra32(o, [64, 16]); o += 4096
            TB1 = ra32(o, [64, 16]); o += 4096
            STG = b16v(RA, o, 1024).rearrange("p (s n) -> p s n", s=4); o += 1024
            STG2 = b16v(RA, o, 1024).rearrange("p (s n) -> p s n", s=4); o += 1024
            ZB = ra32(0, [64, 128])
            ZB4 = f32v(RA, 0, 32768).rearrange("p (c j n) -> p c j n", c=16, j=4)

            lam2 = lambda a: a.rearrange("(s two) p -> two p s", two=2)
            for gl in range(2):
                elem_dma(V["LR"][gl * 64:(gl + 1) * 64, :, 0], lam2(lam_re)[gl], "s0")
                elem_dma(V["LI"][gl * 64:(gl + 1) * 64, :, 0], lam2(lam_im)[gl], "s0")
                P.dma(SP, lambda e, gl=gl: e.dma_start(out=V["LDT"][gl * 64:(gl + 1) * 64, :, 0],
                                                       in_=log_dt.rearrange("(s two) -> two s", two=2)[gl].partition_broadcast(64), allow_slow_non_contiguous=True), "s0")
                P.dma(SP, lambda e, gl=gl: e.dma_start(out=BRE[gl * 64:(gl + 1) * 64], in_=b_re.rearrange("(s two) p h -> two p s h", two=2)[gl]), "s0")
                P.dma(SP, lambda e, gl=gl: e.dma_start(out=BIM[gl * 64:(gl + 1) * 64], in_=b_im.rearrange("(s two) p h -> two p s h", two=2)[gl]), "s0")
            if not P.dry:
                P.res["SPRM"] = [[P.last_dma["s0"]], []]

            def dve(fn, r=(), w=()):
                P.op(DVE, fn, reads=r, writes=w)

            def actop(fn, r=(), w=()):
                P.op(ACT, fn, reads=r, writes=w)

            S_ = "SPRM"
            actop(lambda e: e.activation(out=V["DT"][:], in_=V["LDT"][:], func=AF.Exp), [S_], [S_])
            dve(lambda e: e.tensor_tensor(out=V["LRD"][:], in0=V["LR"][:], in1=V["DT"][:], op=ALU.mult), [S_], [S_])
            dve(lambda e: e.tensor_tensor(out=V["LID"][:], in0=V["LI"][:], in1=V["DT"][:], op=ALU.mult), [S_], [S_])
            actop(lambda e: e.activation(out=V["MAG"][:], in_=V["LRD"][:], func=AF.Exp, scale=1.0 / 32), [S_], [S_])
            actop(lambda e: e.activation(out=V["SINV"][:], in_=V["LID"][:], func=AF.Sin, scale=1.0 / 32), [S_], [S_])
            actop(lambda e: e.activation(out=V["COSV"][:], in_=V["LID"][:], func=AF.Sin, scale=1.0 / 32, bias=0.5 * math.pi), [S_], [S_])
            dve(lambda e: e.tensor_tensor(out=V["ABR"][:], in0=V["MAG"][:], in1=V["COSV"][:], op=ALU.mult), [S_], [S_])
            dve(lambda e: e.tensor_tensor(out=V["ABI"][:], in0=V["MAG"][:], in1=V["SINV"][:], op=ALU.mult), [S_], [S_])
            for _sq in range(5):
                dve(lambda e: e.tensor_tensor(out=V["T1"][:], in0=V["ABR"][:], in1=V["ABR"][:], op=ALU.mult), [S_], [S_])
                dve(lambda e: e.tensor_tensor(out=V["T2s"][:], in0=V["ABI"][:], in1=V["ABI"][:], op=ALU.mult), [S_], [S_])
                dve(lambda e: e.tensor_tensor(out=V["YA"][:], in0=V["ABR"][:], in1=V["ABI"][:], op=ALU.mult), [S_], [S_])
                dve(lambda e: e.tensor_tensor(out=V["ABR"][:], in0=V["T1"][:], in1=V["T2s"][:], op=ALU.subtract), [S_], [S_])
                dve(lambda e: e.tensor_scalar(out=V["ABI"][:], in0=V["YA"][:], scalar1=2.0, scalar2=None, op0=ALU.mult), [S_], [S_])
            dve(lambda e: e.tensor_tensor(out=V["DEN"][:], in0=V["LR"][:], in1=V["LR"][:], op=ALU.mult), [S_], [S_])
            dve(lambda e: e.tensor_tensor(out=V["T1"][:], in0=V["LI"][:], in1=V["LI"][:], op=ALU.mult), [S_], [S_])
            dve(lambda e: e.tensor_tensor(out=V["DEN"][:], in0=V["DEN"][:], in1=V["T1"][:], op=ALU.add), [S_], [S_])
            dve(lambda e: e.reciprocal(out=V["DEN"][:], in_=V["DEN"][:]), [S_], [S_])
            dve(lambda e: e.tensor_scalar(out=V["NR"][:], in0=V["ABR"][:], scalar1=-1.0, scalar2=None, op0=ALU.add), [S_], [S_])
            dve(lambda e: e.tensor_tensor(out=V["T1"][:], in0=V["NR"][:], in1=V["LR"][:], op=ALU.mult), [S_], [S_])
            dve(lambda e: e.tensor_tensor(out=V["T2s"][:], in0=V["ABI"][:], in1=V["LI"][:], op=ALU.mult), [S_], [S_])
            dve(lambda e: e.tensor_tensor(out=V["T1"][:], in0=V["T1"][:], in1=V["T2s"][:], op=ALU.add), [S_], [S_])
            dve(lambda e: e.tensor_tensor(out=V["FRE"][:], in0=V["T1"][:], in1=V["DEN"][:], op=ALU.mult), [S_], [S_])
            dve(lambda e: e.tensor_tensor(out=V["T1"][:], in0=V["ABI"][:], in1=V["LR"][:], op=ALU.mult), [S_], [S_])
            dve(lambda e: e.tensor_tensor(out=V["T2s"][:], in0=V["NR"][:], in1=V["LI"][:], op=ALU.mult), [S_], [S_])
            dve(lambda e: e.tensor_tensor(out=V["T1"][:], in0=V["T1"][:], in1=V["T2s"][:], op=ALU.subtract), [S_], [S_])
            dve(lambda e: e.tensor_tensor(out=V["FIM"][:], in0=V["T1"][:], in1=V["DEN"][:], op=ALU.mult), [S_], [S_])
            dve(lambda e: e.tensor_copy(out=PWR[:, :, 0:1], in_=V["ABR"][:]), [S_], ["PW"])
            dve(lambda e: e.tensor_copy(out=PWI[:, :, 0:1], in_=V["ABI"][:]), [S_], ["PW"])
            for j in range(8):
                dve(lambda e, j=j: e.tensor_tensor(out=V["T1"][:], in0=PWR[:, :, j:j + 1], in1=PWR[:, :, j:j + 1], op=ALU.mult), ["PW", S_], [S_])
                dve(lambda e, j=j: e.tensor_tensor(out=V["T2s"][:], in0=PWI[:, :, j:j + 1], in1=PWI[:, :, j:j + 1], op=ALU.mult), ["PW", S_], [S_])
                dve(lambda e, j=j: e.tensor_tensor(out=PWR[:, :, j + 1:j + 2], in0=V["T1"][:], in1=V["T2s"][:], op=ALU.subtract), [S_], ["PW"])
                dve(lambda e, j=j: e.tensor_tensor(out=V["T1"][:], in0=PWR[:, :, j:j + 1], in1=PWI[:, :, j:j + 1], op=ALU.mult), ["PW", S_], [S_])
                dve(lambda e, j=j: e.tensor_scalar(out=PWI[:, :, j + 1:j + 2], in0=V["T1"][:], scalar1=2.0, scalar2=None, op0=ALU.mult), [S_], ["PW"])
            dve(lambda e: e.tensor_scalar(out=PWN[:], in0=PWI[:], scalar1=-1.0, scalar2=None, op0=ALU.mult), ["PW"], ["PW"])
            bc = lambda a: a.to_broadcast([128, 64, 16])
            dve(lambda e: e.tensor_tensor(out=BBR[:], in0=BRE[:], in1=bc(V["FRE"][:]), op=ALU.mult), [S_], [S_])
            dve(lambda e: e.tensor_tensor(out=TB1[:], in0=BIM[:], in1=bc(V["FIM"][:]), op=ALU.mult), [S_], [S_])
            dve(lambda e: e.tensor_tensor(out=BBR[:], in0=BBR[:], in1=TB1[:], op=ALU.subtract), [S_], [S_])
            dve(lambda e: e.tensor_tensor(out=BBI[:], in0=BIM[:], in1=bc(V["FRE"][:]), op=ALU.mult), [S_], [S_])
            dve(lambda e: e.tensor_tensor(out=TB1[:], in0=BRE[:], in1=bc(V["FIM"][:]), op=ALU.mult), [S_], [S_])
            dve(lambda e: e.tensor_tensor(out=BBI[:], in0=BBI[:], in1=TB1[:], op=ALU.add), [S_], [S_])

            def transpose_store(mi, neg):
                for s4 in range(16):
                    bk = s4 % 2
                    for j in range(4):
                        s = s4 * 4 + j
                        P.op(PE, lambda e, s=s, j=j, bk=bk: e.transpose(out=PS[:, bk, j * 128:(j + 1) * 128], in_=ZB[:, s, :], identity=IDENT[:]),
                             reads=["ZB", "IDENT"], acc=[("ps", bk)])
                    stg = STG if s4 % 2 == 0 else STG2
                    skey = "STG%d" % (s4 % 2)
                    P.op(ACT, lambda e, bk=bk, stg=stg, neg=neg: e.activation(out=stg[:], in_=PS[:, bk, :].rearrange("p (j n) -> p j n", j=4),
                                                                             func=AF.Copy, scale=(-1.0 if neg else 1.0)),
                         writes=[("ps", bk), skey])
                    P.dma(SP, lambda e, s4=s4, stg=stg, mi=mi: e.dma_start(out=LMD[s4 * 4:(s4 + 1) * 4, :, mi, :].rearrange("s p n -> p s n"), in_=stg[:]),
                          "lm%d" % (s4 % 2), reads=[skey], writes=[("LMD", mi, s4)])

            dve(lambda e: e.memset(ZB[:], 0.0), [], ["ZB"])
            for mi, BB in ((0, BBR), (1, BBI)):
                BB4 = BB[:].rearrange("p (c j) h -> p c j h", j=4)
                for gl in range(2):
                    for j in range(4):
                        c0 = 32 * j + 16 * gl
                        dve(lambda e, gl=gl, j=j, c0=c0, BB4=BB4: e.tensor_copy(out=ZB4[gl * 64:(gl + 1) * 64, :, j, c0:c0 + 16],
                                                                               in_=BB4[gl * 64:(gl + 1) * 64, :, j, :]), [S_], ["ZB"])
                transpose_store(mi, False)
            dve(lambda e: e.memset(ZB[:], 0.0), [], ["ZB"])
            for mi, CC in ((2, c_re), (3, c_im)):
                CCv = CC.rearrange("(c r) h p -> r h c p", r=8)
                first = True
                for j in range(4):
                    for gl in range(2):
                        p0 = 32 * j + 16 * gl
                        P.dma(SP, lambda e, j=j, gl=gl, p0=p0, CCv=CCv: e.dma_start(out=ZB4[p0:p0 + 16, :, j, 64 * gl:64 * gl + 64], in_=CCv[2 * j + gl]),
                              "zc", reads=[], acc=["ZB"])
                transpose_store(mi, mi == 3)
            P.barrier()
            actop(lambda e: e.activation(out=RMAG[:], in_=V["LRD"][:], func=AF.Exp), [S_], ["RMAG"])
            WPR = ra32(36864 + 18 * 256 + 5 * 4096 + 2048, [64, 10])
            WPI = ra32(36864 + 18 * 256 + 5 * 4096 + 2048 + 2560, [64, 10])
            assert 36864 + 18 * 256 + 5 * 4096 + 2048 + 5120 <= 72576
            TGo = 36864 + 18 * 256
            TG = f32v(RA, TGo, 8 * 256 * 4).rearrange("p (s n) -> p s n", s=8)
            EG = f32v(RA, 0, 8 * 2 * NPX * 4).rearrange("p (s r n) -> p s r n", s=8, r=2)
            dve(lambda e: e.tensor_copy(out=V["T1"][:], in_=V["COSV"][:]), [S_], [S_])
            dve(lambda e: e.tensor_copy(out=V["T2s"][:], in_=V["SINV"][:]), [S_], [S_])
            for _sq in range(5):
                dve(lambda e: e.tensor_tensor(out=V["NR"][:], in0=V["T1"][:], in1=V["T1"][:], op=ALU.mult), [S_], [S_])
                dve(lambda e: e.tensor_tensor(out=V["DEN"][:], in0=V["T2s"][:], in1=V["T2s"][:], op=ALU.mult), [S_], [S_])
                dve(lambda e: e.tensor_tensor(out=V["YA"][:], in0=V["T1"][:], in1=V["T2s"][:], op=ALU.mult), [S_], [S_])
                dve(lambda e: e.tensor_tensor(out=V["T1"][:], in0=V["NR"][:], in1=V["DEN"][:], op=ALU.subtract), [S_], [S_])
                dve(lambda e: e.tensor_scalar(out=V["T2s"][:], in0=V["YA"][:], scalar1=2.0, scalar2=None, op0=ALU.mult), [S_], [S_])
            dve(lambda e: e.tensor_copy(out=WPR[:, :, 0:1], in_=V["T1"][:]), [S_], ["WP"])
            dve(lambda e: e.tensor_copy(out=WPI[:, :, 0:1], in_=V["T2s"][:]), [S_], ["WP"])
            for j in range(9):
                dve(lambda e, j=j: e.tensor_tensor(out=V["T1"][:], in0=WPR[:, :, j:j + 1], in1=WPR[:, :, j:j + 1], op=ALU.mult), ["WP", S_], [S_])
                dve(lambda e, j=j: e.tensor_tensor(out=V["T2s"][:], in0=WPI[:, :, j:j + 1], in1=WPI[:, :, j:j + 1], op=ALU.mult), ["WP", S_], [S_])
                dve(lambda e, j=j: e.tensor_tensor(out=WPR[:, :, j + 1:j + 2], in0=V["T1"][:], in1=V["T2s"][:], op=ALU.subtract), [S_], ["WP"])
                dve(lambda e, j=j: e.tensor_tensor(out=V["T1"][:], in0=WPR[:, :, j:j + 1], in1=WPI[:, :, j:j + 1], op=ALU.mult), ["WP", S_], [S_])
                dve(lambda e, j=j: e.tensor_scalar(out=WPI[:, :, j + 1:j + 2], in0=V["T1"][:], scalar1=2.0, scalar2=None, op0=ALU.mult), [S_], ["WP"])
            for g8 in range(8):
                s0 = g8 * 8
                dve(lambda e: e.memset(EG[:, :, 0, 0:1], 1.0), [], ["EG"])
                dve(lambda e: e.memset(EG[:, :, 1, 0:1], 0.0), [], ["EG"])
                for j in range(10):
                    n = min(1 << j, NPX - (1 << j))
                    pr = WPR[:, s0:s0 + 8, j:j + 1].to_broadcast([128, 8, n])
                    pi_ = WPI[:, s0:s0 + 8, j:j + 1].to_broadcast([128, 8, n])
                    sr_, si_ = EG[:, :, 0, 0:n], EG[:, :, 1, 0:n]
                    dr_, di_ = EG[:, :, 0, (1 << j):(1 << j) + n], EG[:, :, 1, (1 << j):(1 << j) + n]
                    tg = TG[:, :, 0:n]
                    dve(lambda e, dr_=dr_, sr_=sr_, pr=pr: e.tensor_tensor(out=dr_, in0=sr_, in1=pr, op=ALU.mult), ["WP"], ["EG"])
                    dve(lambda e, tg=tg, si_=si_, pi_=pi_: e.tensor_tensor(out=tg, in0=si_, in1=pi_, op=ALU.mult), ["WP", "EG"], ["TG"])
                    dve(lambda e, dr_=dr_, tg=tg: e.tensor_tensor(out=dr_, in0=dr_, in1=tg, op=ALU.subtract), ["TG"], ["EG"])
                    dve(lambda e, di_=di_, sr_=sr_, pi_=pi_: e.tensor_tensor(out=di_, in0=sr_, in1=pi_, op=ALU.mult), ["WP"], ["EG"])
                    dve(lambda e, tg=tg, si_=si_, pr=pr: e.tensor_tensor(out=tg, in0=si_, in1=pr, op=ALU.mult), ["WP", "EG"], ["TG"])
                    dve(lambda e, di_=di_, tg=tg: e.tensor_tensor(out=di_, in0=di_, in1=tg, op=ALU.add), ["TG"], ["EG"])
                P.dma(SP, lambda e, s0=s0: e.dma_start(out=TBL[s0:s0 + 8].rearrange("s p r t -> p s (r t)"),
                                                        in_=EG.rearrange("p s r t -> p s (r t)")), "tblw", reads=["EG"], writes=[("TBL", g8)])
            P.barrier()

            def norm_stats(slot, rows):
                P.op(ACT, lambda e: e.activation(out=SQJ[:rows, :], in_=XT[:rows, slot, :], func=AF.Square, accum_out=SS[:rows, slot:slot + 1]),
                     reads=[("XT", slot)], writes=["SQJ", ("SS", slot)])
                P.op(ACT, lambda e: e.activation(out=RS[:rows, slot:slot + 1], in_=SS[:rows, slot:slot + 1], func=AF.Sqrt, scale=1.0 / D, bias=EPS),
                     reads=[("SS", slot)], writes=[("RS", slot)])
                P.op(DVE, lambda e: e.reciprocal(out=RS[:rows, slot:slot + 1], in_=RS[:rows, slot:slot + 1]), writes=[("RS", slot)])

            def norm_T(g, src_fn, G, gkey):
                for tb, (rows, c0) in enumerate(g.tbs):
                    src = src_fn(tb)
                    slot = tb % 2
                    P.dma(SP, lambda e, src=src, rows=rows, slot=slot: e.dma_start(out=XT[:rows, slot, :], in_=src), ("xt", slot),
                          writes=[("XT", slot)])
                    norm_stats(slot, rows)
                    P.op(ACT, lambda e, rows=rows, slot=slot: e.activation(out=XT[:rows, slot, :], in_=XT[:rows, slot, :], func=AF.Identity,
                                                                            scale=RS[:rows, slot:slot + 1]),
                         reads=[("RS", slot)], writes=[("XT", slot)])
                    for c4 in range(8):
                        bk = c4 % 2
                        for j in range(4):
                            c = c4 * 4 + j
                            P.op(PE, lambda e, rows=rows, slot=slot, c=c, j=j, bk=bk: e.transpose(
                                out=PS[:, bk, j * 128:j * 128 + rows], in_=XT[:rows, slot, c * 128:(c + 1) * 128], identity=IDENT[:rows, :rows]),
                                reads=[("XT", slot), "IDENT"], acc=[("ps", bk)])
                        P.op(DVE, lambda e, rows=rows, c4=c4, bk=bk, c0=c0, G=G: e.tensor_tensor(
                            out=XN[:, c4 * 4:(c4 + 1) * 4, c0:c0 + rows],
                            in0=PS[:, bk, :].rearrange("p (j n) -> p j n", j=4)[:, :, 0:rows],
                            in1=G[:, c4 * 4:(c4 + 1) * 4, :].to_broadcast([128, 4, rows]), op=ALU.mult),
                            reads=[gkey], writes=[("ps", bk)], acc=[("XN", c4 * 4 + j) for j in range(4)])

            def xn_rhs(k):
                return XN[:, k, :], ("XN", k)

            xi = {"i": 0}

            def run_tile(g, mode, xsrc_fn, xm, samp0, outp):
                full = mode == "full"
                npc, nsc, n = g.np, g.ns, g.n
                P.barrier()
                norm_T(g, xsrc_fn, G1, "G1")
                P.barrier()
                if mode != "ssm":
                    def ep_ain(ci, b):
                        P.op(ACT, lambda e, ci=ci, b=b: e.activation(out=sv(g, GLU[:, ci, PADH:PADH + n]), in_=pv(g, b), func=AF.Copy),
                             writes=pk(g, b) + [("GLU", ci)])
                    linear_fm(g, w_in, 0, CWD, 32, xn_rhs, ep_ain)

                    def ep_agate(ci, b):
                        P.op(ACT, lambda e, b=b: e.activation(out=sv(g, SIGT), in_=pv(g, b), func=AF.Sigmoid), writes=pk(g, b) + ["SIGT"])
                        P.op(DVE, lambda e, ci=ci: e.tensor_tensor(out=GLU[:, ci, PADH:PADH + n], in0=GLU[:, ci, PADH:PADH + n], in1=SIGT[:, 0:n], op=ALU.mult),
                             reads=["SIGT"], writes=[("GLU", ci)])
                    linear_fm(g, w_in, CWD, CWD, 32, xn_rhs, ep_agate)
                    P.op(DVE, lambda e: e.tensor_copy(out=GLU[:, :, 0:PADH], in_=HALO[:]), reads=["HALO"], acc=[("GLU", c) for c in range(16)])
                    if nsc:
                        scv = sc[samp0:samp0 + NS].rearrange("n k f -> (n k) f")
                    for c in range(16):
                        cs = c % 2
                        P.op(ACT, lambda e, c=c: e.activation(out=CV[:, c, 0:npc], in_=GLU[:, c, PADH:PADH + npc], func=AF.Identity,
                                                               scale=CWT[:, c, 30:31], bias=CBS[:, c, :]),
                             reads=[("GLU", c), "CWT", "CBS"], writes=[("CV", c)])
                        for k in range(30):
                            P.op(DVE, lambda e, c=c, k=k: e.scalar_tensor_tensor(out=CV[:, c, 0:npc], in0=GLU[:, c, k:k + npc], scalar=CWT[:, c, k:k + 1],
                                                                                  in1=CV[:, c, 0:npc], op0=ALU.mult, op1=ALU.add),
                                 reads=[("GLU", c), "CWT"], writes=[("CV", c)])
                        if nsc:
                            P.dma(SP, lambda e, c=c, cs=cs, scv=scv: e.dma_start(out=CSTG[:120, cs, :, :],
                                                                                   in_=scv[:, c * 128:(c + 1) * 128].rearrange("(b r) f -> r b f", b=2)),
                                  ("cstg", cs), writes=[("CSTG", cs)])
                            bk = 4 + cs
                            for b2 in range(2):
                                P.op(PE, lambda e, cs=cs, b2=b2, bk=bk: e.transpose(out=PS[:, bk, b2 * 120:(b2 + 1) * 120], in_=CSTG[:120, cs, b2, :],
                                                                                     identity=IDENT[:120, :120]),
                                     reads=[("CSTG", cs), "IDENT"], acc=[("ps", bk)])
                            P.op(DVE, lambda e, c=c, bk=bk: e.tensor_tensor(out=TMPS[:], in0=PS[:, bk, 0:240].rearrange("p (n k) -> p n k", n=8),
                                                                            in1=CWT[:, c:c + 1, 0:30].to_broadcast([128, 8, 30]), op=ALU.mult),
                                 reads=["CWT"], writes=[("ps", bk), "TMPS"])
                            P.op(DVE, lambda e: e.tensor_reduce(out=CSS, in_=TMPS[:], axis=AX.X, op=ALU.add), reads=["TMPS"], writes=["CSS"])
                            P.op(DVE, lambda e, c=c: e.scalar_tensor_tensor(out=CSS, in0=GLU[:, c, PADH + npc:PADH + n], scalar=CWT[:, c, 30:31], in1=CSS,
                                                                             op0=ALU.mult, op1=ALU.add), reads=[("GLU", c), "CWT"], writes=["CSS"])
                            P.op(DVE, lambda e, c=c: e.tensor_scalar(out=CV[:, c, npc:n], in0=CSS, scalar1=CBS[:, c, :], scalar2=None, op0=ALU.add),
                                 reads=["CSS", "CBS"], writes=[("CV", c)])
                        P.op(ACT, lambda e, c=c: e.activation(out=SQ[:, 0:n], in_=CV[:, c, 0:n], func=AF.Square), reads=[("CV", c)], writes=["SQ"])
                        for i, (p0, pn) in enumerate(g.parts):
                            P.op(PE, lambda e, c=c, i=i, p0=p0, pn=pn: e.matmul(PS[:, 0 + i, 0:pn], lhsT=ONES[:], rhs=CV[:, c, p0:p0 + pn], start=(c == 0), stop=(c == 15)),
                                 reads=["ONES", ("CV", c)], acc=[("ps", 0 + i)])
                            P.op(PE, lambda e, c=c, i=i, p0=p0, pn=pn: e.matmul(PS[:, 2 + i, 0:pn], lhsT=ONES[:], rhs=SQ[:, p0:p0 + pn], start=(c == 0), stop=(c == 15)),
                                 reads=["ONES", "SQ"], acc=[("ps", 2 + i)])
                    P.op(DVE, lambda e: e.tensor_copy(out=HALO[:], in_=GLU[:, :, npc:npc + PADH]), reads=[("GLU", c) for c in range(16)], writes=["HALO"])
                    if full and nsc:
                        P.dma(SP, lambda e: e.dma_start(out=conv_s[samp0:samp0 + NS, 0:29, :], in_=sc[samp0:samp0 + NS, 1:30, :]), "cso")
                        for n_ in range(NS):
                            elem_dma(conv_s[samp0 + n_, 29, :].rearrange("(c p) -> p c", p=128), GLU[:, :, PADH + npc + n_], "cso",
                                     reads=[("GLU", c) for c in range(16)])
                    P.op(DVE, lambda e: e.tensor_scalar(out=sv(g, MU), in0=pv(g, 0), scalar1=1.0 / CWD, scalar2=None, op0=ALU.mult),
                         writes=pk(g, 0) + ["MU"])
                    P.op(DVE, lambda e: e.tensor_tensor(out=SQ[:, 0:n], in0=MU[:, 0:n], in1=MU[:, 0:n], op=ALU.mult), reads=["MU"], writes=["SQ"])
                    P.op(DVE, lambda e: e.scalar_tensor_tensor(out=sv(g, RSTD), in0=pv(g, 2), scalar=1.0 / CWD, in1=sv(g, SQ), op0=ALU.mult, op1=ALU.subtract),
                         reads=["SQ"], writes=pk(g, 2) + ["RSTD"])
                    P.op(ACT, lambda e: e.activation(out=RSTD[:, 0:n], in_=RSTD[:, 0:n], func=AF.Sqrt, bias=EPS), writes=["RSTD"])
                    P.op(DVE, lambda e: e.reciprocal(out=RSTD[:, 0:n], in_=RSTD[:, 0:n]), writes=["RSTD"])
                    for c in range(16):
                        P.op(DVE, lambda e, c=c: e.tensor_tensor(out=CV[:, c, 0:n], in0=CV[:, c, 0:n], in1=MU[:, 0:n], op=ALU.subtract), reads=["MU"], writes=[("CV", c)])
                        P.op(DVE, lambda e, c=c: e.tensor_tensor(out=CV[:, c, 0:n], in0=CV[:, c, 0:n], in1=RSTD[:, 0:n], op=ALU.mult), reads=["RSTD"], writes=[("CV", c)])
                        P.op(ACT, lambda e, c=c: e.activation(out=CB16[:, c, 0:n], in_=CV[:, c, 0:n], func=AF.Silu, scale=LNG[:, c, :], bias=LNB[:, c, :]),
                             reads=[("CV", c), "LNG", "LNB"], writes=[("CB", c)])
                    P.barrier()
                def ep_u(ci, b):
                    P.op(ACT, lambda e, ci=ci, b=b: e.activation(out=sv(g, U16[:, ci, :]), in_=pv(g, b), func=AF.Copy),
                         writes=pk(g, b) + [("U", ci)])
                linear_fm(g, w_in, 2 * CWD, SWD, 32, xn_rhs, ep_u)
                if nsc:
                    for n_ in range(NS):
                        row = samp0 + n_
                        elem_dma(H0[:, :, 0, n_], sr[row, :].rearrange("(s p) -> p s", p=128), "h0", acc=["H0"])
                        elem_dma(H0[:, :, 1, n_], si[row, :].rearrange("(s p) -> p s", p=128), "h0", acc=["H0"])

                def bproj(s):
                    c = s // 4
                    sl = s % 2
                    P.dma(SP, lambda e, s=s, sl=sl: e.dma_start(out=LMS[:, sl], in_=LMD[s]), ("lms", sl),
                          reads=[("LMD", mi, s // 4) for mi in range(4)], writes=[("LMS", sl)])
                    P.dma(SP, lambda e, s=s, sl=sl: e.dma_start(out=TB[:, sl], in_=TBL[s]), ("tb", sl),
                          reads=[("TBL", s // 8)], writes=[("TB", sl)])
                    for ri in range(2):
                        for i, (p0, pn) in enumerate(g.parts):
                            P.op(PE, lambda e, sl=sl, c=c, ri=ri, i=i, p0=p0, pn=pn: e.matmul(PS[:, 2 * ri + i, 0:pn], lhsT=LMS[:, sl, ri, :], rhs=U16[:, c, p0:p0 + pn],
                                                                                         start=True, stop=True),
                                 reads=[("LMS", sl), ("U", c)], acc=[("ps", 2 * ri + i)])

                def evac(s):
                    for ri in range(2):
                        P.op(ACT, lambda e, ri=ri: e.activation(out=sv(g, XA[:, ri, :]), in_=pv(g, 2 * ri), func=AF.Copy),
                             writes=pk(g, 2 * ri) + [("XA", ri)])

                def scan(s):
                    ar, ai, an = PWR[:, s, 0:1], PWI[:, s, 0:1], PWN[:, s, 0:1]
                    sl = s % 2
                    c0 = slice(0, 1)
                    sm = slice(npc, n)
                    pc = slice(0, npc)
                    CT, ST = TB[:, sl, 0, 0:npc], TB[:, sl, 1, 0:npc]
                    tbk = ("TB", sl)
                    T0, T1, T2 = T3B[:, 0, 0:npc], T3B[:, 1, 0:npc], T3B[:, 2, 0:npc]

                    def stt(out, in0, scalar, in1, r=(), w=()):
                        P.op(DVE, lambda e: e.scalar_tensor_tensor(out=out, in0=in0, scalar=scalar, in1=in1, op0=ALU.mult, op1=ALU.add),
                             reads=["PW"] + list(r), writes=list(w))

                    def tt(out, in0, in1, op, r=(), w=()):
                        P.op(DVE, lambda e: e.tensor_tensor(out=out, in0=in0, in1=in1, op=op), reads=list(r), writes=list(w))
                    if nsc:
                        stt(HSM[:, 0, :], H0[:, s, 0, :], ar, XA[:, 0, sm], ["H0", ("XA", 0)], ["HSM0"])
                        stt(HSM[:, 0, :], H0[:, s, 1, :], an, HSM[:, 0, :], ["H0"], ["HSM0"])
                        stt(HSM[:, 1, :], H0[:, s, 1, :], ar, XA[:, 1, sm], ["H0", ("XA", 1)], ["HSM1"])
                        stt(HSM[:, 1, :], H0[:, s, 0, :], ai, HSM[:, 1, :], ["H0"], ["HSM1"])
                    stt(XA[:, 0, c0], CARRY[:, s, 0:1], ar, XA[:, 0, c0], ["CARRY"], [("XA", 0)])
                    stt(XA[:, 0, c0], CARRY[:, s, 1:2], an, XA[:, 0, c0], ["CARRY"], [("XA", 0)])
                    stt(XA[:, 1, c0], CARRY[:, s, 1:2], ar, XA[:, 1, c0], ["CARRY"], [("XA", 1)])
                    stt(XA[:, 1, c0], CARRY[:, s, 0:1], ai, XA[:, 1, c0], ["CARRY"], [("XA", 1)])
                    tt(T0, XA[:, 0, pc], CT, ALU.mult, [("XA", 0), tbk], ["T0"])
                    tt(T1, XA[:, 1, pc], ST, ALU.mult, [("XA", 1), tbk], ["T1"])
                    tt(T0, T0, T1, ALU.add, ["T1"], ["T0"])
                    tt(T1, XA[:, 1, pc], CT, ALU.mult, [("XA", 1), tbk], ["T1"])
                    tt(T2, XA[:, 0, pc], ST, ALU.mult, [("XA", 0), tbk], ["T2"])
                    tt(T1, T1, T2, ALU.subtract, ["T2"], ["T1"])
                    rb = RMAG[:, s, :].to_broadcast([128, npc])
                    P.op(DVE, lambda e: e.tensor_tensor_scan(out=XA[:, 0, pc], data0=rb, data1=T0, initial=0.0, op0=ALU.mult, op1=ALU.add),
                         reads=["T0", "RMAG"], writes=[("XA", 0)])
                    P.op(DVE, lambda e: e.tensor_tensor_scan(out=XA[:, 1, pc], data0=rb, data1=T1, initial=0.0, op0=ALU.mult, op1=ALU.add),
                         reads=["T1", "RMAG"], writes=[("XA", 1)])
                    fc = slice(npc - 1, npc) if mode == "ssm" else pc
                    F0, F1, F2 = T3B[:, 0, fc], T3B[:, 1, fc], T3B[:, 2, fc]
                    CF, SF = TB[:, sl, 0, fc], TB[:, sl, 1, fc]
                    tt(F0, XA[:, 0, fc], CF, ALU.mult, [("XA", 0), tbk], ["T0"])
                    tt(F2, XA[:, 1, fc], SF, ALU.mult, [("XA", 1), tbk], ["T2"])
                    tt(F0, F0, F2, ALU.subtract, ["T2"], ["T0"])
                    tt(F1, XA[:, 0, fc], SF, ALU.mult, [("XA", 0), tbk], ["T1"])
                    tt(F2, XA[:, 1, fc], CF, ALU.mult, [("XA", 1), tbk], ["T2"])
                    tt(F1, F1, F2, ALU.add, ["T2"], ["T1"])

                def cast(s):
                    hs = s % 2
                    for ri in range(2):
                        tk = "T%d" % ri
                        hk = "HSM%d" % ri
                        if mode != "ssm":
                            P.op(ACT, lambda e, hs=hs, ri=ri: e.activation(out=H16S[:, hs, ri, 0:npc], in_=T3B[:, ri, 0:npc], func=AF.Copy),
                                 reads=[tk], acc=[("H16S", hs)])
                        P.op(ACT, lambda e, s=s, ri=ri: e.activation(out=CARRY[:, s, ri:ri + 1], in_=T3B[:, ri, npc - 1:npc], func=AF.Copy),
                             reads=[tk], writes=["CARRY"])
                        if nsc:
                            P.op(ACT, lambda e, hs=hs, ri=ri: e.activation(out=H16S[:, hs, ri, npc:n], in_=HSM[:, ri, :], func=AF.Copy),
                                 reads=[hk], acc=[("H16S", hs)])
                            P.op(ACT, lambda e, s=s, ri=ri: e.activation(out=HS[:, s, ri, :], in_=HSM[:, ri, :], func=AF.Copy),
                                 reads=[hk], writes=["HS"])

                def cproj(s):
                    c, j = s // 4, s % 4
                    sl, hs = s % 2, s % 2
                    by = 4 + 2 * (c % 2)
                    for ri in range(2):
                        for i, (p0, pn) in enumerate(g.parts):
                            P.op(PE, lambda e, sl=sl, hs=hs, ri=ri, i=i, p0=p0, pn=pn, by=by, j=j: e.matmul(
                                PS[:, by + i, 0:pn], lhsT=LMS[:, sl, 2 + ri, :], rhs=H16S[:, hs, ri, p0:p0 + pn],
                                start=(j == 0 and ri == 0), stop=(j == 3 and ri == 1)),
                                reads=[("LMS", sl), ("H16S", hs)], acc=[("ps", by + i)])
                    if j == 3:
                        P.op(DVE, lambda e, c=c, by=by: e.scalar_tensor_tensor(out=sv(g, YT), in0=sv(g, U16[:, c, :]), scalar=DSK[:, c, :], in1=pv(g, by),
                                                                             op0=ALU.mult, op1=ALU.add),
                             reads=[("U", c), "DSK"], writes=pk(g, by) + ["YT"])
                        P.op(ACT, lambda e: e.activation(out=SIGT[:, 0:n], in_=YT[:, 0:n], func=AF.Square), reads=["YT"], writes=["SIGT"])
                        P.op(DVE, lambda e: e.tensor_scalar(out=SIGT[:, 0:n], in0=SIGT[:, 0:n], scalar1=0.044715, scalar2=1.0, op0=ALU.mult, op1=ALU.add), writes=["SIGT"])
                        P.op(DVE, lambda e: e.tensor_tensor(out=SIGT[:, 0:n], in0=SIGT[:, 0:n], in1=YT[:, 0:n], op=ALU.mult), reads=["YT"], writes=["SIGT"])
                        P.op(ACT, lambda e: e.activation(out=SIGT[:, 0:n], in_=SIGT[:, 0:n], func=AF.Sigmoid, scale=2.0 * math.sqrt(2.0 / math.pi)), writes=["SIGT"])
                        P.op(DVE, lambda e, c=c: e.tensor_tensor(out=SG16[:, c, 0:n], in0=YT[:, 0:n], in1=SIGT[:, 0:n], op=ALU.mult), reads=["YT", "SIGT"], writes=[("SG", c)])

                bproj(0)
                evac(0)
                for s in range(64):
                    if s + 1 < 64:
                        bproj(s + 1)
                    scan(s)
                    cast(s)
                    if s + 1 < 64:
                        evac(s + 1)
                    if mode != "ssm":
                        cproj(s)
                if mode == "ssm":
                    return
                if full and nsc:
                    for n_ in range(NS):
                        row = samp0 + n_
                        elem_dma(ssr_s[row, :].rearrange("(s p) -> p s", p=128), HS[:, :, 0, n_], "hso", reads=["HS"])
                        elem_dma(ssi_s[row, :].rearrange("(s p) -> p s", p=128), HS[:, :, 1, n_], "hso", reads=["HS"])
                def ep_glu(ci, b):
                    P.op(ACT, lambda e, b=b: e.activation(out=sv(g, SIGT), in_=pv(g, b), func=AF.Sigmoid), writes=pk(g, b) + ["SIGT"])
                    P.op(DVE, lambda e, ci=ci: e.tensor_tensor(out=SGG16[:, ci, 0:n], in0=SG16[:, ci, 0:n], in1=SIGT[:, 0:n], op=ALU.mult),
                         reads=[("SG", ci), "SIGT"], writes=[("SGG", ci)])
                linear_fm(g, w_glu, 0, SWD, 16, lambda k: (SG16[:, k, :], ("SG", k)), ep_glu)
                P.barrier()
                for q in range(16):
                    co = wnext(w_conv_out, 0, 16, q * 256, 256)
                    ga0 = wnext(w_in, 0, 16, 3 * CWD + q * 256, 256)
                    ga1 = wnext(w_in, 2048, 16, 3 * CWD + q * 256, 256)
                    for m in range(2):
                        b1 = next_acc()
                        mm_group(g, [co], 16, m, lambda k: (CB16[:, k, :], ("CB", k)), b1)
                        b2 = next_acc()
                        mm_group(g, [ga0, ga1], 32, m, xn_rhs, b2)
                        P.op(ACT, lambda e, b2=b2: e.activation(out=sv(g, SIGT), in_=pv(g, b2), func=AF.Sigmoid), writes=pk(g, b2) + ["SIGT"])
                        P.op(DVE, lambda e, m=m, b1=b1: e.tensor_tensor(out=sv(g, MA[:, m, :]), in0=pv(g, b1), in1=sv(g, SIGT), op=ALU.mult),
                             reads=["SIGT"], writes=pk(g, b1) + [("MA", m)])
                    wrelease(); wrelease(); wrelease()
                    so = wnext(w_ssm_out, 0, 16, q * 256, 256)
                    gb0 = wnext(w_in, 0, 16, 3 * CWD + D + q * 256, 256)
                    gb1 = wnext(w_in, 2048, 16, 3 * CWD + D + q * 256, 256)
                    for m in range(2):
                        b1 = next_acc()
                        mm_group(g, [so], 16, m, lambda k: (SGG16[:, k, :], ("SGG", k)), b1)
                        b2 = next_acc()
                        mm_group(g, [gb0, gb1], 32, m, xn_rhs, b2)
                        P.op(ACT, lambda e, b2=b2: e.activation(out=sv(g, SIGT), in_=pv(g, b2), func=AF.Sigmoid), writes=pk(g, b2) + ["SIGT"])
                        P.op(DVE, lambda e, b1=b1: e.tensor_tensor(out=sv(g, T2), in0=pv(g, b1), in1=sv(g, SIGT), op=ALU.mult),
                             reads=["SIGT"], writes=pk(g, b1) + ["T2"])
                        P.op(DVE, lambda e, m=m, q=q: e.tensor_tensor(out=MG16[:, 2 * q + m, 0:n], in0=MA[:, m, 0:n], in1=T2[:, 0:n], op=ALU.add),
                             reads=[("MA", m), "T2"], writes=[("MG", 2 * q + m)])
                    wrelease(); wrelease(); wrelease()
                def ep_wo(q, tb, bk):
                    rows, c0 = g.tbs[tb]
                    src = xsrc_fn(tb)
                    sl = xi["i"] % 3
                    xi["i"] += 1
                    P.dma(SP, lambda e: e.dma_start(out=XSL[:rows, sl, :], in_=src[:, q * 256:(q + 1) * 256]), ("xsl", sl), writes=[("XSL", sl)])
                    P.op(DVE, lambda e: e.tensor_tensor(out=XMS[:rows, sl, :], in0=PS[:rows, bk, 0:256], in1=XSL[:rows, sl, :], op=ALU.add),
                         reads=[("XSL", sl)], writes=[("ps", bk), ("XMS", sl)])
                    P.dma(SP, lambda e: e.dma_start(out=XMID[xm, c0:c0 + rows, q * 256:(q + 1) * 256], in_=XMS[:rows, sl, :]), ("xms", sl),
                          reads=[("XMS", sl)], writes=[("xmid", tb, q)])
                linear_tm(g, w_o, 0, 32, lambda k: (MG16[:, k, :], ("MG", k)), ep_wo)
                P.barrier()
                norm_T(g, lambda tb: XMID[xm, g.tbs[tb][1]:g.tbs[tb][1] + g.tbs[tb][0], :], G2, "G2")
                if g.lead:
                    P.op(DVE, lambda e: e.tensor_scalar(out=XN[:, :, 0:g.lead], in0=XN[:, :, 0:g.lead], scalar1=HMASK[:, 0:1], scalar2=None, op0=ALU.mult),
                         reads=["HMASK"], acc=[("XN", c) for c in range(32)])
                P.barrier()
                if nsc:
                    for n_ in range(NS):
                        row = samp0 + n_
                        for k in range(2):
                            elem_dma(FS[:, :, 2 * n_ + k], sf[row, k, :].rearrange("(c p) -> p c", p=128), "fs", acc=["FS"])
                for hf in range(2):
                    ch0 = hf * NCH
                    for s0 in range(0, NCH, 2):
                        ncol = min(2, NCH - s0) * 128
                        colg = (ch0 + s0) * 128
                        g0 = wnext(w_up, 0, 16, colg, ncol)
                        g1 = wnext(w_up, 2048, 16, colg, ncol)
                        for m in range(ncol // 128):
                            i = ch0 + s0 + m
                            b = next_acc()
                            mm_group(g, [g0, g1], 32, m, xn_rhs, b)
                            P.op(ACT, lambda e, b=b: e.activation(out=sv(g, G32[:, 2:2 + n]), in_=pv(g, b), func=AF.Copy), writes=pk(g, b) + ["G32"])
                            if full:
                                P.op(DVE, lambda e, i=i: e.tensor_copy(out=G32[:, 0:2], in_=FH[:, i, :]), reads=["FH"], writes=["G32"])
                                P.op(ACT, lambda e, i=i: e.activation(out=GCT[:, 0:n], in_=G32[:, 2:2 + n], func=AF.Identity, scale=FCW[:, i, 2:3], bias=FCB[:, i, :]),
                                     reads=["G32", "FCW", "FCB"], writes=["GCT"])
                                for k in range(2):
                                    P.op(DVE, lambda e, i=i, k=k: e.scalar_tensor_tensor(out=GCT[:, 0:npc], in0=G32[:, k:k + npc], scalar=FCW[:, i, k:k + 1],
                                                                                          in1=GCT[:, 0:npc], op0=ALU.mult, op1=ALU.add),
                                         reads=["G32", "FCW"], writes=["GCT"])
                                    if nsc:
                                        P.op(DVE, lambda e, i=i, k=k: e.scalar_tensor_tensor(out=GCT[:, npc:n], in0=FS[:, i, k:16:2], scalar=FCW[:, i, k:k + 1],
                                                                                              in1=GCT[:, npc:n], op0=ALU.mult, op1=ALU.add),
                                             reads=["FS", "FCW"], writes=["GCT"])
                            P.op(DVE, lambda e, i=i: e.tensor_copy(out=FH[:, i, :], in_=G32[:, npc:npc + 2]), reads=["G32"], writes=["FH"])
                            if full:
                                if nsc:
                                    P.op(DVE, lambda e, i=i: e.tensor_copy(out=GS[:, i, :], in_=G32[:, 2 + npc:2 + n]), reads=["G32"], writes=["GS"])
                                P.op(ACT, lambda e, m=m: e.activation(out=SILU[:, m, 0:n], in_=GCT[:, 0:n], func=AF.Silu), reads=["GCT"], writes=[("SILU", m)])
                        wrelease(); wrelease()
                        if not full:
                            continue
                        v0 = wnext(w_up, 0, 16, DFF + colg, ncol)
                        v1 = wnext(w_up, 2048, 16, DFF + colg, ncol)
                        for m in range(ncol // 128):
                            il = s0 + m
                            b = next_acc()
                            mm_group(g, [v0, v1], 32, m, xn_rhs, b)
                            P.op(DVE, lambda e, m=m, il=il, b=b: e.tensor_tensor(out=sv(g, H16F[:, il, :]), in0=pv(g, b), in1=sv(g, SILU[:, m, :]), op=ALU.mult),
                                 reads=[("SILU", m)], writes=pk(g, b) + [("HF", il)])
                        wrelease(); wrelease()
                    if not full:
                        continue

                    def ep_down(q, tb, bk):
                        rows, c0 = g.tbs[tb]
                        sl = xi["i"] % 3
                        xi["i"] += 1
                        P.dma(SP, lambda e: e.dma_start(out=XSL2[:rows, sl, :], in_=XMID[xm, c0:c0 + rows, q * 256:(q + 1) * 256]), ("xsl2", sl),
                              reads=[("xmid", tb, q)], writes=[("XSL2", sl)])
                        P.op(DVE, lambda e: e.tensor_tensor(out=XMS2[:rows, sl, :], in0=PS[:rows, bk, 0:256], in1=XSL2[:rows, sl, :], op=ALU.add),
                             reads=[("XSL2", sl)], writes=[("ps", bk), ("XMS2", sl)])
                        P.dma(SP, lambda e: e.dma_start(out=XMID[xm, c0:c0 + rows, q * 256:(q + 1) * 256], in_=XMS2[:rows, sl, :]), ("xms2", sl),
                              reads=[("XMS2", sl)], writes=[("xmid", tb, q)])
                    linear_tm(g, w_down, ch0 * 128, NCH, lambda k: (H16F[:, k, :], ("HF", k)), ep_down)
                if not full:
                    return
                if nsc:
                    P.dma(SP, lambda e: e.dma_start(out=ffn_s[samp0:samp0 + NS, 0, :], in_=sf[samp0:samp0 + NS, 1, :]), "fso")
                    for n_ in range(NS):
                        elem_dma(ffn_s[samp0 + n_, 1, :].rearrange("(c p) -> p c", p=128), GS[:, :, n_], "fso", reads=["GS"])
                P.barrier()
                P.dma(SP, lambda e: e.dma_start(out=GF, in_=final_norm_g.partition_broadcast(128)), "gf", writes=["GF"])
                for tb, (rows, c0) in enumerate(g.tbs):
                    if g.lead and tb == 0:
                        continue
                    slot = tb % 2
                    P.dma(SP, lambda e, rows=rows, c0=c0, slot=slot: e.dma_start(out=XT[:rows, slot, :], in_=XMID[xm, c0:c0 + rows, :]), ("xt", slot),
                          reads=[("xmid", tb, q) for q in range(16)], writes=[("XT", slot)])
                    norm_stats(slot, rows)
                    P.op(DVE, lambda e, rows=rows, slot=slot: e.scalar_tensor_tensor(out=XT[:rows, slot, :], in0=XT[:rows, slot, :], scalar=RS[:rows, slot:slot + 1],
                                                                                      in1=GF[:rows, :], op0=ALU.mult, op1=ALU.mult),
                         reads=[("RS", slot), "GF"], writes=[("XT", slot)])
                    P.dma(SP, lambda e, rows=rows, slot=slot, tb=tb: e.dma_start(out=outp(tb), in_=XT[:rows, slot, :]), ("yo", slot),
                          reads=[("XT", slot)])

            GM = Geo(NP, NS)
            G0 = Geo(NP + LEAD, NS, lead=LEAD) if PRO else GM
            if PRO:
                done = 0
                rest = PRO - LEAD
                while done < rest:
                    cnt = min(512, rest - done)
                    gp = Geo(cnt, 0)
                    run_tile(gp, "ssm", lambda tb, done=done, gp=gp: xprev[done + gp.tbs[tb][1]: done + gp.tbs[tb][1] + gp.tbs[tb][0], :], NT, 0, None)
                    done += cnt
            for t in range(NT):
                gt = G0 if t == 0 else GM
                ld = gt.lead

                def xsrc(tb, t=t, gt=gt, ld=ld):
                    rows, c0 = gt.tbs[tb]
                    if c0 < ld:
                        return xprev[PRO - LEAD:PRO, :]
                    if c0 < gt.np:
                        return xp[t * NP + c0 - ld: t * NP + c0 - ld + rows, :]
                    return xs[t * NS:(t + 1) * NS, :]

                def ydst(tb, t=t, gt=gt, ld=ld):
                    rows, c0 = gt.tbs[tb]
                    if c0 < gt.np:
                        return y_p[t * NP + c0 - ld: t * NP + c0 - ld + rows, :]
                    return y_s[t * NS:(t + 1) * NS, :]
                run_tile(gt, "full", xsrc, t, t * NS, ydst)
            for c in range(16):
                elem_dma(conv_p[:, c * 128:(c + 1) * 128].rearrange("k p -> p k"), HALO[:, c, :], "cpo", reads=["HALO"])
            elem_dma(ssr_p.rearrange("(s p) -> p s", p=128), CARRY[:, :, 0], "hpo", reads=["CARRY"])
            elem_dma(ssi_p.rearrange("(s p) -> p s", p=128), CARRY[:, :, 1], "hpo", reads=["CARRY"])
            for k in range(2):
                elem_dma(ffn_p[k, :].rearrange("(c p) -> p c", p=128), FH[:, :, k], "fpo", reads=["FH"])

        blocks = []
        Pd = Prog(nc, dry=True)
        emit_all(Pd, blocks)
        Pr = Prog(nc, dry=False)
        emit_all(Pr, blocks)
        Pr.emit(st)
    return nc


_NT = 2
_PRO = 1024
_W_KEYS = ["norm_mix_g", "w_in", "conv_w", "conv_b", "ln_g", "ln_b", "w_conv_out", "lam_re", "lam_im", "log_dt",
           "b_re", "b_im", "c_re", "c_im", "d_skip", "w_glu", "w_ssm_out", "w_o", "norm_ffn_g", "w_up",
           "ffn_conv_w", "ffn_conv_b", "w_down"]


def make_in_map(inputs, b, tok0, samp0, NT, PRO=0):
    f = lambda a: np.ascontiguousarray(np.asarray(a, dtype=np.float32))
    nsa = NT * NS
    m = {}
    m["xp"] = f(inputs["x_prompt"][b, tok0:tok0 + NT * NP])
    if PRO:
        if tok0 >= PRO:
            m["xprev"] = f(inputs["x_prompt"][b, tok0 - PRO:tok0])
            m["hmask"] = np.ones((128, 1), np.float32)
        else:
            assert tok0 == 0
            m["xprev"] = np.zeros((PRO, D), np.float32)
            m["hmask"] = np.zeros((128, 1), np.float32)
    m["xs"] = f(inputs["x_sample"][samp0:samp0 + nsa, 0])
    m["sc"] = f(inputs["state_conv"][0, samp0:samp0 + nsa])
    m["sr"] = f(inputs["state_ssm_re"][0, samp0:samp0 + nsa]).reshape(nsa, 8192)
    m["si"] = f(inputs["state_ssm_im"][0, samp0:samp0 + nsa]).reshape(nsa, 8192)
    m["sf"] = f(inputs["state_ffn_conv"][0, samp0:samp0 + nsa])
    for k in _W_KEYS:
        m[k] = f(inputs[k][0])
    m["final_norm_g"] = f(inputs["final_norm_g"])
    m["ident"] = np.eye(128, dtype=np.float32)
    return m


def kernel(**inputs):
    NT, PRO = _NT, _PRO
    nc = build_nc(NT, PRO)
    nsa = NT * NS
    ntok = NT * NP
    in_maps = []
    for c in range(8):
        b, hf = c // 2, c % 2
        in_maps.append(make_in_map(inputs, b, hf * ntok, c * nsa, NT, PRO))
    res = run_bass_kernel_spmd(nc, in_maps, core_ids=list(range(8)))
    r = res.results
    y_prompt = np.zeros((4, 2 * ntok, D), np.float32)
    for c in range(8):
        y_prompt[c // 2, (c % 2) * ntok:(c % 2 + 1) * ntok] = r[c]["y_p"]
    y_sample = np.concatenate([r[c]["y_s"] for c in range(8)])[:, None, :].astype(np.float32)
    last = [2 * b + 1 for b in range(4)]
    conv_prompt = np.stack([r[c]["conv_p"] for c in last])[None].astype(np.float32)
    conv_sample = np.concatenate([r[c]["conv_s"] for c in range(8)])[None].astype(np.float32)
    ssr_p = np.stack([r[c]["ssr_p"].reshape(128, 64) for c in last])[None].astype(np.float32)
    ssi_p = np.stack([r[c]["ssi_p"].reshape(128, 64) for c in last])[None].astype(np.float32)
    ssr_s = np.concatenate([r[c]["ssr_s"].reshape(-1, 128, 64) for c in range(8)])[None].astype(np.float32)
    ssi_s = np.concatenate([r[c]["ssi_s"].reshape(-1, 128, 64) for c in range(8)])[None].astype(np.float32)
    ffn_p = np.stack([r[c]["ffn_p"] for c in last])[None].astype(np.float32)
    ffn_s = np.concatenate([r[c]["ffn_s"] for c in range(8)])[None].astype(np.float32)
    return (y_prompt, y_sample, conv_prompt, conv_sample, ssr_p, ssi_p, ssr_s, ssi_s, ffn_p, ffn_s)
```
